# Optimizing a Trainium2 kernel written in Bass

```python
import jax, jax.numpy as jnp
from jax import lax
import numpy as np

D_MODEL = 1024
BATCH = 32
SEQ = 2048
DEPTH = 4
DEC_BATCH = 16
DEC_SEQ = 4096
PAST_LEN = 128

HEAD_DIM = 64
N_HEADS_NA = 8
N_HEADS_DN = 8
D_NA = N_HEADS_NA * HEAD_DIM
D_DN = N_HEADS_DN * HEAD_DIM
D_MIX = D_NA + D_DN
GRID_W = 64
WIN_H_MAX = 8
WIN_W = 16
QB_W = 16
KB_W = QB_W + WIN_W
SHORT_CONV_W = 3
CHUNK = 64
N_MEM = 256
N_HEADS_X = 4
HEAD_DIM_X = D_MODEL // N_HEADS_X
D_FF = 2816
FFN_CONV_W = 3
RMS_EPS = 1e-6
IN_WIDTHS = (D_NA, D_NA, D_NA, 3 * D_DN, D_DN, N_HEADS_DN, N_HEADS_DN, N_HEADS_DN, N_HEADS_DN)
D_IN = 3 * D_NA + 4 * D_DN + 4 * N_HEADS_DN

kernel_name = "hybrid_natten_bigdn_encoder"


def _rmsnorm(x, g):
    xf = x.astype(jnp.float32)
    y = xf * lax.rsqrt(jnp.mean(xf * xf, axis=-1, keepdims=True) + RMS_EPS)
    return (y * g.astype(jnp.float32)).astype(x.dtype)


def _l2norm(x):
    return x * lax.rsqrt(jnp.sum(x * x, axis=-1, keepdims=True) + 1e-6)


def _dwconv_centred(x, w):
    k = w.shape[0]
    return lax.conv_general_dilated(x, w[:, None, :], window_strides=(1,), padding=[(k // 2, k // 2)],
                                    dimension_numbers=('NWC', 'WIO', 'NWC'), feature_group_count=x.shape[-1])


def _na_col_tables():
    n_cb = GRID_W // QB_W
    kc0 = np.clip(np.arange(n_cb) * QB_W - WIN_W // 2, 0, GRID_W - KB_W)
    col_idx = kc0[:, None] + np.arange(KB_W)[None, :]
    qcol = np.arange(GRID_W).reshape(n_cb, QB_W)
    cs = np.clip(qcol - WIN_W // 2, 0, GRID_W - WIN_W)[..., None]
    kcol = col_idx[:, None, :]
    col_mask = (kcol >= cs) & (kcol < cs + WIN_W)
    col_off = np.clip(kcol - qcol[..., None] + (WIN_W - 1), 0, 2 * WIN_W - 2)
    return col_idx, col_mask, col_off


def _neighbourhood_attention(q, k, v, rpb):
    b, t, _ = q.shape
    rows = t // GRID_W
    kh = min(WIN_H_MAX, rows)
    n_cb = GRID_W // QB_W
    col_idx, col_mask, col_off = _na_col_tables()
    grid = (b, rows, GRID_W, N_HEADS_NA, HEAD_DIM)
    qg = q.reshape(grid) * (HEAD_DIM ** -0.5)
    kg = k.reshape(grid)
    vg = v.reshape(grid)
    col_bias = rpb[:, :, col_off]
    mask = col_mask[:, :, None, :]

    def one_row(r):
        rs = jnp.clip(r - kh // 2, 0, rows - kh)
        kr = lax.dynamic_slice_in_dim(kg, rs, kh, axis=1)[:, :, col_idx]
        vr = lax.dynamic_slice_in_dim(vg, rs, kh, axis=1)[:, :, col_idx]
        qr = lax.dynamic_index_in_dim(qg, r, axis=1, keepdims=False).reshape(b, n_cb, QB_W, N_HEADS_NA, HEAD_DIM)
        s = jnp.einsum('bjqhd,bijchd->bhjqic', qr, kr).astype(jnp.float32)
        row_off = rs + jnp.arange(kh) - r + (WIN_H_MAX - 1)
        bias = jnp.take(col_bias, row_off, axis=1)
        s = s + jnp.transpose(bias, (0, 2, 3, 1, 4)).astype(jnp.float32)
        s = jnp.where(mask, s, -jnp.inf)
        p = jax.nn.softmax(s, axis=(-2, -1)).astype(vr.dtype)
        o = jnp.einsum('bhjqic,bijchd->bjqhd', p, vr)
        return o.reshape(b, GRID_W, D_NA)

    out = lax.map(one_row, jnp.arange(rows))
    return jnp.transpose(out, (1, 0, 2, 3)).reshape(b, t, D_NA)


def _gated_delta_chunked(q, k, v, g, beta):
    b, t, h, dk = q.shape
    dv = v.shape[-1]
    n = t // CHUNK

    def chunks(a):
        a = jnp.moveaxis(a, 2, 1)
        return a.reshape((b, h, n, CHUNK) + a.shape[3:])

    q, k, v, g, beta = (chunks(a) for a in (q, k, v, g, beta))
    gc = jnp.cumsum(g, axis=-1)
    tril = np.tril(np.ones((CHUNK, CHUNK), dtype=bool))
    strict = np.tril(np.ones((CHUNK, CHUNK), dtype=bool), -1)
    decay = jnp.exp(jnp.where(tril, gc[..., :, None] - gc[..., None, :], -jnp.inf))
    kb = k * beta[..., None]
    lmat = jnp.where(strict, jnp.einsum('bhncd,bhnsd->bhncs', kb, k) * decay, 0.0)
    eye = jnp.broadcast_to(jnp.eye(CHUNK, dtype=jnp.float32), lmat.shape)
    tinv = lax.linalg.triangular_solve(lmat, eye, left_side=True, lower=True, unit_diagonal=True)
    u = jnp.einsum('bhncs,bhnsd->bhncd', tinv, v * beta[..., None])
    w = jnp.einsum('bhncs,bhnsd->bhncd', tinv, kb * jnp.exp(gc)[..., None])
    qk = jnp.einsum('bhncd,bhnsd->bhncs', q, k) * decay
    qg = q * jnp.exp(gc)[..., None]
    kg = k * jnp.exp(gc[..., -1:] - gc)[..., None]
    glast = jnp.exp(gc[..., -1])

    def step(state, xs):
        qk_i, qg_i, w_i, u_i, kg_i, gl_i = xs
        v_new = u_i - jnp.einsum('bhcd,bhde->bhce', w_i, state)
        o_i = jnp.einsum('bhcd,bhde->bhce', qg_i, state) + jnp.einsum('bhcs,bhse->bhce', qk_i, v_new)
        state = state * gl_i[..., None, None] + jnp.einsum('bhcd,bhce->bhde', kg_i, v_new)
        return state, o_i

    xs = tuple(jnp.moveaxis(a, 2, 0) for a in (qk, qg, w, u, kg, glast))
    s0 = jnp.zeros((b, h, dk, dv), jnp.float32)
    _, o = lax.scan(step, s0, xs)
    return jnp.transpose(o, (1, 0, 3, 2, 4)).reshape(b, t, h, dv)


def _bidir_gated_deltanet(qkv, z, b_f, b_b, a_f, a_b, conv_w, a_log, dt_bias, norm_o):
    bsz, t, _ = qkv.shape
    out_dtype = z.dtype
    hs = (bsz, t, N_HEADS_DN, HEAD_DIM)
    act = jax.nn.silu(_dwconv_centred(qkv, conv_w)).astype(jnp.float32)
    q, k, v = jnp.split(act, 3, axis=-1)
    q = _l2norm(q.reshape(hs)) * (HEAD_DIM ** -0.5)
    k = _l2norm(k.reshape(hs))
    v = v.reshape(hs)

    def gates(a_raw, b_raw, d):
        g = -jnp.exp(a_log[d].astype(jnp.float32)) * jax.nn.softplus(a_raw.astype(jnp.float32) + dt_bias[d].astype(jnp.float32))
        return g, jax.nn.sigmoid(b_raw.astype(jnp.float32))

    g_f, beta_f = gates(a_f, b_f, 0)
    g_b, beta_b = gates(a_b, b_b, 1)
    flip = lambda a: jnp.flip(a, axis=1)
    o_f = _gated_delta_chunked(q, k, v, g_f, beta_f)
    o_b = flip(_gated_delta_chunked(flip(q), flip(k), flip(v), flip(g_b), flip(beta_b)))
    o = o_f + o_b
    o = o * lax.rsqrt(jnp.mean(o * o, axis=-1, keepdims=True) + RMS_EPS) * norm_o.astype(jnp.float32)
    o = o * jax.nn.silu(z.astype(jnp.float32).reshape(hs))
    return o.reshape(bsz, t, D_DN).astype(out_dtype)


def _memory_attention(xn, mem, norm_mem, w_q, w_kv, w_o):
    b, t, _ = xn.shape
    q = (xn @ w_q).reshape(b, t, N_HEADS_X, HEAD_DIM_X)
    k, v = jnp.split(_rmsnorm(mem, norm_mem) @ w_kv, 2, axis=-1)
    k = k.reshape(b, N_MEM, N_HEADS_X, HEAD_DIM_X)
    v = v.reshape(b, N_MEM, N_HEADS_X, HEAD_DIM_X)
    s = jnp.einsum('bthd,bmhd->bhtm', q, k).astype(jnp.float32) * (HEAD_DIM_X ** -0.5)
    p = jax.nn.softmax(s, axis=-1).astype(v.dtype)
    o = jnp.einsum('bhtm,bmhd->bthd', p, v).reshape(b, t, D_MODEL)
    return o @ w_o


def _conv_glu_ffn(xn, w_up, conv_w, conv_b, w_down):
    h = _dwconv_centred(xn @ w_up, conv_w) + conv_b
    val, gate = jnp.split(h, 2, axis=-1)
    return (jax.nn.silu(gate) * val) @ w_down


def _trunk(x, mem, norm_mix, w_in, rpb, conv_qkv, a_log, dt_bias, norm_o, w_out,
           norm_x, norm_mem, w_xq, w_xkv, w_xo, norm_ffn, w_up, conv_ffn, conv_ffn_b, w_down, norm_final):
    splits = list(np.cumsum(IN_WIDTHS)[:-1])
    for l in range(DEPTH):
        proj = _rmsnorm(x, norm_mix[l]) @ w_in[l]
        q_na, k_na, v_na, qkv_dn, z_dn, b_f, b_b, a_f, a_b = jnp.split(proj, splits, axis=-1)
        y_na = _neighbourhood_attention(q_na, k_na, v_na, rpb[l])
        y_dn = _bidir_gated_deltanet(qkv_dn, z_dn, b_f, b_b, a_f, a_b, conv_qkv[l], a_log[l], dt_bias[l], norm_o[l])
        x = x + jnp.concatenate([y_na, y_dn], axis=-1) @ w_out[l]
        x = x + _memory_attention(_rmsnorm(x, norm_x[l]), mem, norm_mem[l], w_xq[l], w_xkv[l], w_xo[l])
        x = x + _conv_glu_ffn(_rmsnorm(x, norm_ffn[l]), w_up[l], conv_ffn[l], conv_ffn_b[l], w_down[l])
    return _rmsnorm(x, norm_final)


def setup_inputs(seed: int = 0) -> dict:
    key = jax.random.key(seed)
    ks = jax.random.split(key, 24)
    f32 = jnp.float32

    def nrm(k, shape, scale):
        return jax.random.normal(k, shape, f32) * scale

    def gain(k, shape):
        return 1.0 + 0.02 * jax.random.normal(k, shape, f32)

    dt = jnp.exp(jax.random.uniform(ks[9], (DEPTH, 2, N_HEADS_DN), f32, np.log(1e-3), np.log(1e-1)))
    return {
        'x_prompt': nrm(ks[0], (BATCH, SEQ, D_MODEL), 1.0),
        'x_sample': nrm(ks[1], (DEC_BATCH, DEC_SEQ, D_MODEL), 1.0),
        'mem_prompt': nrm(ks[2], (BATCH, N_MEM, D_MODEL), 1.0),
        'mem_sample': nrm(ks[3], (DEC_BATCH, N_MEM, D_MODEL), 1.0),
        'norm_mix': gain(ks[4], (DEPTH, D_MODEL)),
        'w_in': nrm(ks[5], (DEPTH, D_MODEL, D_IN), D_MODEL ** -0.5),
        'rpb': nrm(ks[6], (DEPTH, N_HEADS_NA, 2 * WIN_H_MAX - 1, 2 * WIN_W - 1), 0.5),
        'conv_qkv': nrm(ks[7], (DEPTH, SHORT_CONV_W, 3 * D_DN), SHORT_CONV_W ** -0.5),
        'a_log': jnp.log(jax.random.uniform(ks[8], (DEPTH, 2, N_HEADS_DN), f32, 1.0, 16.0)),
        'dt_bias': dt + jnp.log(-jnp.expm1(-dt)),
        'norm_o': gain(ks[10], (DEPTH, HEAD_DIM)),
        'w_out': nrm(ks[11], (DEPTH, D_MIX, D_MODEL), D_MIX ** -0.5),
        'norm_x': gain(ks[12], (DEPTH, D_MODEL)),
        'norm_mem': gain(ks[13], (DEPTH, D_MODEL)),
        'w_xq': nrm(ks[14], (DEPTH, D_MODEL, D_MODEL), D_MODEL ** -0.5),
        'w_xkv': nrm(ks[15], (DEPTH, D_MODEL, 2 * D_MODEL), D_MODEL ** -0.5),
        'w_xo': nrm(ks[16], (DEPTH, D_MODEL, D_MODEL), D_MODEL ** -0.5),
        'norm_ffn': gain(ks[17], (DEPTH, D_MODEL)),
        'w_up': nrm(ks[18], (DEPTH, D_MODEL, 2 * D_FF), D_MODEL ** -0.5),
        'conv_ffn': nrm(ks[19], (DEPTH, FFN_CONV_W, 2 * D_FF), FFN_CONV_W ** -0.5),
        'conv_ffn_b': nrm(ks[20], (DEPTH, 2 * D_FF), 0.02),
        'w_down': nrm(ks[21], (DEPTH, D_FF, D_MODEL), D_FF ** -0.5),
        'norm_final': gain(ks[22], (D_MODEL,)),
    }


def reference(x_prompt, x_sample, mem_prompt, mem_sample, norm_mix, w_in, rpb, conv_qkv, a_log, dt_bias, norm_o,
              w_out, norm_x, norm_mem, w_xq, w_xkv, w_xo, norm_ffn, w_up, conv_ffn, conv_ffn_b, w_down, norm_final):
    weights = (norm_mix, w_in, rpb, conv_qkv, a_log, dt_bias, norm_o, w_out, norm_x, norm_mem, w_xq, w_xkv, w_xo,
               norm_ffn, w_up, conv_ffn, conv_ffn_b, w_down, norm_final)
    y_prompt = _trunk(x_prompt, mem_prompt, *weights)
    y_sample = _trunk(x_sample, mem_sample, *weights)
    return (y_prompt, y_sample)
```

```python
import numpy as np
from contextlib import ExitStack

import concourse.bass as bass
import concourse.mybir as mybir
from concourse.bass_utils import run_bass_kernel_spmd

F32 = mybir.dt.float32
BF16 = mybir.dt.bfloat16
AF = mybir.ActivationFunctionType
ALU = mybir.AluOpType

D = 1024
KC = 8
DIN = 3616
DFF = 2816
NMEM = 256
NEG = -30000.0
EPS = 1e-6
TT = 512

SAME_ENGINE_SYNC = True
ATTACH_WAIT = True


class Res:
    __slots__ = ("name", "last_w", "readers", "excl")

    def __init__(self, name="", last_w=None, excl=False):
        self.name = name
        self.last_w = last_w
        self.readers = []
        self.excl = excl


class Op:
    __slots__ = ("eng", "fn", "reads", "writes", "dma", "semkey", "val", "needs_inc", "idx", "barrier", "mm_stop", "atom")


class Sched:
    def __init__(self, nc):
        self.nc = nc
        self.ops = []
        self.eng_obj = {"pe": nc.tensor, "act": nc.scalar, "dve": nc.vector, "pool": nc.gpsimd, "sp": nc.sync}
        self.last_barrier = None
        self.all_res = []
        self.cap = None
        self.atom = None
        self.atom_ctr = 0

    def begin_atomic(self):
        self.atom_ctr += 1
        self.atom = self.atom_ctr

    def end_atomic(self):
        self.atom = None

    def res(self, name=""):
        r = Res(name, self.last_barrier)
        self.all_res.append(r)
        return r

    def pres(self, name=""):
        r = Res(name, self.last_barrier, excl=True)
        self.all_res.append(r)
        return r

    def _new(self, eng, fn, reads, writes):
        o = Op()
        ex = [r for r in reads if r.excl]
        if ex:
            reads = [r for r in reads if not r.excl]
            writes = list(writes) + [r for r in ex if r not in writes]
        o.eng, o.fn, o.reads, o.writes = eng, fn, tuple(reads), tuple(writes)
        o.dma, o.semkey, o.val, o.needs_inc, o.barrier = False, None, None, False, False
        o.mm_stop = None
        o.atom = self.atom
        if self.cap is not None:
            self.cap.append(o)
        else:
            self.ops.append(o)
        return o

    def op(self, eng, fn, reads=(), writes=()):
        return self._new(eng, fn, reads, writes)

    def dma(self, queue, fn, semkey, reads=(), writes=()):
        o = self._new(queue, fn, reads, writes)
        o.dma, o.semkey = True, semkey
        return o

    def capture(self):
        assert self.cap is None
        self.cap = []

    def end_capture(self):
        c, self.cap = self.cap, None
        return c

    def extend(self, ops):
        assert self.cap is None
        self.ops.extend(ops)

    def barrier(self):
        o = self._new("sp", lambda e: e.nop(), [], [])
        o.barrier = True
        return o

    def emit(self):
        nc = self.nc
        ops = self.ops
        n = len(ops)
        for i, o in enumerate(ops):
            o.idx = i
        deps_all = [None] * n
        last_dma_on_sem = {}
        last_on_eng = {}
        outstanding_dma = set()
        cur_barrier = None
        for o in ops:
            deps = set()
            if o.barrier:
                for e, d in last_on_eng.items():
                    deps.add(d)
                deps.update(outstanding_dma)
                outstanding_dma = set()
                cur_barrier = o.idx
            else:
                for r in o.reads:
                    lw = r.last_w
                    if lw is not None:
                        deps.add(lw.idx if isinstance(lw, Op) else lw)
                for w in o.writes:
                    lw = w.last_w
                    if lw is not None:
                        deps.add(lw.idx if isinstance(lw, Op) else lw)
                    deps.update(w.readers)
                if o.dma:
                    p = last_dma_on_sem.get(o.semkey)
                    if p is not None:
                        deps.add(p)
                    last_dma_on_sem[o.semkey] = o.idx
                    outstanding_dma.add(o.idx)
                if cur_barrier is not None:
                    deps.add(cur_barrier)
            deps.discard(o.idx)
            deps_all[o.idx] = deps
            for r in o.reads:
                r.readers.append(o.idx)
            for w in o.writes:
                w.last_w = o.idx
                w.readers = []
            last_on_eng[o.eng] = o.idx
        known = {e: {} for e in self.eng_obj}
        waits = [None] * n
        for o in ops:
            best = {}
            dmadeps = []
            for d in deps_all[o.idx]:
                p = ops[d]
                if p.dma:
                    dmadeps.append(d)
                else:
                    if p.eng == o.eng and not o.dma:
                        if p.eng in ("pe", "sp") or not SAME_ENGINE_SYNC:
                            continue
                    if p.eng not in best or best[p.eng] < d:
                        best[p.eng] = d
            w = []
            for e, d in best.items():
                kn = known[o.eng].get(("c", e), -1)
                if d > kn:
                    known[o.eng][("c", e)] = d
                    w.append(d)
                    ops[d].needs_inc = True
            for d in dmadeps:
                key = ("d", ops[d].semkey)
                kn = known[o.eng].get(key, -1)
                if d > kn:
                    known[o.eng][key] = d
                    w.append(d)
            waits[o.idx] = w
        sems = {}
        self._stack = []

        def get_sem(key):
            if key not in sems:
                cm = nc.semaphore("s%d" % len(sems))
                s = cm.__enter__()
                self._stack.append(cm)
                sems[key] = [s, 0]
            return sems[key]

        self.n_inst = {e: 0 for e in self.eng_obj}
        for o in ops:
            eng = self.eng_obj[o.eng]
            wl = waits[o.idx]
            attach = None
            if ATTACH_WAIT and wl and not o.barrier:
                attach = wl[-1]
                wl = wl[:-1]
            for d in wl:
                p = ops[d]
                key = ("dma", p.semkey) if p.dma else ("eng", p.eng)
                s = get_sem(key)[0]
                assert p.val is not None
                eng.wait_ge(s, p.val)
                self.n_inst[o.eng] += 1
            inst = o.fn(eng)
            if attach is not None:
                p = ops[attach]
                key = ("dma", p.semkey) if p.dma else ("eng", p.eng)
                inst.wait_op(get_sem(key)[0], p.val, "sem-ge")
            self.n_inst[o.eng] += 1
            if o.dma:
                sv = get_sem(("dma", o.semkey))
                sv[1] += 16
                inst.then_inc(sv[0], 16)
                o.val = sv[1]
            elif o.needs_inc:
                sv = get_sem(("eng", o.eng))
                sv[1] += 1
                inst.then_inc(sv[0], 1)
                o.val = sv[1]
        feng = self.eng_obj["sp"]
        for key, (s, v) in sems.items():
            if key[0] == "dma" and v > 0:
                feng.wait_ge(s, v)
        self.n_sems = len(sems)

    def close(self):
        for cm in reversed(self._stack):
            cm.__exit__(None, None, None)


def interleave(*lists):
    out = []
    its = [list(l) for l in lists]
    pos = [0] * len(its)
    tot = sum(len(l) for l in its)
    while len(out) < tot:
        best, bi = None, -1
        for i, l in enumerate(its):
            if pos[i] < len(l):
                frac = pos[i] / len(l)
                if best is None or frac < best:
                    best, bi = frac, i
        out.append(its[bi][pos[bi]])
        pos[bi] += 1
        if out[-1].atom is not None:
            a = out[-1].atom
            while pos[bi] < len(its[bi]) and its[bi][pos[bi]].atom == a:
                out.append(its[bi][pos[bi]])
                pos[bi] += 1
        elif out[-1].mm_stop is False:
            while pos[bi] < len(its[bi]):
                o = its[bi][pos[bi]]
                out.append(o)
                pos[bi] += 1
                if o.mm_stop is True:
                    break
    return out


def _na_pairs(T):
    rows = T // 64
    kh = 8
    npair = rows // 2
    out = []
    for i in range(npair):
        lst = []
        for j in range(npair):
            V = np.zeros((2, 2), bool)
            for b in range(2):
                r = 2 * i + b
                rs = min(max(r - kh // 2, 0), rows - kh)
                for a in range(2):
                    kr = 2 * j + a
                    V[a, b] = (rs <= kr < rs + kh)
            if not V.any():
                continue
            delta = j - i
            if V.all():
                assert -3 <= delta <= 3
                cls = delta + 3
            elif delta == -2 and V[0, 0] and V[1, 0] and V[1, 1] and not V[0, 1]:
                cls = 7
            elif delta == 2 and V[0, 1] and not V[0, 0] and not V[1, 0] and not V[1, 1]:
                cls = 8
            else:
                raise AssertionError("unexpected NA window pattern %s %d %d" % (V, i, j))
            lst.append((j, cls))
        out.append(lst)
    return out


def _na_bias_table(rpb):
    L = rpb.shape[0]
    a = np.arange(2)[:, None, None, None]
    kc = np.arange(64)[None, :, None, None]
    b = np.arange(2)[None, None, :, None]
    qc = np.arange(64)[None, None, None, :]
    cs = np.clip(qc - 8, 0, 48)
    colok = (kc >= cs) & (kc < cs + 16)
    coff = np.clip(kc - qc + 15, 0, 30)
    tab = np.full((L, 8, 9, 2, 64, 2, 64), NEG, np.float32)
    for cls in range(9):
        if cls < 7:
            delta = cls - 3
            rowok = np.ones((2, 1, 2, 1), bool)
        elif cls == 7:
            delta = -2
            rowok = ~((a == 0) & (b == 1))
        else:
            delta = 2
            rowok = (a == 0) & (b == 1)
        dr = 2 * delta + a - b + 7
        ok = np.broadcast_to(colok & rowok & (dr >= 0) & (dr <= 14), (2, 64, 2, 64))
        drc = np.broadcast_to(np.clip(dr, 0, 14), (2, 64, 2, 64))
        cof = np.broadcast_to(coff, (2, 64, 2, 64))
        g = rpb[:, :, drc, cof]
        tab[:, :, cls] = np.where(ok[None, None], g, np.float32(NEG))
    tab = tab.reshape(L, 8, 9, 128, 128)
    return np.ascontiguousarray(tab.transpose(0, 3, 1, 2, 4))


def _gdn_masks():
    p = np.arange(128)
    same = (p[:, None] // 64) == (p[None, :] // 64)
    r, c = p[:, None], p[None, :]
    m = np.zeros((128, 2, 3, 128), np.float32)
    m[:, 0, 0] = np.where(same & (r < c), 0.0, NEG)
    m[:, 0, 1] = np.where(same & (r <= c), 0.0, NEG)
    m[:, 0, 2] = np.where(same & (c < r), 0.0, -NEG)
    m[:, 1, 0] = np.where(same & (r > c), 0.0, NEG)
    m[:, 1, 1] = np.where(same & (r >= c), 0.0, NEG)
    m[:, 1, 2] = np.where(same & (c > r), 0.0, -NEG)
    return m


def _consts():
    c = {}
    c["c_ident"] = np.eye(128, dtype=np.float32)
    p = np.arange(128)
    c["c_onesbd"] = ((p[:, None] // 64) == (p[None, :] // 64)).astype(np.float32)
    c["c_gmask"] = _gdn_masks().reshape(128, 6 * 128)
    sel = np.zeros((48, 16, 128), np.float32)
    for i in range(16):
        sel[i, i, :] = 1.0
        sel[32 + i, i, :] = 1.0
    c["c_sel"] = sel.reshape(48, 16 * 128)
    selr = np.zeros((128, 16, 128), np.float32)
    for d_ in range(2):
        for h in range(8):
            r = 32 * d_ + h
            selr[r, d_ * 8 + h, :] = 1.0
            selr[64 + r, d_ * 8 + h, :] = 1.0
    c["c_selr"] = selr.reshape(128, 16 * 128)
    selp = np.zeros((128, 8, 128), np.float32)
    for d_ in range(2):
        for hp in range(4):
            for hh in range(2):
                r = 32 * d_ + 2 * hp + hh
                selp[r, d_ * 4 + hp, 64 * hh:64 * hh + 64] = 1.0
                selp[64 + r, d_ * 4 + hp, 64 * hh:64 * hh + 64] = 1.0
    c["c_selp"] = selp.reshape(128, 8 * 128)
    rst = np.ones((128, 512), np.float32)
    rst[:, ::64] = 0.0
    c["c_reset"] = rst
    return c


class Cfg:
    def __init__(self, n_p, T_p, n_s, T_s, depth, taps=(), stop_after=None, skip_gdn=False, gdn_stage=9):
        self.gdn_stage = gdn_stage
        self.n_p, self.T_p, self.n_s, self.T_s, self.depth = n_p, T_p, n_s, T_s, depth
        self.seqs = [("p", i, T_p) for i in range(n_p)] + [("s", i, T_s) for i in range(n_s)]
        self.offs = []
        o = 0
        for (_, _, T) in self.seqs:
            self.offs.append(o)
            o += T
        self.NT = o
        self.NS = len(self.seqs)
        self.taps = tuple(taps)
        self.stop_after = stop_after
        self.skip_gdn = skip_gdn


class B:
    pass


def build_program(cfg):
    nc = bass.Bass("TRN2", target_bir_lowering=False)
    L = cfg.depth
    NT, NS = cfg.NT, cfg.NS
    g = B()
    g.nc, g.cfg = nc, cfg

    def din(name, shape, dt=F32):
        return nc.dram_tensor(name, list(shape), dt, kind="ExternalInput").ap()

    def dscr(name, shape, dt):
        kind = "ExternalOutput" if name in cfg.taps else "Internal"
        return nc.dram_tensor(name, list(shape), dt, kind=kind).ap()

    I = {}
    I["x_p"] = din("x_p", [max(cfg.n_p, 1), cfg.T_p, D])
    I["x_s"] = din("x_s", [max(cfg.n_s, 1), cfg.T_s, D])
    I["m_p"] = din("m_p", [max(cfg.n_p, 1), NMEM, D])
    I["m_s"] = din("m_s", [max(cfg.n_s, 1), NMEM, D])
    for nm, shp in [("norm_mix", [L, D]), ("w_in", [L, D, DIN]), ("conv_qkv", [L, 3, 1536]), ("a_log", [L, 16]),
                    ("dt_bias", [L, 16]), ("norm_o", [L, 64]), ("w_out", [L, D, D]), ("norm_x", [L, D]),
                    ("norm_mem", [L, D]), ("w_xq", [L, D, D]), ("w_xkv", [L, D, 2 * D]), ("w_xo", [L, D, D]),
                    ("norm_ffn", [L, D]), ("w_up", [L, D, 2 * DFF]), ("conv_ffn", [L, 3, 2 * DFF]),
                    ("conv_ffn_b", [L, 2 * DFF]), ("w_down", [L, DFF, D]), ("norm_final", [D]),
                    ("na_bias", [L, 128, 8 * 9 * 128]), ("c_ident", [128, 128]), ("c_onesbd", [128, 128]),
                    ("c_gmask", [128, 768]), ("c_sel", [48, 2048]), ("c_selr", [128, 2048]),
                    ("c_selp", [128, 1024]), ("c_reset", [128, 512])]:
        I[nm] = din(nm, shp)
    O = {}
    O["y_p"] = nc.dram_tensor("y_p", [max(cfg.n_p, 1), cfg.T_p, D], F32, kind="ExternalOutput").ap()
    O["y_s"] = nc.dram_tensor("y_s", [max(cfg.n_s, 1), cfg.T_s, D], F32, kind="ExternalOutput").ap()
    W = {}
    W["XT"] = dscr("XT", [D, NT], F32)
    W["MH"] = dscr("MH", [D, NS * NMEM], F32)
    W["QKN"] = dscr("QKN", [1024, NT], BF16)
    W["VN"] = dscr("VN", [NT, 512], BF16)
    W["QKVD"] = dscr("QKVD", [1536, NT], BF16)
    W["ZT"] = dscr("ZT", [512, NT], BF16)
    W["GR"] = dscr("GR", [32, NT], F32)
    W["YMIX"] = dscr("YMIX", [1024, NT], BF16)
    W["XN3"] = dscr("XN3", [1024, NT], BF16)
    g.I, g.O, g.W = I, O, W

    S = Sched(nc)
    g.S = S
    with ExitStack() as top:
        uniq = [0]

        def sbt(es, name, shape, dt=F32):
            uniq[0] += 1
            return es.enter_context(nc.sbuf_tensor("%s_%d" % (name, uniq[0]), list(shape), dt))

        def pst(es, name, shape, dt=F32):
            uniq[0] += 1
            return es.enter_context(nc.psum_tensor("%s_%d" % (name, uniq[0]), list(shape), dt))

        g.sbt, g.pst = sbt, pst
        g.IDB = sbt(top, "IDB", [128, 128], BF16)
        g.ONESB = sbt(top, "ONESB", [128, 128], BF16)
        g.ONESBD = sbt(top, "ONESBD", [128, 128], BF16)
        g.NORMS = sbt(top, "NORMS", [128, 3 * L + 1, KC], F32)
        g.NMEMG = sbt(top, "NMEMG", [128, L, KC], F32)
        g.rconst = S.res("const")
        S.dma("pool", lambda e: e.dma_start(out=g.IDB[:], in_=I["c_ident"]), "c0", writes=[g.rconst])
        S.dma("pool", lambda e: e.dma_start(out=g.ONESBD[:], in_=I["c_onesbd"]), "c1", writes=[g.rconst])
        S.op("dve", lambda e: e.memset(g.ONESB[:], 1.0), writes=[g.rconst])
        with nc.allow_non_contiguous_dma("tiny per-partition gain vectors"):
            k = 0
            for l in range(L):
                for j, nm in enumerate(("norm_mix", "norm_x", "norm_ffn")):
                    src = I[nm][l].rearrange("(kc p) -> p kc", p=128)
                    S.dma("sp", lambda e, src=src, idx=3 * l + j: e.dma_start(out=g.NORMS[:, idx, :], in_=src, allow_slow_non_contiguous=True),
                          "c2", writes=[g.rconst])
                src = I["norm_mem"][l].rearrange("(kc p) -> p kc", p=128)
                S.dma("sp", lambda e, src=src, l=l: e.dma_start(out=g.NMEMG[:, l, :], in_=src, allow_slow_non_contiguous=True), "c2", writes=[g.rconst])
            src = I["norm_final"].rearrange("(kc p) -> p kc", p=128)
            S.dma("sp", lambda e, src=src: e.dma_start(out=g.NORMS[:, 3 * L, :], in_=src, allow_slow_non_contiguous=True), "c2", writes=[g.rconst])

        phase0(g)
        if cfg.stop_after != "0":
            for l in range(L):
                phaseA(g, l)
                if cfg.stop_after == "A":
                    break
                phaseB(g, l)
                if cfg.stop_after == "B":
                    break
                phaseC(g, l)
                if cfg.stop_after == "C":
                    break
                phaseM(g, l)
                phaseD(g, l)
                if cfg.stop_after == "D":
                    break
                phaseE(g, l)
            if cfg.stop_after is None:
                phaseF(g)
        S.barrier()
        S.emit()
        S.close()
    g.n_inst = S.n_inst
    return nc, g


def MM(g, out, lhsT, rhs, start, stop, reads, writes):
    o = g.S.op("pe", lambda e: e.matmul(out, lhsT, rhs, start=start, stop=stop), reads, writes)
    o.mm_stop = bool(stop)
    return o


def ACT(g, out, in_, func, reads, writes, **kw):
    return g.S.op("act", lambda e: e.activation(out=out, in_=in_, func=func, **kw), reads, writes)


def TT_(g, eng, out, in0, in1, op, reads, writes):
    return g.S.op(eng, lambda e: e.tensor_tensor(out=out, in0=in0, in1=in1, op=op), reads, writes)


def TS_(g, eng, out, in0, s1, s2, op0, op1, reads, writes, **kw):
    if op1 is None and eng == "pool" and op0 == ALU.mult:
        op1, s2 = ALU.add, 0.0
    if op1 is None:
        return g.S.op(eng, lambda e: e.tensor_scalar(out=out, in0=in0, scalar1=s1, scalar2=None, op0=op0, **kw), reads, writes)
    return g.S.op(eng, lambda e: e.tensor_scalar(out=out, in0=in0, scalar1=s1, scalar2=s2, op0=op0, op1=op1, **kw), reads, writes)


def STT(g, out, in0, scalar, in1, op0, op1, reads, writes):
    return g.S.op("dve", lambda e: e.scalar_tensor_tensor(out=out, in0=in0, scalar=scalar, in1=in1, op0=op0, op1=op1), reads, writes)


def CP(g, eng, out, in_, reads, writes):
    if eng == "act":
        return g.S.op("act", lambda e: e.activation(out=out, in_=in_, func=AF.Copy), reads, writes)
    return g.S.op(eng, lambda e: e.tensor_copy(out=out, in_=in_), reads, writes)


def DMA(g, q, out, in_, semkey, reads, writes):
    return g.S.dma(q, lambda e: e.dma_start(out=out, in_=in_), semkey, reads, writes)


def x_src(g, si):
    grp, i, T = g.cfg.seqs[si]
    return (g.I["x_p"] if grp == "p" else g.I["x_s"])[i]


def y_dst(g, si):
    grp, i, T = g.cfg.seqs[si]
    return (g.O["y_p"] if grp == "p" else g.O["y_s"])[i]


def m_src(g, si):
    grp, i, T = g.cfg.seqs[si]
    return (g.I["m_p"] if grp == "p" else g.I["m_s"])[i]


def phase0(g):
    nc, S, cfg = g.nc, g.S, g.cfg
    S.barrier()
    with ExitStack() as es:
        NB = 2
        XL = [g.sbt(es, "p0_xl%d" % i, [128, D], F32) for i in range(NB)]
        HI = [g.sbt(es, "p0_hi%d" % i, [128, 3, D], BF16) for i in range(NB)]
        R1 = [g.sbt(es, "p0_r%d" % i, [128, D], F32) for i in range(NB)]
        ST = [g.sbt(es, "p0_st%d" % i, [128, KC, 512], F32) for i in range(2)]
        SS = g.sbt(es, "p0_ss", [128, 4], F32)
        PS = [g.pst(es, "p0_ps%d" % i, [128, 512], F32) for i in range(4)]
        rXL = [S.res() for _ in range(NB)]
        rHI = [S.res() for _ in range(NB)]
        rR1 = [S.res() for _ in range(NB)]
        rST = [S.res() for _ in range(2)]
        rSS = S.res()
        rPS = [S.pres() for _ in range(4)]
        rdram = S.res()
        cnt = [0]

        def do_tile(src_rows, dst, dst_col0, slot_in_group, stg, is_mem):
            b = cnt[0] % NB
            cnt[0] += 1
            DMA(g, "sp", XL[b][:], src_rows, "p0l%d" % b, [], [rXL[b]])
            xin = XL[b]
            if is_mem:
                ACT(g, R1[b][:], XL[b][:], AF.Square, [rXL[b]], [rR1[b], rSS], accum_out=SS[:, 0:1])
                ACT(g, SS[:, 1:2], SS[:, 0:1], AF.Sqrt, [rSS], [rSS], scale=1.0 / D, bias=EPS)
                S.op("dve", lambda e: e.reciprocal(out=SS[:, 2:3], in_=SS[:, 1:2]), [rSS], [rSS])
                TS_(g, "dve", XL[b][:], XL[b][:], SS[:, 2:3], None, ALU.mult, None, [rXL[b], rSS], [rXL[b]])
            CP(g, "act", HI[b][:, 0, :], xin[:], [rXL[b]], [rHI[b]])
            TT_(g, "dve", R1[b][:], xin[:], HI[b][:, 0, :], ALU.subtract, [rXL[b], rHI[b]], [rR1[b]])
            CP(g, "act", HI[b][:, 1, :], R1[b][:], [rR1[b]], [rHI[b]])
            TT_(g, "dve", R1[b][:], R1[b][:], HI[b][:, 1, :], ALU.subtract, [rR1[b], rHI[b]], [rR1[b]])
            CP(g, "act", HI[b][:, 2, :], R1[b][:], [rR1[b]], [rHI[b]])
            for half in range(2):
                pi = (2 * cnt[0] + half) % 4
                for q in range(4):
                    fc = half * 4 + q
                    for part in range(3):
                        MM(g, PS[pi][:, q * 128:(q + 1) * 128], HI[b][:, part, fc * 128:(fc + 1) * 128], g.IDB[:],
                           part == 0, part == 2, [rHI[b], g.rconst], [rPS[pi]])
                outv = ST[stg][:, half * 4:(half + 1) * 4, slot_in_group * 128:(slot_in_group + 1) * 128]
                inv = PS[pi][:].rearrange("p (q t) -> p q t", q=4)
                if half == 0:
                    CP(g, "act", outv, inv, [rPS[pi]], [rST[stg]])
                else:
                    CP(g, "dve", outv, inv, [rPS[pi]], [rST[stg]])

        grp_i = 0
        for si, (grp, i, T) in enumerate(cfg.seqs):
            xs = x_src(g, si)
            for t4 in range(T // 512):
                stg = grp_i % 2
                grp_i += 1
                for s in range(4):
                    r0 = t4 * 512 + s * 128
                    do_tile(xs[r0:r0 + 128, :], None, None, s, stg, False)
                col0 = cfg.offs[si] + t4 * 512
                dst = g.W["XT"].rearrange("(kc p) t -> p kc t", p=128)[:, :, col0:col0 + 512]
                DMA(g, "act", dst, ST[stg][:], "p0s%d" % stg, [rST[stg]], [rdram])
        for si in range(cfg.NS):
            ms = m_src(g, si)
            stg = grp_i % 2
            grp_i += 1
            for s in range(2):
                do_tile(ms[s * 128:(s + 1) * 128, :], None, None, s, stg, True)
            col0 = si * NMEM
            dst = g.W["MH"].rearrange("(kc p) t -> p kc t", p=128)[:, :, col0:col0 + 256]
            DMA(g, "act", dst, ST[stg][:, :, 0:256], "p0s%d" % stg, [rST[stg]], [rdram])
        S.barrier()


def rmsnorm_fm(g, XF, rXF, XN, rXN, SQ, rSQ, RSTD, rRSTD, PSs, rPSs, gain_idx, ncols, col0=0, engs=("dve", "pool")):
    ACT(g, SQ[:, :, 0:ncols], XF[:, :, 0:ncols], AF.Square, [rXF], [rSQ])
    for kc in range(KC):
        MM(g, PSs[:, 0:ncols], g.ONESB[:], SQ[:, kc, 0:ncols], kc == 0, kc == KC - 1, [rSQ, g.rconst], [rPSs])
    ACT(g, RSTD[:, 0:ncols], PSs[:, 0:ncols], AF.Sqrt, [rPSs], [rRSTD], scale=1.0 / D, bias=EPS)
    g.S.op("dve", lambda e: e.reciprocal(out=RSTD[:, 0:ncols], in_=RSTD[:, 0:ncols]), [rRSTD], [rRSTD])
    for kc in range(KC):
        STT(g, XN[:, kc, col0:col0 + ncols], XF[:, kc, 0:ncols], g.NORMS[:, gain_idx, kc:kc + 1], RSTD[:, 0:ncols],
            ALU.mult, ALU.mult, [rXF, rRSTD, g.rconst], [rXN])


def phaseA(g, l):
    nc, S, cfg = g.nc, g.S, g.cfg
    S.barrier()
    with ExitStack() as es:
        WIN = g.sbt(es, "A_win", [128, KC, DIN], BF16)
        rW = S.res()
        wsrc = g.I["w_in"][l].rearrange("(kc p) n -> p kc n", p=128)
        for kc in range(KC):
            DMA(g, "pool", WIN[:, kc, :], wsrc[:, kc, :], "A_w%d" % (kc % 4), [], [rW])
        XF = [g.sbt(es, "A_xf%d" % i, [128, KC, TT], F32) for i in range(2)]
        rXF = [S.res() for _ in range(2)]
        SQ = g.sbt(es, "A_sq", [128, KC, TT], BF16)
        rSQ = S.res()
        RSTD = g.sbt(es, "A_rstd", [128, TT], F32)
        rRSTD = S.res()
        XN = [g.sbt(es, "A_xn%d" % i, [128, KC, TT], BF16) for i in range(2)]
        rXN = [S.res() for _ in range(2)]
        STG = [g.sbt(es, "A_stg%d" % i, [128, 4, TT], BF16) for i in range(3)]
        rSTG = [S.res() for _ in range(3)]
        GST = [g.sbt(es, "A_gst%d" % i, [32, TT], F32) for i in range(2)]
        rGST = [S.res() for _ in range(2)]
        PSs = g.pst(es, "A_pss", [128, TT], F32)
        rPSs = S.pres()
        PS = [g.pst(es, "A_ps%d" % i, [128, TT], F32) for i in range(6)]
        rPS = [S.pres() for _ in range(6)]
        rdram = S.res()
        ntile = cfg.NT // TT
        psi = [0]
        stgi = [0]
        XTv = g.W["XT"].rearrange("(kc p) t -> p kc t", p=128)
        def prepart(ti):
            c0 = ti * TT
            b = ti % 2
            DMA(g, "sp", XF[b][:], XTv[:, :, c0:c0 + TT], "A_x%d" % b, [], [rXF[b]])
            rmsnorm_fm(g, XF[b], rXF[b], XN[b], rXN[b], SQ, rSQ, RSTD, rRSTD, PSs, rPSs, 3 * l + 0, TT)

        def mainpart(ti):
            c0 = ti * TT
            b = ti % 2
            groups = [(g.W["QKN"], 0, 0, 4, 0.125), (g.W["QKN"], 512, 512, 4, None),
                      (g.W["QKVD"], 0, 1536, 4, None), (g.W["QKVD"], 512, 2048, 4, None), (g.W["QKVD"], 1024, 2560, 4, None),
                      (g.W["ZT"], 0, 3072, 4, None)]
            for (dst, r0, wc0, nm, scale) in groups:
                sg = stgi[0] % 3
                stgi[0] += 1
                for m in range(nm):
                    p = psi[0] % 6
                    psi[0] += 1
                    for kc in range(KC):
                        MM(g, PS[p][:], WIN[:, kc, wc0 + m * 128: wc0 + (m + 1) * 128], XN[b][:, kc, :], kc == 0, kc == KC - 1,
                           [rW, rXN[b]], [rPS[p]])
                    if scale is not None:
                        ACT(g, STG[sg][:, m, :], PS[p][:], AF.Copy, [rPS[p]], [rSTG[sg]], scale=scale)
                    elif m % 2 == 0:
                        CP(g, "act", STG[sg][:, m, :], PS[p][:], [rPS[p]], [rSTG[sg]])
                    else:
                        CP(g, "dve", STG[sg][:, m, :], PS[p][:], [rPS[p]], [rSTG[sg]])
                d = dst[r0:r0 + 512, c0:c0 + TT].rearrange("(m p) t -> p m t", p=128)
                DMA(g, "sp", d, STG[sg][:], "A_st%d" % sg, [rSTG[sg]], [rdram])
            p = psi[0] % 6
            psi[0] += 1
            for kc in range(KC):
                MM(g, PS[p][0:32, :], WIN[:, kc, 3584:3616], XN[b][:, kc, :], kc == 0, kc == KC - 1, [rW, rXN[b]], [rPS[p]])
            gb = ti % 2
            CP(g, "dve", GST[gb][:], PS[p][0:32, :], [rPS[p]], [rGST[gb]])
            DMA(g, "sp", g.W["GR"][:, c0:c0 + TT], GST[gb][:], "A_g%d" % gb, [rGST[gb]], [rdram])
            sg = stgi[0] % 3
            stgi[0] += 1
            for ts in range(4):
                p = psi[0] % 6
                psi[0] += 1
                for kc in range(KC):
                    MM(g, PS[p][:], XN[b][:, kc, ts * 128:(ts + 1) * 128], WIN[:, kc, 1024:1536], kc == 0, kc == KC - 1,
                       [rW, rXN[b]], [rPS[p]])
                CP(g, "act" if ts % 2 == 0 else "dve", STG[sg][:, ts, :], PS[p][:], [rPS[p]], [rSTG[sg]])
            d = g.W["VN"][c0:c0 + TT, :].rearrange("(s p) f -> p s f", p=128)
            DMA(g, "sp", d, STG[sg][:], "A_st%d" % sg, [rSTG[sg]], [rdram])

        def cap(fn, *a):
            S.capture()
            fn(*a)
            return S.end_capture()

        S.extend(cap(prepart, 0))
        for ti in range(ntile):
            lists = [cap(mainpart, ti)]
            if ti + 1 < ntile:
                lists.append(cap(prepart, ti + 1))
            S.extend(interleave(*lists))
        S.barrier()


def phaseB(g, l):
    nc, S, cfg = g.nc, g.S, g.cfg
    S.barrier()
    Tmax = max(T for (_, _, T) in cfg.seqs)
    with ExitStack() as es:
        BIAS = g.sbt(es, "B_bias", [128, 8, 9, 128], BF16)
        rBIAS = S.res()
        bsrc = g.I["na_bias"][l].rearrange("p (h c q) -> p h c q", h=8, c=9)
        for h in range(8):
            DMA(g, "pool", BIAS[:, h], bsrc[:, h], "B_b%d" % (h % 2), [], [rBIAS])
        NB = 2
        QT = [g.sbt(es, "B_qt%d" % i, [128, Tmax], BF16) for i in range(NB)]
        KT = [g.sbt(es, "B_kt%d" % i, [128, Tmax], BF16) for i in range(NB)]
        V = [g.sbt(es, "B_v%d" % i, [128, Tmax // 128, 2, 65], BF16) for i in range(NB)]
        YT = [g.sbt(es, "B_yt%d" % i, [128, Tmax], BF16) for i in range(NB)]
        rQT = [S.res() for _ in range(NB)]
        rKT = [S.res() for _ in range(NB)]
        rV = [S.res() for _ in range(NB)]
        rYT = [S.res() for _ in range(NB)]
        for i in range(NB):
            S.op("pool", lambda e, i=i: e.memset(V[i][:], 1.0), [], [rV[i]])
        NP_ = 3
        PT = [g.sbt(es, "B_pt%d" % i, [128, 5, 128], BF16) for i in range(NP_)]
        rPT = [S.res() for _ in range(NP_)]
        YTM = [g.sbt(es, "B_ytm%d" % i, [128, 128], BF16) for i in range(2)]
        rYTM = [S.res() for _ in range(2)]
        RC = [g.sbt(es, "B_rc%d" % i, [128, 1], F32) for i in range(4)]
        rRC = [S.res() for _ in range(4)]
        PSS = [g.pst(es, "B_pss%d" % i, [128, 1024], F32) for i in range(2)]
        rPSS = [S.pres() for _ in range(2)]
        PSO = [g.pst(es, "B_pso%d" % i, [128, 512], F32) for i in range(2)]
        rPSO = [S.pres() for _ in range(2)]
        PSTr = [g.pst(es, "B_pst%d" % i, [128, 512], F32) for i in range(2)]
        rPSTr = [S.pres() for _ in range(2)]
        rdram = S.res()
        it = 0
        cnt = 0
        units = [(si_, hp_) for si_ in range(len(cfg.seqs)) for hp_ in range(4)]

        def load_unit(u):
            si_, hp_ = units[u]
            T_ = cfg.seqs[si_][2]
            c0_ = cfg.offs[si_]
            b_ = u % NB
            DMA(g, "sp", QT[b_][:, 0:T_], g.W["QKN"][hp_ * 128:(hp_ + 1) * 128, c0_:c0_ + T_], "B_q%d" % b_, [], [rQT[b_]])
            DMA(g, "sp", KT[b_][:, 0:T_], g.W["QKN"][512 + hp_ * 128:512 + (hp_ + 1) * 128, c0_:c0_ + T_], "B_k%d" % b_, [], [rKT[b_]])
            vsrc = g.W["VN"][c0_:c0_ + T_, hp_ * 128:(hp_ + 1) * 128].rearrange("(t p) (h e) -> p t h e", p=128, h=2)
            for hh_ in range(2):
                DMA(g, "pool", V[b_][:, 0:T_ // 128, hh_, 0:64], vsrc[:, :, hh_, :], "B_v%d%d" % (b_, hh_), [], [rV[b_]])

        load_unit(0)
        for si, (grp, sidx, T) in enumerate(cfg.seqs):
            pairs = _na_pairs(T)
            c0 = cfg.offs[si]
            ntl = T // 128
            for hp in range(4):
                b = it % NB
                it += 1
                if it < len(units):
                    load_unit(it)
                for i in range(ntl):
                    yb = i % 2
                    for hh in range(2):
                        h = 2 * hp + hh
                        base = 64 * hh
                        lst = pairs[i]
                        nt = len(lst)
                        sp_ = cnt % 2
                        pp = cnt % NP_
                        rc = cnt % 4
                        cnt += 1
                        for jt, (j, cls) in enumerate(lst):
                            MM(g, PSS[sp_][:, jt * 128:(jt + 1) * 128], KT[b][base:base + 64, j * 128:(j + 1) * 128],
                               QT[b][base:base + 64, i * 128:(i + 1) * 128], True, False, [rKT[b], rQT[b]], [rPSS[sp_]])
                            MM(g, PSS[sp_][:, jt * 128:(jt + 1) * 128], g.IDB[:], BIAS[:, h, cls, :], False, True,
                               [g.rconst, rBIAS], [rPSS[sp_]])
                        n1 = min(nt, 4)
                        ACT(g, PT[pp][:, 0:n1, :], PSS[sp_][:, 0:n1 * 128].rearrange("p (a b) -> p a b", a=n1), AF.Exp,
                            [rPSS[sp_]], [rPT[pp]])
                        if nt > 4:
                            ACT(g, PT[pp][:, 4, :], PSS[sp_][:, 512:640], AF.Exp, [rPSS[sp_]], [rPT[pp]])
                        for jt, (j, cls) in enumerate(lst):
                            MM(g, PSO[sp_][:, 0:65], PT[pp][:, jt, :], V[b][:, j, hh, :], jt == 0, jt == nt - 1,
                               [rPT[pp], rV[b]], [rPSO[sp_]])
                        S.op("dve", lambda e, rc=rc, sp_=sp_: e.reciprocal(out=RC[rc][:], in_=PSO[sp_][:, 64:65]),
                             [rPSO[sp_]], [rRC[rc]])
                        TS_(g, "dve", YTM[yb][:, base:base + 64], PSO[sp_][:, 0:64], RC[rc][:, 0:1], None, ALU.mult, None,
                            [rPSO[sp_], rRC[rc]], [rYTM[yb]])
                    tp = i % 2
                    MM(g, PSTr[tp][:, 0:128], YTM[yb][:], g.IDB[:], True, True, [rYTM[yb], g.rconst], [rPSTr[tp]])
                    CP(g, "act", YT[b][:, i * 128:(i + 1) * 128], PSTr[tp][:, 0:128], [rPSTr[tp]], [rYT[b]])
                DMA(g, "sp", g.W["YMIX"][hp * 128:(hp + 1) * 128, c0:c0 + T], YT[b][:, 0:T], "B_y%d" % b, [rYT[b]], [rdram])
        S.barrier()


def phaseC(g, l):
    if g.cfg.skip_gdn:
        nc, S, cfg = g.nc, g.S, g.cfg
        S.barrier()
        with ExitStack() as es:
            Z = g.sbt(es, "C_z", [128, 4, 2048], BF16)
            rZ, rd = S.res(), S.res()
            S.op("dve", lambda e: e.memset(Z[:], 0.0), [], [rZ])
            for c0 in range(0, cfg.NT, 2048):
                w = min(2048, cfg.NT - c0)
                DMA(g, "sp", g.W["YMIX"][512:1024, c0:c0 + w].rearrange("(m p) t -> p m t", p=128), Z[:, :, 0:w], "C_z", [rZ], [rd])
            S.barrier()
        return
    phaseC_real(g, l)


def phaseC_real(g, l):
    nc, S, cfg = g.nc, g.S, g.cfg
    S.barrier()
    Tmax = max(T for (_, _, T) in cfg.seqs)
    NTLmax = Tmax // 128
    NCHmax = Tmax // 64
    GB = 256
    CB = 1024 if all(T % 1024 == 0 for (_, _, T) in cfg.seqs) else 512
    I, W = g.I, g.W
    with ExitStack() as es:
        sb = lambda n, shp, dt=F32: g.sbt(es, "C_" + n, shp, dt)
        MASK = sb("mask", [128, 2, 3, 128], BF16)
        SELR = sb("selr", [128, 16, 128], BF16)
        SELP = sb("selp", [128, 8, 128], BF16)
        IDF = sb("idf", [128, 128], F32)
        RESET = sb("reset", [128, GB], F32)
        CWQ = sb("cwq", [128, 12, 3], F32)
        NORMO = sb("normo", [128, 1], F32)
        DTB = sb("dtb", [40, 1], F32)
        NEGA = sb("nega", [40, 1], F32)
        rC = S.res()
        DMA(g, "pool", MASK[:].rearrange("p a b c -> p (a b c)"), I["c_gmask"], "C_c0", [], [rC])
        DMA(g, "pool", SELR[:].rearrange("p a b -> p (a b)"), I["c_selr"], "C_c1", [], [rC])
        DMA(g, "pool", SELP[:].rearrange("p a b -> p (a b)"), I["c_selp"], "C_c0", [], [rC])
        DMA(g, "sp", IDF[:], I["c_ident"], "C_c2", [], [rC])
        DMA(g, "sp", RESET[:], I["c_reset"][:, 0:GB], "C_c2", [], [rC])
        for k in range(3):
            small_vec_load(g, "sp", CWQ[:, :, k], I["conv_qkv"][l, k].rearrange("(f p) -> p f", p=128), "C_c2", [rC])
        for hh in range(2):
            small_vec_load(g, "sp", NORMO[64 * hh:64 * hh + 64, :], I["norm_o"][l].rearrange("(p o) -> p o", o=1), "C_c2", [rC])
        S.op("dve", lambda e: e.memset(DTB[:], 0.0), [], [rC])
        S.op("dve", lambda e: e.memset(NEGA[:], 0.0), [], [rC])
        for d_ in range(2):
            small_vec_load(g, "sp", DTB[32 * d_:32 * d_ + 8, :], I["dt_bias"][l, 8 * d_:8 * d_ + 8].rearrange("(p o) -> p o", o=1), "C_c2", [rC])
            small_vec_load(g, "sp", NEGA[32 * d_:32 * d_ + 8, :], I["a_log"][l, 8 * d_:8 * d_ + 8].rearrange("(p o) -> p o", o=1), "C_c2", [rC])
        ACT(g, NEGA[:], NEGA[:], AF.Exp, [rC], [rC])
        TS_(g, "dve", NEGA[:], NEGA[:], -1.0, None, ALU.mult, None, [rC], [rC])

        GCS = sb("gcs", [128, Tmax], BF16)
        HS = sb("hs", [128, Tmax], BF16)
        GLS = sb("gls", [128, NCHmax], BF16)
        SCT = sb("sct", [128, NTLmax, 200], F32)
        rGCS, rHS, rGLS, rSCT = S.res(), S.res(), S.res(), S.res()
        for t_ in (GCS, HS, GLS):
            S.op("pool", lambda e, t_=t_: e.memset(t_[:], 0.0), [], [rGCS, rHS, rGLS])
        AR = sb("ar", [40, GB]); BR = sb("br", [40, GB])
        rAR, rBR = S.res(), S.res()
        S.op("pool", lambda e: e.memset(AR[:], 0.0), [], [rAR])
        S.op("pool", lambda e: e.memset(BR[:], 0.0), [], [rBR])
        G_ = sb("g", [40, GB]); PRE = sb("pre", [40, GB]); GC = sb("gc", [40, GB]); LNB = sb("lnb", [40, GB])
        TMPG = sb("tmpg", [40, GB])
        RR = sb("rr", [40, 5, GB])
        RH = sb("rh", [40, 5, GB], BF16); RL = sb("rl", [40, 5, GB], BF16)
        GLF = sb("glf", [40, GB // 64])
        rG = S.res()
        S.op("dve", lambda e: e.memset(GC[:], 0.0), [], [rG])

        XR = sb("xr", [128, 3, CB + 2], BF16)
        ACC = [sb("acc%d" % i, [128, CB]) for i in range(2)]
        SQB = sb("sqb", [128, CB], BF16)
        RSB = sb("rsb", [128, 512])
        QT = sb("qt", [128, Tmax], BF16); KT = sb("kt", [128, Tmax], BF16); VT = sb("vt", [128, Tmax], BF16)
        KTZ = sb("ktz", [128, 2, Tmax], BF16)
        rKTZ = S.res()
        S.op("pool", lambda e: e.memset(KTZ[:], 0.0), [], [rKTZ])
        SZ = sb("sz", [128, Tmax], BF16)
        YT = sb("yt", [128, Tmax], BF16)
        OF = sb("of", [128, NTLmax, 128])
        GLB = sb("glb", [128, 2, NCHmax])
        rXR, rSQB, rRSB, rQT, rKT, rVT, rSZ, rYT, rOF, rGLB = [S.res() for _ in range(10)]
        rACC = [S.res() for _ in range(2)]
        KVTM = [sb("kvtm%d" % i, [128, 2, 128], BF16) for i in range(4)]
        E01 = [sb("e01%d" % i, [128, 2, 256]) for i in range(4)]
        E2 = [sb("e2%d" % i, [128, 2, 128]) for i in range(4)]
        EGC = [sb("egc%d" % i, [128, 128]) for i in range(4)]
        PB_A = [[sb("pb%d_%d" % (d_, i), [128, 2, 128], BF16) for i in range(2)] for d_ in range(2)]
        PTY_A = [[sb("pty%d_%d" % (d_, i), [128, 2, 2, 128], BF16) for i in range(2)] for d_ in range(2)]
        YF_A = [sb("yf%d" % d_, [128, 2, 128]) for d_ in range(2)]
        TB = [sb("tb%d" % i, [128, 2, 128], BF16) for i in range(4)]
        TBG = [sb("tbg%d" % i, [128, 2, 128], BF16) for i in range(4)]
        U = [sb("u%d" % i, [128, 128]) for i in range(4)]
        WT = [sb("wt%d" % i, [128, 2, 128], BF16) for i in range(4)]
        QGT = [sb("qgt%d" % i, [128, 2, 128], BF16) for i in range(4)]
        QKT = [sb("qkt%d" % i, [128, 2, 128], BF16) for i in range(4)]
        KGZ = [sb("kgz%d" % i, [128, 2, 128], BF16) for i in range(4)]
        rKVTM, rE01, rE2, rEGC, rTB, rTBG, rU, rWT, rQGT, rQKT, rKGZ = [[S.res() for _ in range(4)] for _ in range(11)]
        rPB_A = [[S.res() for _ in range(2)] for _ in range(2)]
        rPTY_A = [[S.res() for _ in range(2)] for _ in range(2)]
        rYF_A = [S.res() for _ in range(2)]
        for i in range(4):
            S.op("pool", lambda e, i=i: e.memset(KGZ[i][:], 0.0), [], [rKGZ[i]])
            S.op("pool", lambda e, i=i: e.memset(WT[i][:], 0.0), [], [rWT[i]])
            S.op("pool", lambda e, i=i: e.memset(QGT[i][:], 0.0), [], [rQGT[i]])
        SF_A = [sb("sf%d" % d_, [128, 64]) for d_ in range(2)]
        SBs_A = [sb("sbs%d" % d_, [128, 64], BF16) for d_ in range(2)]
        rSF_A = [S.res() for _ in range(2)]
        rSBs_A = [S.res() for _ in range(2)]
        VNEW_A = [[sb("vnew%d_%d" % (d_, i), [128, 128], BF16) for i in range(2)] for d_ in range(2)]
        rVNEW_A = [[S.res() for _ in range(2)] for _ in range(2)]
        OT = [sb("ot%d" % i, [128, 128]) for i in range(2)]
        ON = [sb("on%d" % i, [128, 128], BF16) for i in range(2)]
        SSQ = [sb("ssq%d" % i, [128, 4]) for i in range(2)]
        JUNK = sb("junk", [128, 64])
        rOT = [S.res() for _ in range(2)]
        rON = [S.res() for _ in range(2)]
        rSSQ = [S.res() for _ in range(2)]
        rJ = S.res()
        PD = g.pst(es, "C_pd", [128, 512]); rPD = S.pres()
        PG = g.pst(es, "C_pg", [128, 512]); rPG = S.pres()
        PX, rPX = PD, rPD
        PA_A = [g.pst(es, "C_pa%d" % d_, [128, 512]) for d_ in range(2)]
        rPA_A = [S.pres() for _ in range(2)]
        PBk_A = [g.pst(es, "C_pbk%d" % d_, [128, 512]) for d_ in range(2)]
        rPBk_A = [S.pres() for _ in range(2)]
        CH_A = [g.pst(es, "C_ch%d" % d_, [128, 512]) for d_ in range(2)]
        rCH_A = [S.pres() for _ in range(2)]
        rdram = S.res()
        IDB = g.IDB
        rI = g.rconst

        def row(d_, h):
            return 32 * d_ + h

        def gate_prep(si, T, c0):
            ntl = T // 128
            for gb in range(T // GB):
                cg = c0 + gb * GB
                for d_ in range(2):
                    DMA(g, "sp", AR[32 * d_:32 * d_ + 8, :], W["GR"][16 + 8 * d_:24 + 8 * d_, cg:cg + GB], "C_ga", [], [rAR])
                    DMA(g, "sp", BR[32 * d_:32 * d_ + 8, :], W["GR"][8 * d_:8 + 8 * d_, cg:cg + GB], "C_gb", [], [rBR])
                R = [rG, rC]
                TS_(g, "dve", G_[:], AR[:], DTB[:, 0:1], None, ALU.add, None, [rAR] + R, [rG])
                ACT(g, G_[:], G_[:], AF.Exp, R, [rG])
                ACT(g, G_[:], G_[:], AF.Ln, R, [rG], bias=1.0)
                TS_(g, "dve", G_[:], G_[:], NEGA[:, 0:1], None, ALU.mult, None, R, [rG])
                S.op("dve", lambda e: e.tensor_tensor_scan(out=PRE[:], data0=RESET[0:40, :], data1=G_[:], initial=0.0,
                                                           op0=ALU.mult, op1=ALU.add), R, [rG])
                totb = PRE[:].rearrange("p (n c) -> p n c", c=64)[:, :, 63:64].to_broadcast([40, GB // 64, 64])
                v3 = lambda t_: t_[:].rearrange("p (n c) -> p n c", c=64)
                CP(g, "dve", GC[0:8, :], PRE[0:8, :], R, [rG])
                TT_(g, "dve", v3(TMPG)[32:40], totb[32:40], v3(PRE)[32:40], ALU.subtract, R, [rG])
                TT_(g, "dve", GC[32:40, :], TMPG[32:40, :], G_[32:40, :], ALU.add, R, [rG])
                ACT(g, RR[:, 2, :], BR[:], AF.Sigmoid, [rBR] + R, [rG])
                ACT(g, LNB[:], RR[:, 2, :], AF.Ln, R, [rG])
                TS_(g, "dve", RR[:, 0, :], GC[:], -1.0, None, ALU.mult, None, R, [rG])
                TT_(g, "dve", RR[:, 1, :], GC[:], LNB[:], ALU.add, R, [rG])
                ACT(g, TMPG[:], GC[:], AF.Exp, R, [rG])
                TT_(g, "dve", RR[:, 3, :], RR[:, 2, :], TMPG[:], ALU.mult, R, [rG])
                TT_(g, "dve", v3(TMPG), totb, v3(GC), ALU.subtract, R, [rG])
                ACT(g, RR[:, 4, :], TMPG[:], AF.Exp, R, [rG])
                ACT(g, GLF[:], PRE[:].rearrange("p (n c) -> p n c", c=64)[:, :, 63], AF.Exp, R, [rG])
                CP(g, "act", RH[:], RR[:], R, [rG])
                TT_(g, "dve", RL[:], RR[:], RH[:], ALU.subtract, R, [rG])
                cs = slice(gb * GB, (gb + 1) * GB)
                CP(g, "act", GCS[0:40, cs], GC[:], R, [rGCS])
                TT_(g, "dve", GCS[64:104, cs], GC[:], GCS[0:40, cs], ALU.subtract, R + [rGCS], [rGCS])
                CP(g, "act", HS[0:40, cs], RR[:, 1, :], R, [rHS])
                TT_(g, "dve", HS[64:104, cs], RR[:, 1, :], HS[0:40, cs], ALU.subtract, R + [rHS], [rHS])
                ns = slice(gb * (GB // 64), (gb + 1) * (GB // 64))
                CP(g, "act", GLS[0:40, ns], GLF[:], R, [rGLS])
                TT_(g, "dve", GLS[64:104, ns], GLF[:], GLS[0:40, ns], ALU.subtract, R + [rGLS], [rGLS])
                for t4 in range(GB // 128):
                    m = gb * (GB // 128) + t4
                    for k in range(5):
                        MM(g, PD[:, k * 40:(k + 1) * 40], RH[:, k, t4 * 128:(t4 + 1) * 128], IDB[0:40, 0:40], True, False, [rG, rI], [rPD])
                        MM(g, PD[:, k * 40:(k + 1) * 40], RL[:, k, t4 * 128:(t4 + 1) * 128], IDB[0:40, 0:40], False, True, [rG, rI], [rPD])
                    CP(g, "dve", SCT[:, m, :], PD[:, 0:200], [rPD], [rSCT])

        def sc(m, k, d_, h):
            c = k * 40 + row(d_, h)
            return SCT[:, m, c:c + 1]

        def qkv_prep(si, T, c0, hp):
            for cb in range(T // CB):
                cc = c0 + cb * CB
                lo = 0 if cb > 0 else 1
                hi = CB + 2 if cb < T // CB - 1 else CB + 1
                if lo == 1:
                    S.op("pool", lambda e: e.memset(XR[:, :, 0:1], 0.0), [], [rXR])
                if hi == CB + 1:
                    S.op("pool", lambda e: e.memset(XR[:, :, CB + 1:CB + 2], 0.0), [], [rXR])
                for j in range(3):
                    r0 = j * 512 + hp * 128
                    DMA(g, "sp" if j != 1 else "act", XR[:, j, lo:hi], W["QKVD"][r0:r0 + 128, cc - 1 + lo:cc - 1 + hi], "C_xr%d" % j, [], [rXR])
                cs = slice(cb * CB, (cb + 1) * CB)
                for j in range(3):
                    f = j * 4 + hp
                    a = j % 2
                    A_, rA = ACC[a], rACC[a]
                    ACT(g, A_[:], XR[:, j, 0:CB], AF.Copy, [rXR, rC], [rA], scale=CWQ[:, f, 0:1])
                    STT(g, A_[:], XR[:, j, 1:CB + 1], CWQ[:, f, 1:2], A_[:], ALU.mult, ALU.add, [rXR, rC, rA], [rA])
                    STT(g, A_[:], XR[:, j, 2:CB + 2], CWQ[:, f, 2:3], A_[:], ALU.mult, ALU.add, [rXR, rC, rA], [rA])
                    if j == 2:
                        ACT(g, VT[:, cs], A_[:], AF.Silu, [rA], [rVT])
                        continue
                    ACT(g, A_[:], A_[:], AF.Silu, [rA], [rA])
                    ACT(g, SQB[:], A_[:], AF.Square, [rA], [rSQB])
                    dst, rdst = (QT, rQT) if j == 0 else (KT, rKT)
                    for hf in range(CB // 512):
                        MM(g, PG[:], g.ONESBD[:], SQB[:, hf * 512:(hf + 1) * 512], True, True, [rSQB, rI], [rPG])
                        ACT(g, RSB[:], PG[:], AF.Sqrt, [rPG], [rRSB], bias=1e-6, scale=1.0)
                        S.op("dve", lambda e: e.reciprocal(out=RSB[:], in_=RSB[:]), [rRSB], [rRSB])
                        dsl = slice(cb * CB + hf * 512, cb * CB + (hf + 1) * 512)
                        if j == 0:
                            STT(g, dst[:, dsl], A_[:, hf * 512:(hf + 1) * 512], 0.125, RSB[:], ALU.mult, ALU.mult, [rA, rRSB], [rdst])
                        else:
                            TT_(g, "dve", dst[:, dsl], A_[:, hf * 512:(hf + 1) * 512], RSB[:], ALU.mult, [rA, rRSB], [rdst])
                            for hh in range(2):
                                b_ = slice(64 * hh, 64 * hh + 64)
                                TT_(g, "pool", KTZ[b_, hh, dsl], A_[b_, hf * 512:(hf + 1) * 512], RSB[b_, :], ALU.mult, [rA, rRSB], [rKTZ])
            DMA(g, "act", SZ[:, 0:T], W["ZT"][hp * 128:(hp + 1) * 128, c0:c0 + T], "C_z", [], [rSZ])
            ACT(g, SZ[:, 0:T], SZ[:, 0:T], AF.Silu, [rSZ], [rSZ])
            nch = T // 64
            for d_ in range(2):
                MM(g, PX[:, 0:nch], SELP[:, d_ * 4 + hp, :], GLS[:, 0:nch], True, True, [rC, rGLS], [rPX])
                CP(g, "dve", GLB[:, d_, 0:nch], PX[:, 0:nch], [rPX], [rGLB])

        def pre(m, d_, hp, pb):
            ts_ = slice(m * 128, (m + 1) * 128)
            PB_, PTY, YF, rPB_, rPTY, rYF = PB_A[d_], PTY_A[d_], YF_A[d_], rPB_A[d_], rPTY_A[d_], rYF_A[d_]
            PA, PBk, rPA, rPBk = PA_A[d_], PBk_A[d_], rPA_A[d_], rPBk_A[d_]
            S.begin_atomic()
            MM(g, PX[:, 0:128], KT[:, ts_], IDB[:], True, True, [rKT, rI], [rPX])
            MM(g, PX[:, 128:256], VT[:, ts_], IDB[:], True, True, [rVT, rI], [rPX])
            CP(g, "act", KVTM[pb][:].rearrange("p a b -> p (a b)"), PX[:, 0:256], [rPX], [rKVTM[pb]])
            if cfg.gdn_stage < 3.07:
                S.end_atomic()
                return
            for hh in range(2):
                b_ = slice(64 * hh, 64 * hh + 64)
                MM(g, PG[:, hh * 256:hh * 256 + 128], KTZ[:, hh, ts_], KT[:, ts_], True, True, [rKTZ, rKT], [rPG])
                MM(g, PG[:, hh * 256 + 128:hh * 256 + 256], KTZ[:, hh, ts_], QT[:, ts_], True, True, [rKTZ, rQT], [rPG])
            if cfg.gdn_stage < 3.15:
                S.end_atomic()
                return
            for hh in range(2):
                h = 2 * hp + hh
                sel = SELR[:, d_ * 8 + h, :]
                MM(g, PD[:, 0:128], sel, HS[:, ts_], True, False, [rC, rHS], [rPD])
                MM(g, PD[:, 128:512].rearrange("p (a b) -> p a b", a=3), sel, GCS[:, None, ts_].to_broadcast([128, 3, 128]),
                   False, False, [rC, rGCS], [rPD])
                MM(g, PD[:, 0:384], IDB[:], MASK[:, d_, :, :].rearrange("p a b -> p (a b)"), False, True, [rI, rC], [rPD])
                ACT(g, E01[pb][:, hh, :], PD[:, 0:256], AF.Exp, [rPD, rSCT], [rE01[pb]], bias=sc(m, 0, d_, h), scale=1.0)
                ACT(g, E2[pb][:, hh, :], PD[:, 256:384], AF.Exp, [rPD, rSCT], [rE2[pb]], bias=sc(m, 1, d_, h), scale=-1.0)
                b_ = slice(64 * hh, 64 * hh + 64)
                ACT(g, EGC[pb][b_, :], PD[b_, 384:512], AF.Exp, [rPD], [rEGC[pb]])
            if cfg.gdn_stage < 3.25:
                S.end_atomic()
                return
            PGv = PG[:].rearrange("p (a b) -> p a b", a=2)
            STT(g, PTY[0][:, :, 0, :], PGv[:, :, 0:128], -1.0, E01[pb][:, :, 0:128], ALU.mult, ALU.mult, [rPG, rE01[pb]], [rPTY[0]])
            TT_(g, "dve", PTY[1][:, :, 1, :], PTY[0][:, :, 0, :], IDF[:, None, :].to_broadcast([128, 2, 128]), ALU.add,
                [rPTY[0], rC], [rPTY[1]])
            STT(g, PB_[0][:], PGv[:, :, 0:128], -1.0, E2[pb][:], ALU.mult, ALU.mult, [rPG, rE2[pb]], [rPB_[0]])
            TT_(g, "dve", QKT[pb][:], PGv[:, :, 128:256], E01[pb][:, :, 128:256], ALU.mult, [rPG, rE01[pb]], [rQKT[pb]])
            S.end_atomic()
            for hh in range(2):
                b_ = slice(64 * hh, 64 * hh + 64)
                TT_(g, "pool", QGT[pb][b_, hh, :], QT[b_, ts_], EGC[pb][b_, :], ALU.mult, [rQT, rEGC[pb]], [rQGT[pb]])
            for hh in range(2):
                h = 2 * hp + hh
                b_ = slice(64 * hh, 64 * hh + 64)
                ACT(g, KGZ[pb][:, hh, b_], KVTM[pb][:, 0, b_], AF.Copy, [rKVTM[pb], rSCT], [rKGZ[pb]], scale=sc(m, 4, d_, h))
            if cfg.gdn_stage < 3.35:
                return
            PAv = PA[:].rearrange("p (a b) -> p a b", a=2)
            PBv = PBk[:, 0:256].rearrange("p (a b) -> p a b", a=2)
            for k in range(6):
                cur, nxt = k % 2, (k + 1) % 2
                for hh in range(2):
                    if k == 0:
                        MM(g, PAv[:, hh, 0:128], PB_[cur][:, hh, :], PTY[cur][:, hh, 0, :], True, True, [rPB_[cur], rPTY[cur]], [rPA])
                    elif k < 5:
                        MM(g, PAv[:, hh, :], PB_[cur][:, hh, :], PTY[cur][:, hh].rearrange("p a b -> p (a b)"), True, True,
                           [rPB_[cur], rPTY[cur]], [rPA])
                    else:
                        MM(g, PAv[:, hh, 128:256], PB_[cur][:, hh, :], PTY[cur][:, hh, 1, :], True, True, [rPB_[cur], rPTY[cur]], [rPA])
                    if k < 5:
                        MM(g, PBv[:, hh, :], PTY[cur][:, hh, 0, :], PB_[cur][:, hh, :], True, True, [rPB_[cur], rPTY[cur]], [rPBk])
                if k < 5:
                    CP(g, "act", PTY[nxt][:, :, 0, :], PAv[:, :, 0:128], [rPA], [rPTY[nxt]])
                    if k == 0:
                        pass
                    else:
                        TT_(g, "dve", PTY[nxt][:, :, 1, :], PAv[:, :, 128:256], PTY[cur][:, :, 1, :], ALU.add, [rPA, rPTY[cur]], [rPTY[nxt]])
                    CP(g, "act" if k % 2 else "dve", PB_[nxt][:], PBv, [rPBk], [rPB_[nxt]])
                else:
                    TT_(g, "dve", YF[:], PAv[:, :, 128:256], PTY[cur][:, :, 1, :], ALU.add, [rPA, rPTY[cur]], [rYF])
            if cfg.gdn_stage < 3.45:
                return
            for hh in range(2):
                h = 2 * hp + hh
                TS_(g, "dve", TB[pb][:, hh, :], YF[:, hh, :], sc(m, 2, d_, h), None, ALU.mult, None, [rYF, rSCT], [rTB[pb]])
                ACT(g, TBG[pb][:, hh, :], YF[:, hh, :], AF.Copy, [rYF, rSCT], [rTBG[pb]], scale=sc(m, 3, d_, h))
            S.begin_atomic()
            for hh in range(2):
                MM(g, PX[:, hh * 64:(hh + 1) * 64], TB[pb][:, hh, :], KVTM[pb][:, 1, hh * 64:(hh + 1) * 64], True, True,
                   [rTB[pb], rKVTM[pb]], [rPX])
                MM(g, PX[:, 128 + hh * 128:256 + hh * 128], KVTM[pb][:, 0, :], TBG[pb][:, hh, :], True, True,
                   [rTBG[pb], rKVTM[pb]], [rPX])
            CP(g, "act", U[pb][:], PX[:, 0:128], [rPX], [rU[pb]])
            for hh in range(2):
                b_ = slice(64 * hh, 64 * hh + 64)
                CP(g, "dve", WT[pb][b_, hh, :], PX[b_, 128 + hh * 128:256 + hh * 128], [rPX], [rWT[pb]])
            S.end_atomic()

        vc = [0]
        oc = [0]

        def chain(m, d_, hp, pb, T, second):
            SF, SBs, rSF, rSBs = SF_A[d_], SBs_A[d_], rSF_A[d_], rSBs_A[d_]
            VNEW, rVNEW = VNEW_A[d_], rVNEW_A[d_]
            CH, rCH = CH_A[d_], rCH_A[d_]
            PW, PO, PSt = CH[:, 0:128], CH[:, 128:256], CH[:, 256:320]
            rPW = rPO = rPSt = rCH
            order = (0, 1) if d_ == 0 else (1, 0)
            for j in order:
                n = 2 * m + j
                tj = slice(64 * j, 64 * j + 64)
                v = vc[0] % 2
                vc[0] += 1
                for hh in range(2):
                    b_ = slice(64 * hh, 64 * hh + 64)
                    MM(g, PW[:, hh * 64:(hh + 1) * 64], WT[pb][:, hh, :], SBs[:, :], True, True, [rWT[pb], rSBs], [rPW])
                TT_(g, "dve", VNEW[v][tj, :], U[pb][tj, :], PW[tj, :], ALU.subtract, [rU[pb], rPW], [rVNEW[v]])
                for hh in range(2):
                    b_ = slice(64 * hh, 64 * hh + 64)
                    MM(g, PO[:, hh * 64:(hh + 1) * 64], QGT[pb][:, hh, :], SBs[:, :], True, False, [rQGT[pb], rSBs], [rPO])
                    MM(g, PO[:, hh * 64:(hh + 1) * 64], QKT[pb][tj, hh, :], VNEW[v][tj, hh * 64:(hh + 1) * 64], False, True,
                       [rQKT[pb], rVNEW[v]], [rPO])
                if not second:
                    CP(g, "act", OF[tj, m, :], PO[tj, :], [rPO], [rOF])
                else:
                    ob = oc[0] % 2
                    TT_(g, "dve", OT[ob][tj, :], PO[tj, :], OF[tj, m, :], ALU.add, [rPO, rOF], [rOT[ob]])
                for hh in range(2):
                    MM(g, PSt[:, :], KGZ[pb][tj, hh, :], VNEW[v][tj, hh * 64:(hh + 1) * 64], hh == 0, hh == 1,
                       [rKGZ[pb], rVNEW[v]], [rPSt])
                STT(g, SF[:], SF[:], GLB[:, d_, n:n + 1], PSt[:, :], ALU.mult, ALU.add, [rSF, rGLB, rPSt], [rSF])
                CP(g, "act", SBs[:], SF[:], [rSF], [rSBs])
            if second:
                ob = oc[0] % 2
                oc[0] += 1
                for hh in range(2):
                    ACT(g, JUNK[:], OT[ob][:, hh * 64:(hh + 1) * 64], AF.Square, [rOT[ob]], [rJ, rSSQ[ob]], accum_out=SSQ[ob][:, hh:hh + 1])
                ACT(g, SSQ[ob][:, 2:4], SSQ[ob][:, 0:2], AF.Sqrt, [rSSQ[ob]], [rSSQ[ob]], scale=1.0 / 64, bias=EPS)
                S.op("dve", lambda e, ob=ob: e.reciprocal(out=SSQ[ob][:, 2:4], in_=SSQ[ob][:, 2:4]), [rSSQ[ob]], [rSSQ[ob]])
                for hh in range(2):
                    TS_(g, "dve", ON[ob][:, hh * 64:(hh + 1) * 64], OT[ob][:, hh * 64:(hh + 1) * 64], SSQ[ob][:, 2 + hh:3 + hh], None,
                        ALU.mult, None, [rOT[ob], rSSQ[ob]], [rON[ob]])
                S.begin_atomic()
                MM(g, PX[:, 384:512], ON[ob][:], IDB[:], True, True, [rON[ob], rI], [rPX])
                ts_ = slice(m * 128, (m + 1) * 128)
                STT(g, YT[:, ts_], PX[:, 384:512], NORMO[:, 0:1], SZ[:, ts_], ALU.mult, ALU.mult, [rPX, rC, rSZ], [rYT])
                S.end_atomic()

        for si, (grp, sidx, T) in enumerate(cfg.seqs):
            c0 = cfg.offs[si]
            ntl = T // 128
            gate_prep(si, T, c0)
            if cfg.gdn_stage <= 1:
                continue
            for hp in range(4):
                qkv_prep(si, T, c0, hp)
                if cfg.gdn_stage <= 2:
                    continue
                for d_ in range(2):
                    S.op("dve", lambda e, d_=d_: e.memset(SF_A[d_][:], 0.0), [], [rSF_A[d_]])
                    S.op("pool", lambda e, d_=d_: e.memset(SBs_A[d_][:], 0.0), [], [rSBs_A[d_]])
                tl = [list(range(ntl)), list(range(ntl - 1, -1, -1))]
                seen = set()

                def cap(fn, *a):
                    S.capture()
                    fn(*a)
                    return S.end_capture()

                S.extend(interleave(cap(pre, tl[0][0], 0, hp, 0), cap(pre, tl[1][0], 1, hp, 2)))
                for ix in range(ntl):
                    lists = []
                    for d_ in range(2):
                        m = tl[d_][ix]
                        if cfg.gdn_stage >= 4:
                            lists.append(cap(chain, m, d_, hp, 2 * d_ + ix % 2, T, m in seen))
                        seen.add(m)
                    if ix + 1 < ntl:
                        for d_ in range(2):
                            lists.append(cap(pre, tl[d_][ix + 1], d_, hp, 2 * d_ + (ix + 1) % 2))
                    S.extend(interleave(*lists))
                DMA(g, "sp", W["YMIX"][512 + hp * 128:512 + (hp + 1) * 128, c0:c0 + T], YT[:, 0:T], "C_y", [rYT], [rdram])
        S.barrier()


def phaseM(g, l):
    pass


def small_vec_load(g, q, dst, src, semkey, writes):
    return g.S.dma(q, lambda e: e.dma_start(out=dst, in_=src, allow_slow_non_contiguous=True), semkey, [], writes)


def phaseD(g, l):
    nc, S, cfg = g.nc, g.S, g.cfg
    NS = cfg.NS
    S.barrier()
    with ExitStack() as es:
        KMT = g.sbt(es, "D_kmt", [128, 8, NS, NMEM], BF16)
        VM = g.sbt(es, "D_vm", [128, NS, 2, D], BF16)
        rKMT, rVM = S.res(), S.res()
        WOUT = g.sbt(es, "D_wout", [128, KC, D], BF16)
        WXQ = g.sbt(es, "D_wxq", [128, KC, D], BF16)
        WXO = g.sbt(es, "D_wxo", [128, KC, D], BF16)
        rWD = S.res()
        for nm, Wt in (("w_out", WOUT), ("w_xq", WXQ), ("w_xo", WXO)):
            wsrc = g.I[nm][l].rearrange("(kc p) n -> p kc n", p=128)
            for kc in range(KC):
                DMA(g, "pool", Wt[:, kc, :], wsrc[:, kc, :], "D_w%d" % (kc % 4), [], [rWD])
        with ExitStack() as esm:
            WKV = g.sbt(esm, "M_wkv", [128, KC, 2 * D], BF16)
            rWKV = S.res()
            wsrc = g.I["w_xkv"][l].rearrange("(kc p) n -> p kc n", p=128)
            for kc in range(KC):
                DMA(g, "pool", WKV[:, kc, :], wsrc[:, kc, :], "M_w%d" % (kc % 4), [], [rWKV])
            MHF = [g.sbt(esm, "M_mhf%d" % i, [128, KC, NMEM], F32) for i in range(2)]
            MN = [g.sbt(esm, "M_mn%d" % i, [128, KC, NMEM], BF16) for i in range(2)]
            rMHF = [S.res() for _ in range(2)]
            rMN = [S.res() for _ in range(2)]
            PSm = [g.pst(esm, "M_ps%d" % i, [128, 512], F32) for i in range(4)]
            rPSm = [S.pres() for _ in range(4)]
            MHv = g.W["MH"].rearrange("(kc p) t -> p kc t", p=128)
            pc = 0
            for si in range(NS):
                b = si % 2
                DMA(g, "sp", MHF[b][:], MHv[:, :, si * NMEM:(si + 1) * NMEM], "M_m%d" % b, [], [rMHF[b]])
                for kc in range(KC):
                    TS_(g, "dve" if kc % 2 == 0 else "pool", MN[b][:, kc, :], MHF[b][:, kc, :], g.NMEMG[:, l, kc:kc + 1], None,
                        ALU.mult, None, [rMHF[b], g.rconst], [rMN[b]])
                for m in range(8):
                    p = pc % 4
                    pc += 1
                    for kc in range(KC):
                        MM(g, PSm[p][:, 0:NMEM], WKV[:, kc, m * 128:(m + 1) * 128], MN[b][:, kc, :], kc == 0, kc == KC - 1,
                           [rWKV, rMN[b]], [rPSm[p]])
                    CP(g, "act" if m % 2 == 0 else "dve", KMT[:, m, si, :], PSm[p][:, 0:NMEM], [rPSm[p]], [rKMT])
                for mt in range(2):
                    for nh in range(2):
                        p = pc % 4
                        pc += 1
                        for kc in range(KC):
                            MM(g, PSm[p][:], MN[b][:, kc, mt * 128:(mt + 1) * 128], WKV[:, kc, D + nh * 512:D + (nh + 1) * 512],
                               kc == 0, kc == KC - 1, [rWKV, rMN[b]], [rPSm[p]])
                        CP(g, "act" if nh == 0 else "dve", VM[:, si, mt, nh * 512:(nh + 1) * 512], PSm[p][:], [rPSm[p]], [rVM])
            S.barrier()
        YM = g.sbt(es, "D_ym", [128, KC, TT], BF16)
        XF_A = [g.sbt(es, "D_xf%d" % i, [128, KC, TT], F32) for i in range(2)]
        SQ = g.sbt(es, "D_sq", [128, KC, TT], BF16)
        RSTD = g.sbt(es, "D_rstd", [128, TT], F32)
        XN = g.sbt(es, "D_xn", [128, KC, TT], BF16)
        QX_A = [g.sbt(es, "D_qx%d" % i, [128, KC, TT], BF16) for i in range(2)]
        PT = [g.sbt(es, "D_pt%d" % i, [128, 2, TT], BF16) for i in range(2)]
        RCP = [g.sbt(es, "D_rcp%d" % i, [128, TT], F32) for i in range(2)]
        ATT = g.sbt(es, "D_att", [128, KC, TT], BF16)
        XN3 = g.sbt(es, "D_xn3", [128, KC, TT], BF16)
        rYM, rSQ, rRSTD, rXN, rATT, rXN3 = [S.res() for _ in range(6)]
        rXF_A = [S.res() for _ in range(2)]
        rQX_A = [S.res() for _ in range(2)]
        rPT = [S.res() for _ in range(2)]
        rRCP = [S.res() for _ in range(2)]
        PSs = g.pst(es, "D_pss", [128, TT], F32)
        rPSs = S.pres()
        PS2 = g.pst(es, "D_pssc", [128, 2 * TT], F32)
        rPS2 = S.pres()
        PSd = g.pst(es, "D_psd", [128, TT], F32)
        rPSd = S.pres()
        PS = [g.pst(es, "D_ps%d" % i, [128, TT], F32) for i in range(4)]
        rPS = [S.pres() for _ in range(4)]
        rdram = S.res()
        XTv = g.W["XT"].rearrange("(kc p) t -> p kc t", p=128)
        YMv = g.W["YMIX"].rearrange("(kc p) t -> p kc t", p=128)
        X3v = g.W["XN3"].rearrange("(kc p) t -> p kc t", p=128)
        pcs = [0]
        hcs = [0, 0]
        tiles = []
        for si, (grp, sidx, T) in enumerate(cfg.seqs):
            for t4 in range(T // TT):
                tiles.append((si, cfg.offs[si] + t4 * TT))

        def part1(ix):
            si, c0 = tiles[ix]
            XF, rXF, QX, rQX = XF_A[ix % 2], rXF_A[ix % 2], QX_A[ix % 2], rQX_A[ix % 2]
            DMA(g, "sp", YM[:], YMv[:, :, c0:c0 + TT], "D_ym", [], [rYM])
            DMA(g, "act", XF[:], XTv[:, :, c0:c0 + TT], "D_xf%d" % (ix % 2), [], [rXF])
            for m in range(KC):
                p = pcs[0] % 2
                pcs[0] += 1
                for kc in range(KC):
                    MM(g, PS[p][:], WOUT[:, kc, m * 128:(m + 1) * 128], YM[:, kc, :], kc == 0, kc == KC - 1, [rWD, rYM], [rPS[p]])
                TT_(g, "dve", XF[:, m, :], PS[p][:], XF[:, m, :], ALU.add, [rPS[p], rXF], [rXF])
            S.begin_atomic()
            rmsnorm_fm(g, XF, rXF, XN, rXN, SQ, rSQ, RSTD, rRSTD, PSs, rPSs, 3 * l + 1, TT)
            S.end_atomic()
            for m in range(KC):
                p = pcs[0] % 2
                pcs[0] += 1
                for kc in range(KC):
                    MM(g, PS[p][:], WXQ[:, kc, m * 128:(m + 1) * 128], XN[:, kc, :], kc == 0, kc == KC - 1, [rWD, rXN], [rPS[p]])
                ACT(g, QX[:, m, :], PS[p][:], AF.Copy, [rPS[p]], [rQX], scale=1.0 / 16.0)

        def part2(ix):
            si, c0 = tiles[ix]
            XF, rXF, QX, rQX = XF_A[ix % 2], rXF_A[ix % 2], QX_A[ix % 2], rQX_A[ix % 2]
            for hx in range(4):
                hb = hcs[0] % 2
                hcs[0] += 1
                for mt in range(2):
                    for c in range(2):
                        MM(g, PS2[:, mt * TT:(mt + 1) * TT], KMT[:, hx * 2 + c, si, mt * 128:(mt + 1) * 128], QX[:, hx * 2 + c, :],
                           c == 0, c == 1, [rKMT, rQX], [rPS2])
                ACT(g, PT[hb][:], PS2[:].rearrange("p (a b) -> p a b", a=2), AF.Exp, [rPS2], [rPT[hb]])
                for mt in range(2):
                    MM(g, PSd[:], g.ONESB[:], PT[hb][:, mt, :], mt == 0, mt == 1, [g.rconst, rPT[hb]], [rPSd])
                S.op("dve", lambda e, hb=hb: e.reciprocal(out=RCP[hb][:], in_=PSd[:]), [rPSd], [rRCP[hb]])
                for c in range(2):
                    p = 2 + hcs[1] % 2
                    hcs[1] += 1
                    for mt in range(2):
                        MM(g, PS[p][:], VM[:, si, mt, hx * 256 + c * 128:hx * 256 + (c + 1) * 128], PT[hb][:, mt, :],
                           mt == 0, mt == 1, [rVM, rPT[hb]], [rPS[p]])
                    TT_(g, "dve", ATT[:, hx * 2 + c, :], PS[p][:], RCP[hb][:], ALU.mult, [rPS[p], rRCP[hb]], [rATT])
            for m in range(KC):
                p = 2 + hcs[1] % 2
                hcs[1] += 1
                for kc in range(KC):
                    MM(g, PS[p][:], WXO[:, kc, m * 128:(m + 1) * 128], ATT[:, kc, :], kc == 0, kc == KC - 1, [rWD, rATT], [rPS[p]])
                TT_(g, "dve", XF[:, m, :], PS[p][:], XF[:, m, :], ALU.add, [rPS[p], rXF], [rXF])
            DMA(g, "sp", XTv[:, :, c0:c0 + TT], XF[:], "D_xs%d" % (ix % 2), [rXF], [rdram])
            S.begin_atomic()
            rmsnorm_fm(g, XF, rXF, XN3, rXN3, SQ, rSQ, RSTD, rRSTD, PSs, rPSs, 3 * l + 2, TT)
            S.end_atomic()
            DMA(g, "act", X3v[:, :, c0:c0 + TT], XN3[:], "D_x3", [rXN3], [rdram])

        def cap(fn, *a):
            S.capture()
            fn(*a)
            return S.end_capture()

        S.extend(cap(part1, 0))
        for ix in range(len(tiles)):
            lists = [cap(part2, ix)]
            if ix + 1 < len(tiles):
                lists.append(cap(part1, ix + 1))
            S.extend(interleave(*lists))
        S.barrier()


def phaseE(g, l):
    nc, S, cfg = g.nc, g.S, g.cfg
    S.barrier()
    NF = 2 * DFF // 128
    NP2 = DFF // 128
    with ExitStack() as es:
        WUP = g.sbt(es, "E_wup", [128, KC, 2 * DFF], BF16)
        WDN = g.sbt(es, "E_wdn", [128, NP2, D], BF16)
        rWU, rWDn = S.res(), S.res()
        wsrc = g.I["w_up"][l].rearrange("(kc p) n -> p kc n", p=128)
        for kc in range(KC):
            DMA(g, "pool", WUP[:, kc, :], wsrc[:, kc, :], "E_w%d" % (kc % 4), [], [rWU])
        wsrc = g.I["w_down"][l].rearrange("(k p) n -> p k n", p=128)
        for k in range(NP2):
            DMA(g, "pool", WDN[:, k, :], wsrc[:, k, :], "E_w%d" % (k % 4), [], [rWDn])
        CW = g.sbt(es, "E_cw", [128, NF, 4], F32)
        rCW = S.res()
        for k in range(3):
            small_vec_load(g, "sp", CW[:, :, k], g.I["conv_ffn"][l, k].rearrange("(f p) -> p f", p=128), "E_c", [rCW])
        small_vec_load(g, "sp", CW[:, :, 3], g.I["conv_ffn_b"][l].rearrange("(f p) -> p f", p=128), "E_c", [rCW])
        XH = [g.sbt(es, "E_xh%d" % i, [128, KC, TT + 2], BF16) for i in range(2)]
        rXH = [S.res() for _ in range(2)]
        ACTT = g.sbt(es, "E_act", [128, NP2, TT], BF16)
        rACTT = S.res()
        XF = g.sbt(es, "E_xf", [128, KC, TT], F32)
        rXF = S.res()
        T1 = [g.sbt(es, "E_t1%d" % i, [128, TT], F32) for i in range(2)]
        T2 = [g.sbt(es, "E_t2%d" % i, [128, TT], F32) for i in range(2)]
        T3 = [g.sbt(es, "E_t3%d" % i, [128, TT], F32) for i in range(2)]
        rT1 = [S.res() for _ in range(2)]
        rT2 = [S.res() for _ in range(2)]
        rT3 = [S.res() for _ in range(2)]
        PSH = g.pst(es, "E_psh", [128, 6 * 512], F32)
        rSLOT = [S.pres() for _ in range(3)]
        PSO = [g.pst(es, "E_pso%d" % i, [128, TT], F32) for i in range(2)]
        rPSO = [S.pres() for _ in range(2)]
        rdram = S.res()
        XTv = g.W["XT"].rearrange("(kc p) t -> p kc t", p=128)
        X3v = g.W["XN3"].rearrange("(kc p) t -> p kc t", p=128)
        W_ = TT + 2

        def slot_pieces(k):
            a = 1024 * k
            return [(a, a + 512, 0, 512), (a + 512, a + W_, 512, W_)]

        ti = 0
        pcount = 0
        oc = 0
        slotc = [0]
        etiles = []
        for si, (grp, sidx, T) in enumerate(cfg.seqs):
            for t4 in range(T // TT):
                etiles.append((si, t4, T // TT))

        def load_xh(ix):
            si, t4, nt4 = etiles[ix]
            c0 = cfg.offs[si] + t4 * TT
            b = ix % 2
            lo = 0 if t4 > 0 else 1
            hi = W_ if t4 < nt4 - 1 else W_ - 1
            if lo == 1:
                S.op("pool", lambda e, b=b: e.memset(XH[b][:, :, 0:1], 0.0), [], [rXH[b]])
            if hi == W_ - 1:
                S.op("pool", lambda e, b=b: e.memset(XH[b][:, :, W_ - 1:W_], 0.0), [], [rXH[b]])
            DMA(g, "sp", XH[b][:, :, lo:hi], X3v[:, :, c0 - 1 + lo:c0 - 1 + hi], "E_xh%d" % b, [], [rXH[b]])

        load_xh(0)
        for ix_, (si, t4, nt4) in enumerate(etiles):
            if True:
                c0 = cfg.offs[si] + t4 * TT
                b = ti % 2
                ti += 1
                if ix_ + 1 < len(etiles):
                    load_xh(ix_ + 1)
                DMA(g, "act", XF[:], XTv[:, :, c0:c0 + TT], "E_xf", [], [rXF])
                for i in range(NP2):
                    pr = pcount % 2
                    pcount += 1
                    for which, f in ((0, i), (1, NP2 + i)):
                        k = slotc[0] % 3
                        slotc[0] += 1
                        for kc in range(KC):
                            for (pa, pb, ra, rb) in slot_pieces(k):
                                MM(g, PSH[:, pa:pb], WUP[:, kc, f * 128:(f + 1) * 128], XH[b][:, kc, ra:rb], kc == 0, kc == KC - 1,
                                   [rWU, rXH[b]], [rSLOT[k]])
                        a0 = 1024 * k
                        Tt, rTt = (T1[pr], rT1[pr]) if which == 0 else (T2[pr], rT2[pr])
                        ACT(g, Tt[:], PSH[:, a0 + 1:a0 + 1 + TT], AF.Identity, [rSLOT[k], rCW], [rTt],
                            scale=CW[:, f, 1:2], bias=CW[:, f, 3:4])
                        STT(g, Tt[:], PSH[:, a0:a0 + TT], CW[:, f, 0:1], Tt[:], ALU.mult, ALU.add, [rSLOT[k], rCW, rTt], [rTt])
                        STT(g, Tt[:], PSH[:, a0 + 2:a0 + 2 + TT], CW[:, f, 2:3], Tt[:], ALU.mult, ALU.add, [rSLOT[k], rCW, rTt], [rTt])
                    ACT(g, T3[pr][:], T2[pr][:], AF.Silu, [rT2[pr]], [rT3[pr]])
                    TT_(g, "pool", ACTT[:, i, :], T3[pr][:], T1[pr][:], ALU.mult, [rT3[pr], rT1[pr]], [rACTT])
                for m in range(KC):
                    p = oc % 2
                    oc += 1
                    for k in range(NP2):
                        MM(g, PSO[p][:], WDN[:, k, m * 128:(m + 1) * 128], ACTT[:, k, :], k == 0, k == NP2 - 1, [rWDn, rACTT], [rPSO[p]])
                    TT_(g, "dve", XF[:, m, :], PSO[p][:], XF[:, m, :], ALU.add, [rPSO[p], rXF], [rXF])
                DMA(g, "sp", XTv[:, :, c0:c0 + TT], XF[:], "E_xs", [rXF], [rdram])
        S.barrier()


def phaseF(g):
    nc, S, cfg = g.nc, g.S, g.cfg
    L = cfg.depth
    S.barrier()
    with ExitStack() as es:
        XF = [g.sbt(es, "F_xf%d" % i, [128, KC, TT], F32) for i in range(2)]
        rXF = [S.res() for _ in range(2)]
        SQ = g.sbt(es, "F_sq", [128, KC, TT], BF16)
        RSTD = g.sbt(es, "F_rstd", [128, TT], F32)
        R1 = g.sbt(es, "F_r1", [128, KC, TT], F32)
        HI = g.sbt(es, "F_hi", [128, 3, KC, TT], BF16)
        OUT = [g.sbt(es, "F_out%d" % i, [128, 4, D], F32) for i in range(2)]
        rSQ, rRSTD, rR1, rHI = [S.res() for _ in range(4)]
        rOUT = [S.res() for _ in range(2)]
        PSs = g.pst(es, "F_pss", [128, TT], F32)
        rPSs = S.pres()
        PS = [g.pst(es, "F_ps%d" % i, [128, TT], F32) for i in range(4)]
        rPS = [S.pres() for _ in range(4)]
        rdram = S.res()
        XTv = g.W["XT"].rearrange("(kc p) t -> p kc t", p=128)
        ti = 0
        pc = 0
        for si, (grp, sidx, T) in enumerate(cfg.seqs):
            ydst = y_dst(g, si)
            for t4 in range(T // TT):
                c0 = cfg.offs[si] + t4 * TT
                b = ti % 2
                ti += 1
                DMA(g, "sp", XF[b][:], XTv[:, :, c0:c0 + TT], "F_x%d" % b, [], [rXF[b]])
                ACT(g, SQ[:], XF[b][:], AF.Square, [rXF[b]], [rSQ])
                for kc in range(KC):
                    MM(g, PSs[:], g.ONESB[:], SQ[:, kc, :], kc == 0, kc == KC - 1, [rSQ, g.rconst], [rPSs])
                ACT(g, RSTD[:], PSs[:], AF.Sqrt, [rPSs], [rRSTD], scale=1.0 / D, bias=EPS)
                S.op("dve", lambda e: e.reciprocal(out=RSTD[:], in_=RSTD[:]), [rRSTD], [rRSTD])
                for kc in range(KC):
                    STT(g, XF[b][:, kc, :], XF[b][:, kc, :], g.NORMS[:, 3 * L, kc:kc + 1], RSTD[:], ALU.mult, ALU.mult,
                        [rXF[b], rRSTD, g.rconst], [rXF[b]])
                CP(g, "act", HI[:, 0], XF[b][:], [rXF[b]], [rHI])
                TT_(g, "dve", R1[:], XF[b][:], HI[:, 0], ALU.subtract, [rXF[b], rHI], [rR1])
                CP(g, "act", HI[:, 1], R1[:], [rR1], [rHI])
                TT_(g, "pool", R1[:], R1[:], HI[:, 1], ALU.subtract, [rR1, rHI], [rR1])
                CP(g, "act", HI[:, 2], R1[:], [rR1], [rHI])
                for ts in range(4):
                    for half in range(2):
                        p = pc % 4
                        pc += 1
                        for q in range(4):
                            kc = half * 4 + q
                            for part in range(3):
                                MM(g, PS[p][:, q * 128:(q + 1) * 128], HI[:, part, kc, ts * 128:(ts + 1) * 128], g.IDB[:],
                                   part == 0, part == 2, [rHI, g.rconst], [rPS[p]])
                        CP(g, "act" if half == 0 else "dve", OUT[b][:, ts, half * 512:(half + 1) * 512], PS[p][:], [rPS[p]], [rOUT[b]])
                t0 = t4 * TT
                DMA(g, "act", ydst[t0:t0 + TT, :].rearrange("(s p) f -> p s f", p=128), OUT[b][:], "F_o%d" % b, [rOUT[b]], [rdram])
        S.barrier()


N_CORES = 8
_FULL = dict(n_p=4, T_p=2048, n_s=2, T_s=4096, depth=4)
SKIP_GDN = False


def _device_inputs(inputs, L):
    m = {}
    f = lambda a: np.ascontiguousarray(np.asarray(a, dtype=np.float32))
    for k in ("norm_mix", "w_in", "conv_qkv", "norm_o", "w_out", "norm_x", "norm_mem", "w_xq", "w_xkv", "w_xo",
              "norm_ffn", "w_up", "conv_ffn", "conv_ffn_b", "w_down", "norm_final"):
        m[k] = f(inputs[k])
    m["a_log"] = f(inputs["a_log"]).reshape(L, 16)
    m["dt_bias"] = f(inputs["dt_bias"]).reshape(L, 16)
    m["na_bias"] = _na_bias_table(f(inputs["rpb"])).reshape(L, 128, 8 * 9 * 128)
    m.update(_consts())
    return m


def kernel(**inputs):
    cfg = Cfg(_FULL["n_p"], _FULL["T_p"], _FULL["n_s"], _FULL["T_s"], _FULL["depth"], skip_gdn=SKIP_GDN)
    nc, g = build_program(cfg)
    shared = _device_inputs(inputs, cfg.depth)
    xp = np.asarray(inputs["x_prompt"], dtype=np.float32)
    xs = np.asarray(inputs["x_sample"], dtype=np.float32)
    mp = np.asarray(inputs["mem_prompt"], dtype=np.float32)
    ms = np.asarray(inputs["mem_sample"], dtype=np.float32)
    in_maps = []
    for c in range(N_CORES):
        m = dict(shared)
        m["x_p"] = np.ascontiguousarray(xp[cfg.n_p * c:cfg.n_p * (c + 1)])
        m["x_s"] = np.ascontiguousarray(xs[cfg.n_s * c:cfg.n_s * (c + 1)])
        m["m_p"] = np.ascontiguousarray(mp[cfg.n_p * c:cfg.n_p * (c + 1)])
        m["m_s"] = np.ascontiguousarray(ms[cfg.n_s * c:cfg.n_s * (c + 1)])
        in_maps.append(m)
    res = run_bass_kernel_spmd(nc, in_maps, core_ids=list(range(N_CORES)))
    y_p = np.concatenate([np.asarray(r["y_p"], dtype=np.float32) for r in res.results], axis=0)
    y_s = np.concatenate([np.asarray(r["y_s"], dtype=np.float32) for r in res.results], axis=0)
    return (y_p, y_s)
```

```python
import numpy as np
from contextlib import ExitStack

import concourse.bass as bass
import concourse.mybir as mybir
from concourse.bass_utils import run_bass_kernel_spmd

F32 = mybir.dt.float32
BF16 = mybir.dt.bfloat16
AF = mybir.ActivationFunctionType
ALU = mybir.AluOpType

D = 1024
KC = 8
DIN = 3616
DFF = 2816
NMEM = 256
NEG = -30000.0
EPS = 1e-6
TT = 512

SAME_ENGINE_SYNC = True
ATTACH_WAIT = True


class Res:
    __slots__ = ("name", "last_w", "readers", "excl")

    def __init__(self, name="", last_w=None, excl=False):
        self.name = name
        self.last_w = last_w
        self.readers = []
        self.excl = excl


class Op:
    __slots__ = ("eng", "fn", "reads", "writes", "dma", "semkey", "val", "needs_inc", "idx", "barrier", "mm_stop", "atom")


class Sched:
    def __init__(self, nc):
        self.nc = nc
        self.ops = []
        self.eng_obj = {"pe": nc.tensor, "act": nc.scalar, "dve": nc.vector, "pool": nc.gpsimd, "sp": nc.sync}
        self.last_barrier = None
        self.all_res = []
        self.cap = None
        self.atom = None
        self.atom_ctr = 0

    def begin_atomic(self):
        self.atom_ctr += 1
        self.atom = self.atom_ctr

    def end_atomic(self):
        self.atom = None

    def res(self, name=""):
        r = Res(name, self.last_barrier)
        self.all_res.append(r)
        return r

    def pres(self, name=""):
        r = Res(name, self.last_barrier, excl=True)
        self.all_res.append(r)
        return r

    def _new(self, eng, fn, reads, writes):
        o = Op()
        ex = [r for r in reads if r.excl]
        if ex:
            reads = [r for r in reads if not r.excl]
            writes = list(writes) + [r for r in ex if r not in writes]
        o.eng, o.fn, o.reads, o.writes = eng, fn, tuple(reads), tuple(writes)
        o.dma, o.semkey, o.val, o.needs_inc, o.barrier = False, None, None, False, False
        o.mm_stop = None
        o.atom = self.atom
        if self.cap is not None:
            self.cap.append(o)
        else:
            self.ops.append(o)
        return o

    def op(self, eng, fn, reads=(), writes=()):
        return self._new(eng, fn, reads, writes)

    def dma(self, queue, fn, semkey, reads=(), writes=()):
        o = self._new(queue, fn, reads, writes)
        o.dma, o.semkey = True, semkey
        return o

    def capture(self):
        assert self.cap is None
        self.cap = []

    def end_capture(self):
        c, self.cap = self.cap, None
        return c

    def extend(self, ops):
        assert self.cap is None
        self.ops.extend(ops)

    def barrier(self):
        o = self._new("sp", lambda e: e.nop(), [], [])
        o.barrier = True
        return o

    def emit(self):
        nc = self.nc
        ops = self.ops
        n = len(ops)
        for i, o in enumerate(ops):
            o.idx = i
        deps_all = [None] * n
        last_dma_on_sem = {}
        last_on_eng = {}
        outstanding_dma = set()
        cur_barrier = None
        for o in ops:
            deps = set()
            if o.barrier:
                for e, d in last_on_eng.items():
                    deps.add(d)
                deps.update(outstanding_dma)
                outstanding_dma = set()
                cur_barrier = o.idx
            else:
                for r in o.reads:
                    lw = r.last_w
                    if lw is not None:
                        deps.add(lw.idx if isinstance(lw, Op) else lw)
                for w in o.writes:
                    lw = w.last_w
                    if lw is not None:
                        deps.add(lw.idx if isinstance(lw, Op) else lw)
                    deps.update(w.readers)
                if o.dma:
                    p = last_dma_on_sem.get(o.semkey)
                    if p is not None:
                        deps.add(p)
                    last_dma_on_sem[o.semkey] = o.idx
                    outstanding_dma.add(o.idx)
                if cur_barrier is not None:
                    deps.add(cur_barrier)
            deps.discard(o.idx)
            deps_all[o.idx] = deps
            for r in o.reads:
                r.readers.append(o.idx)
            for w in o.writes:
                w.last_w = o.idx
                w.readers = []
            last_on_eng[o.eng] = o.idx
        known = {e: {} for e in self.eng_obj}
        waits = [None] * n
        for o in ops:
            best = {}
            dmadeps = []
            for d in deps_all[o.idx]:
                p = ops[d]
                if p.dma:
                    dmadeps.append(d)
                else:
                    if p.eng == o.eng and not o.dma:
                        if p.eng in ("pe", "sp") or not SAME_ENGINE_SYNC:
                            continue
                    if p.eng not in best or best[p.eng] < d:
                        best[p.eng] = d
            w = []
            for e, d in best.items():
                kn = known[o.eng].get(("c", e), -1)
                if d > kn:
                    known[o.eng][("c", e)] = d
                    w.append(d)
                    ops[d].needs_inc = True
            for d in dmadeps:
                key = ("d", ops[d].semkey)
                kn = known[o.eng].get(key, -1)
                if d > kn:
                    known[o.eng][key] = d
                    w.append(d)
            waits[o.idx] = w
        sems = {}
        self._stack = []

        def get_sem(key):
            if key not in sems:
                cm = nc.semaphore("s%d" % len(sems))
                s = cm.__enter__()
                self._stack.append(cm)
                sems[key] = [s, 0]
            return sems[key]

        self.n_inst = {e: 0 for e in self.eng_obj}
        for o in ops:
            eng = self.eng_obj[o.eng]
            wl = waits[o.idx]
            attach = None
            if ATTACH_WAIT and wl and not o.barrier:
                attach = wl[-1]
                wl = wl[:-1]
            for d in wl:
                p = ops[d]
                key = ("dma", p.semkey) if p.dma else ("eng", p.eng)
                s = get_sem(key)[0]
                assert p.val is not None
                eng.wait_ge(s, p.val)
                self.n_inst[o.eng] += 1
            inst = o.fn(eng)
            if attach is not None:
                p = ops[attach]
                key = ("dma", p.semkey) if p.dma else ("eng", p.eng)
                inst.wait_op(get_sem(key)[0], p.val, "sem-ge")
            self.n_inst[o.eng] += 1
            if o.dma:
                sv = get_sem(("dma", o.semkey))
                sv[1] += 16
                inst.then_inc(sv[0], 16)
                o.val = sv[1]
            elif o.needs_inc:
                sv = get_sem(("eng", o.eng))
                sv[1] += 1
                inst.then_inc(sv[0], 1)
                o.val = sv[1]
        feng = self.eng_obj["sp"]
        for key, (s, v) in sems.items():
            if key[0] == "dma" and v > 0:
                feng.wait_ge(s, v)
        self.n_sems = len(sems)

    def close(self):
        for cm in reversed(self._stack):
            cm.__exit__(None, None, None)


def interleave(*lists):
    out = []
    its = [list(l) for l in lists]
    pos = [0] * len(its)
    tot = sum(len(l) for l in its)
    while len(out) < tot:
        best, bi = None, -1
        for i, l in enumerate(its):
            if pos[i] < len(l):
                frac = pos[i] / len(l)
                if best is None or frac < best:
                    best, bi = frac, i
        out.append(its[bi][pos[bi]])
        pos[bi] += 1
        if out[-1].atom is not None:
            a = out[-1].atom
            while pos[bi] < len(its[bi]) and its[bi][pos[bi]].atom == a:
                out.append(its[bi][pos[bi]])
                pos[bi] += 1
        elif out[-1].mm_stop is False:
            while pos[bi] < len(its[bi]):
                o = its[bi][pos[bi]]
                out.append(o)
                pos[bi] += 1
                if o.mm_stop is True:
                    break
    return out


def _na_pairs(T):
    rows = T // 64
    kh = 8
    npair = rows // 2
    out = []
    for i in range(npair):
        lst = []
        for j in range(npair):
            V = np.zeros((2, 2), bool)
            for b in range(2):
                r = 2 * i + b
                rs = min(max(r - kh // 2, 0), rows - kh)
                for a in range(2):
                    kr = 2 * j + a
                    V[a, b] = (rs <= kr < rs + kh)
            if not V.any():
                continue
            delta = j - i
            if V.all():
                assert -3 <= delta <= 3
                cls = delta + 3
            elif delta == -2 and V[0, 0] and V[1, 0] and V[1, 1] and not V[0, 1]:
                cls = 7
            elif delta == 2 and V[0, 1] and not V[0, 0] and not V[1, 0] and not V[1, 1]:
                cls = 8
            else:
                raise AssertionError("unexpected NA window pattern %s %d %d" % (V, i, j))
            lst.append((j, cls))
        out.append(lst)
    return out


def _na_bias_table(rpb):
    L = rpb.shape[0]
    a = np.arange(2)[:, None, None, None]
    kc = np.arange(64)[None, :, None, None]
    b = np.arange(2)[None, None, :, None]
    qc = np.arange(64)[None, None, None, :]
    cs = np.clip(qc - 8, 0, 48)
    colok = (kc >= cs) & (kc < cs + 16)
    coff = np.clip(kc - qc + 15, 0, 30)
    tab = np.full((L, 8, 9, 2, 64, 2, 64), NEG, np.float32)
    for cls in range(9):
        if cls < 7:
            delta = cls - 3
            rowok = np.ones((2, 1, 2, 1), bool)
        elif cls == 7:
            delta = -2
            rowok = ~((a == 0) & (b == 1))
        else:
            delta = 2
            rowok = (a == 0) & (b == 1)
        dr = 2 * delta + a - b + 7
        ok = np.broadcast_to(colok & rowok & (dr >= 0) & (dr <= 14), (2, 64, 2, 64))
        drc = np.broadcast_to(np.clip(dr, 0, 14), (2, 64, 2, 64))
        cof = np.broadcast_to(coff, (2, 64, 2, 64))
        g = rpb[:, :, drc, cof]
        tab[:, :, cls] = np.where(ok[None, None], g, np.float32(NEG))
    tab = tab.reshape(L, 8, 9, 128, 128)
    return np.ascontiguousarray(tab.transpose(0, 3, 1, 2, 4))


def _gdn_masks():
    p = np.arange(128)
    same = (p[:, None] // 64) == (p[None, :] // 64)
    r, c = p[:, None], p[None, :]
    m = np.zeros((128, 2, 3, 128), np.float32)
    m[:, 0, 0] = np.where(same & (r < c), 0.0, NEG)
    m[:, 0, 1] = np.where(same & (r <= c), 0.0, NEG)
    m[:, 0, 2] = np.where(same & (c < r), 0.0, -NEG)
    m[:, 1, 0] = np.where(same & (r > c), 0.0, NEG)
    m[:, 1, 1] = np.where(same & (r >= c), 0.0, NEG)
    m[:, 1, 2] = np.where(same & (c > r), 0.0, -NEG)
    return m


def _consts():
    c = {}
    c["c_ident"] = np.eye(128, dtype=np.float32)
    p = np.arange(128)
    c["c_onesbd"] = ((p[:, None] // 64) == (p[None, :] // 64)).astype(np.float32)
    c["c_gmask"] = _gdn_masks().reshape(128, 6 * 128)
    sel = np.zeros((48, 16, 128), np.float32)
    for i in range(16):
        sel[i, i, :] = 1.0
        sel[32 + i, i, :] = 1.0
    c["c_sel"] = sel.reshape(48, 16 * 128)
    selr = np.zeros((128, 16, 128), np.float32)
    for d_ in range(2):
        for h in range(8):
            r = 32 * d_ + h
            selr[r, d_ * 8 + h, :] = 1.0
            selr[64 + r, d_ * 8 + h, :] = 1.0
    c["c_selr"] = selr.reshape(128, 16 * 128)
    selp = np.zeros((128, 8, 128), np.float32)
    for d_ in range(2):
        for hp in range(4):
            for hh in range(2):
                r = 32 * d_ + 2 * hp + hh
                selp[r, d_ * 4 + hp, 64 * hh:64 * hh + 64] = 1.0
                selp[64 + r, d_ * 4 + hp, 64 * hh:64 * hh + 64] = 1.0
    c["c_selp"] = selp.reshape(128, 8 * 128)
    rst = np.ones((128, 512), np.float32)
    rst[:, ::64] = 0.0
    c["c_reset"] = rst
    return c


class Cfg:
    def __init__(self, n_p, T_p, n_s, T_s, depth, taps=(), stop_after=None, skip_gdn=False, gdn_stage=9):
        self.gdn_stage = gdn_stage
        self.n_p, self.T_p, self.n_s, self.T_s, self.depth = n_p, T_p, n_s, T_s, depth
        self.seqs = [("p", i, T_p) for i in range(n_p)] + [("s", i, T_s) for i in range(n_s)]
        self.offs = []
        o = 0
        for (_, _, T) in self.seqs:
            self.offs.append(o)
            o += T
        self.NT = o
        self.NS = len(self.seqs)
        self.taps = tuple(taps)
        self.stop_after = stop_after
        self.skip_gdn = skip_gdn


class B:
    pass


def build_program(cfg):
    nc = bass.Bass("TRN2", target_bir_lowering=False)
    L = cfg.depth
    NT, NS = cfg.NT, cfg.NS
    g = B()
    g.nc, g.cfg = nc, cfg

    def din(name, shape, dt=F32):
        return nc.dram_tensor(name, list(shape), dt, kind="ExternalInput").ap()

    def dscr(name, shape, dt):
        kind = "ExternalOutput" if name in cfg.taps else "Internal"
        return nc.dram_tensor(name, list(shape), dt, kind=kind).ap()

    I = {}
    I["x_p"] = din("x_p", [max(cfg.n_p, 1), cfg.T_p, D])
    I["x_s"] = din("x_s", [max(cfg.n_s, 1), cfg.T_s, D])
    I["m_p"] = din("m_p", [max(cfg.n_p, 1), NMEM, D])
    I["m_s"] = din("m_s", [max(cfg.n_s, 1), NMEM, D])
    for nm, shp in [("norm_mix", [L, D]), ("w_in", [L, D, DIN]), ("conv_qkv", [L, 3, 1536]), ("a_log", [L, 16]),
                    ("dt_bias", [L, 16]), ("norm_o", [L, 64]), ("w_out", [L, D, D]), ("norm_x", [L, D]),
                    ("norm_mem", [L, D]), ("w_xq", [L, D, D]), ("w_xkv", [L, D, 2 * D]), ("w_xo", [L, D, D]),
                    ("norm_ffn", [L, D]), ("w_up", [L, D, 2 * DFF]), ("conv_ffn", [L, 3, 2 * DFF]),
                    ("conv_ffn_b", [L, 2 * DFF]), ("w_down", [L, DFF, D]), ("norm_final", [D]),
                    ("na_bias", [L, 128, 8 * 9 * 128]), ("c_ident", [128, 128]), ("c_onesbd", [128, 128]),
                    ("c_gmask", [128, 768]), ("c_sel", [48, 2048]), ("c_selr", [128, 2048]),
                    ("c_selp", [128, 1024]), ("c_reset", [128, 512])]:
        I[nm] = din(nm, shp)
    O = {}
    O["y_p"] = nc.dram_tensor("y_p", [max(cfg.n_p, 1), cfg.T_p, D], F32, kind="ExternalOutput").ap()
    O["y_s"] = nc.dram_tensor("y_s", [max(cfg.n_s, 1), cfg.T_s, D], F32, kind="ExternalOutput").ap()
    W = {}
    W["XT"] = dscr("XT", [D, NT], F32)
    W["MH"] = dscr("MH", [D, NS * NMEM], F32)
    W["QKN"] = dscr("QKN", [1024, NT], BF16)
    W["VN"] = dscr("VN", [NT, 512], BF16)
    W["QKVD"] = dscr("QKVD", [1536, NT], BF16)
    W["ZT"] = dscr("ZT", [512, NT], BF16)
    W["GR"] = dscr("GR", [32, NT], F32)
    W["YMIX"] = dscr("YMIX", [1024, NT], BF16)
    W["XN3"] = dscr("XN3", [1024, NT], BF16)
    g.I, g.O, g.W = I, O, W

    S = Sched(nc)
    g.S = S
    with ExitStack() as top:
        uniq = [0]

        def sbt(es, name, shape, dt=F32):
            uniq[0] += 1
            return es.enter_context(nc.sbuf_tensor("%s_%d" % (name, uniq[0]), list(shape), dt))

        def pst(es, name, shape, dt=F32):
            uniq[0] += 1
            return es.enter_context(nc.psum_tensor("%s_%d" % (name, uniq[0]), list(shape), dt))

        g.sbt, g.pst = sbt, pst
        g.IDB = sbt(top, "IDB", [128, 128], BF16)
        g.ONESB = sbt(top, "ONESB", [128, 128], BF16)
        g.ONESBD = sbt(top, "ONESBD", [128, 128], BF16)
        g.NORMS = sbt(top, "NORMS", [128, 3 * L + 1, KC], F32)
        g.NMEMG = sbt(top, "NMEMG", [128, L, KC], F32)
        g.rconst = S.res("const")
        S.dma("pool", lambda e: e.dma_start(out=g.IDB[:], in_=I["c_ident"]), "c0", writes=[g.rconst])
        S.dma("pool", lambda e: e.dma_start(out=g.ONESBD[:], in_=I["c_onesbd"]), "c1", writes=[g.rconst])
        S.op("dve", lambda e: e.memset(g.ONESB[:], 1.0), writes=[g.rconst])
        with nc.allow_non_contiguous_dma("tiny per-partition gain vectors"):
            k = 0
            for l in range(L):
                for j, nm in enumerate(("norm_mix", "norm_x", "norm_ffn")):
                    src = I[nm][l].rearrange("(kc p) -> p kc", p=128)
                    S.dma("sp", lambda e, src=src, idx=3 * l + j: e.dma_start(out=g.NORMS[:, idx, :], in_=src, allow_slow_non_contiguous=True),
                          "c2", writes=[g.rconst])
                src = I["norm_mem"][l].rearrange("(kc p) -> p kc", p=128)
                S.dma("sp", lambda e, src=src, l=l: e.dma_start(out=g.NMEMG[:, l, :], in_=src, allow_slow_non_contiguous=True), "c2", writes=[g.rconst])
            src = I["norm_final"].rearrange("(kc p) -> p kc", p=128)
            S.dma("sp", lambda e, src=src: e.dma_start(out=g.NORMS[:, 3 * L, :], in_=src, allow_slow_non_contiguous=True), "c2", writes=[g.rconst])

        phase0(g)
        if cfg.stop_after != "0":
            for l in range(L):
                phaseA(g, l)
                if cfg.stop_after == "A":
                    break
                phaseB(g, l)
                if cfg.stop_after == "B":
                    break
                phaseC(g, l)
                if cfg.stop_after == "C":
                    break
                phaseM(g, l)
                phaseD(g, l)
                if cfg.stop_after == "D":
                    break
                phaseE(g, l)
            if cfg.stop_after is None:
                phaseF(g)
        S.barrier()
        S.emit()
        S.close()
    g.n_inst = S.n_inst
    return nc, g


def MM(g, out, lhsT, rhs, start, stop, reads, writes):
    o = g.S.op("pe", lambda e: e.matmul(out, lhsT, rhs, start=start, stop=stop), reads, writes)
    o.mm_stop = bool(stop)
    return o


def ACT(g, out, in_, func, reads, writes, **kw):
    return g.S.op("act", lambda e: e.activation(out=out, in_=in_, func=func, **kw), reads, writes)


def TT_(g, eng, out, in0, in1, op, reads, writes):
    return g.S.op(eng, lambda e: e.tensor_tensor(out=out, in0=in0, in1=in1, op=op), reads, writes)


def TS_(g, eng, out, in0, s1, s2, op0, op1, reads, writes, **kw):
    if op1 is None and eng == "pool" and op0 == ALU.mult:
        op1, s2 = ALU.add, 0.0
    if op1 is None:
        return g.S.op(eng, lambda e: e.tensor_scalar(out=out, in0=in0, scalar1=s1, scalar2=None, op0=op0, **kw), reads, writes)
    return g.S.op(eng, lambda e: e.tensor_scalar(out=out, in0=in0, scalar1=s1, scalar2=s2, op0=op0, op1=op1, **kw), reads, writes)


def STT(g, out, in0, scalar, in1, op0, op1, reads, writes):
    return g.S.op("dve", lambda e: e.scalar_tensor_tensor(out=out, in0=in0, scalar=scalar, in1=in1, op0=op0, op1=op1), reads, writes)


def CP(g, eng, out, in_, reads, writes):
    if eng == "act":
        return g.S.op("act", lambda e: e.activation(out=out, in_=in_, func=AF.Copy), reads, writes)
    return g.S.op(eng, lambda e: e.tensor_copy(out=out, in_=in_), reads, writes)


def DMA(g, q, out, in_, semkey, reads, writes):
    return g.S.dma(q, lambda e: e.dma_start(out=out, in_=in_), semkey, reads, writes)


def x_src(g, si):
    grp, i, T = g.cfg.seqs[si]
    return (g.I["x_p"] if grp == "p" else g.I["x_s"])[i]


def y_dst(g, si):
    grp, i, T = g.cfg.seqs[si]
    return (g.O["y_p"] if grp == "p" else g.O["y_s"])[i]


def m_src(g, si):
    grp, i, T = g.cfg.seqs[si]
    return (g.I["m_p"] if grp == "p" else g.I["m_s"])[i]


def phase0(g):
    nc, S, cfg = g.nc, g.S, g.cfg
    S.barrier()
    with ExitStack() as es:
        NB = 2
        XL = [g.sbt(es, "p0_xl%d" % i, [128, D], F32) for i in range(NB)]
        HI = [g.sbt(es, "p0_hi%d" % i, [128, 3, D], BF16) for i in range(NB)]
        R1 = [g.sbt(es, "p0_r%d" % i, [128, D], F32) for i in range(NB)]
        ST = [g.sbt(es, "p0_st%d" % i, [128, KC, 512], F32) for i in range(2)]
        SS = g.sbt(es, "p0_ss", [128, 4], F32)
        PS = [g.pst(es, "p0_ps%d" % i, [128, 512], F32) for i in range(4)]
        rXL = [S.res() for _ in range(NB)]
        rHI = [S.res() for _ in range(NB)]
        rR1 = [S.res() for _ in range(NB)]
        rST = [S.res() for _ in range(2)]
        rSS = S.res()
        rPS = [S.pres() for _ in range(4)]
        rdram = S.res()
        cnt = [0]

        def do_tile(src_rows, dst, dst_col0, slot_in_group, stg, is_mem):
            b = cnt[0] % NB
            cnt[0] += 1
            DMA(g, "sp", XL[b][:], src_rows, "p0l%d" % b, [], [rXL[b]])
            xin = XL[b]
            if is_mem:
                ACT(g, R1[b][:], XL[b][:], AF.Square, [rXL[b]], [rR1[b], rSS], accum_out=SS[:, 0:1])
                ACT(g, SS[:, 1:2], SS[:, 0:1], AF.Sqrt, [rSS], [rSS], scale=1.0 / D, bias=EPS)
                S.op("dve", lambda e: e.reciprocal(out=SS[:, 2:3], in_=SS[:, 1:2]), [rSS], [rSS])
                TS_(g, "dve", XL[b][:], XL[b][:], SS[:, 2:3], None, ALU.mult, None, [rXL[b], rSS], [rXL[b]])
            CP(g, "act", HI[b][:, 0, :], xin[:], [rXL[b]], [rHI[b]])
            TT_(g, "dve", R1[b][:], xin[:], HI[b][:, 0, :], ALU.subtract, [rXL[b], rHI[b]], [rR1[b]])
            CP(g, "act", HI[b][:, 1, :], R1[b][:], [rR1[b]], [rHI[b]])
            TT_(g, "dve", R1[b][:], R1[b][:], HI[b][:, 1, :], ALU.subtract, [rR1[b], rHI[b]], [rR1[b]])
            CP(g, "act", HI[b][:, 2, :], R1[b][:], [rR1[b]], [rHI[b]])
            for half in range(2):
                pi = (2 * cnt[0] + half) % 4
                for q in range(4):
                    fc = half * 4 + q
                    for part in range(3):
                        MM(g, PS[pi][:, q * 128:(q + 1) * 128], HI[b][:, part, fc * 128:(fc + 1) * 128], g.IDB[:],
                           part == 0, part == 2, [rHI[b], g.rconst], [rPS[pi]])
                outv = ST[stg][:, half * 4:(half + 1) * 4, slot_in_group * 128:(slot_in_group + 1) * 128]
                inv = PS[pi][:].rearrange("p (q t) -> p q t", q=4)
                if half == 0:
                    CP(g, "act", outv, inv, [rPS[pi]], [rST[stg]])
                else:
                    CP(g, "dve", outv, inv, [rPS[pi]], [rST[stg]])

        grp_i = 0
        for si, (grp, i, T) in enumerate(cfg.seqs):
            xs = x_src(g, si)
            for t4 in range(T // 512):
                stg = grp_i % 2
                grp_i += 1
                for s in range(4):
                    r0 = t4 * 512 + s * 128
                    do_tile(xs[r0:r0 + 128, :], None, None, s, stg, False)
                col0 = cfg.offs[si] + t4 * 512
                dst = g.W["XT"].rearrange("(kc p) t -> p kc t", p=128)[:, :, col0:col0 + 512]
                DMA(g, "act", dst, ST[stg][:], "p0s%d" % stg, [rST[stg]], [rdram])
        for si in range(cfg.NS):
            ms = m_src(g, si)
            stg = grp_i % 2
            grp_i += 1
            for s in range(2):
                do_tile(ms[s * 128:(s + 1) * 128, :], None, None, s, stg, True)
            col0 = si * NMEM
            dst = g.W["MH"].rearrange("(kc p) t -> p kc t", p=128)[:, :, col0:col0 + 256]
            DMA(g, "act", dst, ST[stg][:, :, 0:256], "p0s%d" % stg, [rST[stg]], [rdram])
        S.barrier()


def rmsnorm_fm(g, XF, rXF, XN, rXN, SQ, rSQ, RSTD, rRSTD, PSs, rPSs, gain_idx, ncols, col0=0, engs=("dve", "pool")):
    ACT(g, SQ[:, :, 0:ncols], XF[:, :, 0:ncols], AF.Square, [rXF], [rSQ])
    for kc in range(KC):
        MM(g, PSs[:, 0:ncols], g.ONESB[:], SQ[:, kc, 0:ncols], kc == 0, kc == KC - 1, [rSQ, g.rconst], [rPSs])
    ACT(g, RSTD[:, 0:ncols], PSs[:, 0:ncols], AF.Sqrt, [rPSs], [rRSTD], scale=1.0 / D, bias=EPS)
    g.S.op("dve", lambda e: e.reciprocal(out=RSTD[:, 0:ncols], in_=RSTD[:, 0:ncols]), [rRSTD], [rRSTD])
    for kc in range(KC):
        STT(g, XN[:, kc, col0:col0 + ncols], XF[:, kc, 0:ncols], g.NORMS[:, gain_idx, kc:kc + 1], RSTD[:, 0:ncols],
            ALU.mult, ALU.mult, [rXF, rRSTD, g.rconst], [rXN])


def phaseA(g, l):
    nc, S, cfg = g.nc, g.S, g.cfg
    S.barrier()
    with ExitStack() as es:
        WIN = g.sbt(es, "A_win", [128, KC, DIN], BF16)
        rW = S.res()
        wsrc = g.I["w_in"][l].rearrange("(kc p) n -> p kc n", p=128)
        for kc in range(KC):
            DMA(g, "pool", WIN[:, kc, :], wsrc[:, kc, :], "A_w%d" % (kc % 4), [], [rW])
        XF = [g.sbt(es, "A_xf%d" % i, [128, KC, TT], F32) for i in range(2)]
        rXF = [S.res() for _ in range(2)]
        SQ = g.sbt(es, "A_sq", [128, KC, TT], BF16)
        rSQ = S.res()
        RSTD = g.sbt(es, "A_rstd", [128, TT], F32)
        rRSTD = S.res()
        XN = [g.sbt(es, "A_xn%d" % i, [128, KC, TT], BF16) for i in range(2)]
        rXN = [S.res() for _ in range(2)]
        STG = [g.sbt(es, "A_stg%d" % i, [128, 4, TT], BF16) for i in range(3)]
        rSTG = [S.res() for _ in range(3)]
        GST = [g.sbt(es, "A_gst%d" % i, [32, TT], F32) for i in range(2)]
        rGST = [S.res() for _ in range(2)]
        PSs = g.pst(es, "A_pss", [128, TT], F32)
        rPSs = S.pres()
        PS = [g.pst(es, "A_ps%d" % i, [128, TT], F32) for i in range(6)]
        rPS = [S.pres() for _ in range(6)]
        rdram = S.res()
        ntile = cfg.NT // TT
        psi = [0]
        stgi = [0]
        XTv = g.W["XT"].rearrange("(kc p) t -> p kc t", p=128)
        def prepart(ti):
            c0 = ti * TT
            b = ti % 2
            DMA(g, "sp", XF[b][:], XTv[:, :, c0:c0 + TT], "A_x%d" % b, [], [rXF[b]])
            rmsnorm_fm(g, XF[b], rXF[b], XN[b], rXN[b], SQ, rSQ, RSTD, rRSTD, PSs, rPSs, 3 * l + 0, TT)

        def mainpart(ti):
            c0 = ti * TT
            b = ti % 2
            groups = [(g.W["QKN"], 0, 0, 4, 0.125), (g.W["QKN"], 512, 512, 4, None),
                      (g.W["QKVD"], 0, 1536, 4, None), (g.W["QKVD"], 512, 2048, 4, None), (g.W["QKVD"], 1024, 2560, 4, None),
                      (g.W["ZT"], 0, 3072, 4, None)]
            for (dst, r0, wc0, nm, scale) in groups:
                sg = stgi[0] % 3
                stgi[0] += 1
                for m in range(nm):
                    p = psi[0] % 6
                    psi[0] += 1
                    for kc in range(KC):
                        MM(g, PS[p][:], WIN[:, kc, wc0 + m * 128: wc0 + (m + 1) * 128], XN[b][:, kc, :], kc == 0, kc == KC - 1,
                           [rW, rXN[b]], [rPS[p]])
                    if scale is not None:
                        ACT(g, STG[sg][:, m, :], PS[p][:], AF.Copy, [rPS[p]], [rSTG[sg]], scale=scale)
                    elif m % 2 == 0:
                        CP(g, "act", STG[sg][:, m, :], PS[p][:], [rPS[p]], [rSTG[sg]])
                    else:
                        CP(g, "dve", STG[sg][:, m, :], PS[p][:], [rPS[p]], [rSTG[sg]])
                d = dst[r0:r0 + 512, c0:c0 + TT].rearrange("(m p) t -> p m t", p=128)
                DMA(g, "sp", d, STG[sg][:], "A_st%d" % sg, [rSTG[sg]], [rdram])
            p = psi[0] % 6
            psi[0] += 1
            for kc in range(KC):
                MM(g, PS[p][0:32, :], WIN[:, kc, 3584:3616], XN[b][:, kc, :], kc == 0, kc == KC - 1, [rW, rXN[b]], [rPS[p]])
            gb = ti % 2
            CP(g, "dve", GST[gb][:], PS[p][0:32, :], [rPS[p]], [rGST[gb]])
            DMA(g, "sp", g.W["GR"][:, c0:c0 + TT], GST[gb][:], "A_g%d" % gb, [rGST[gb]], [rdram])
            sg = stgi[0] % 3
            stgi[0] += 1
            for ts in range(4):
                p = psi[0] % 6
                psi[0] += 1
                for kc in range(KC):
                    MM(g, PS[p][:], XN[b][:, kc, ts * 128:(ts + 1) * 128], WIN[:, kc, 1024:1536], kc == 0, kc == KC - 1,
                       [rW, rXN[b]], [rPS[p]])
                CP(g, "act" if ts % 2 == 0 else "dve", STG[sg][:, ts, :], PS[p][:], [rPS[p]], [rSTG[sg]])
            d = g.W["VN"][c0:c0 + TT, :].rearrange("(s p) f -> p s f", p=128)
            DMA(g, "sp", d, STG[sg][:], "A_st%d" % sg, [rSTG[sg]], [rdram])

        def cap(fn, *a):
            S.capture()
            fn(*a)
            return S.end_capture()

        S.extend(cap(prepart, 0))
        for ti in range(ntile):
            lists = [cap(mainpart, ti)]
            if ti + 1 < ntile:
                lists.append(cap(prepart, ti + 1))
            S.extend(interleave(*lists))
        S.barrier()


def phaseB(g, l):
    nc, S, cfg = g.nc, g.S, g.cfg
    S.barrier()
    Tmax = max(T for (_, _, T) in cfg.seqs)
    with ExitStack() as es:
        BIAS = g.sbt(es, "B_bias", [128, 8, 9, 128], BF16)
        rBIAS = S.res()
        bsrc = g.I["na_bias"][l].rearrange("p (h c q) -> p h c q", h=8, c=9)
        for h in range(8):
            DMA(g, "pool", BIAS[:, h], bsrc[:, h], "B_b%d" % (h % 2), [], [rBIAS])
        NB = 2
        QT = [g.sbt(es, "B_qt%d" % i, [128, Tmax], BF16) for i in range(NB)]
        KT = [g.sbt(es, "B_kt%d" % i, [128, Tmax], BF16) for i in range(NB)]
        V = [g.sbt(es, "B_v%d" % i, [128, Tmax // 128, 2, 65], BF16) for i in range(NB)]
        YT = [g.sbt(es, "B_yt%d" % i, [128, Tmax], BF16) for i in range(NB)]
        rQT = [S.res() for _ in range(NB)]
        rKT = [S.res() for _ in range(NB)]
        rV = [S.res() for _ in range(NB)]
        rYT = [S.res() for _ in range(NB)]
        for i in range(NB):
            S.op("pool", lambda e, i=i: e.memset(V[i][:], 1.0), [], [rV[i]])
        NP_ = 2
        PT = [g.sbt(es, "B_pt%d" % i, [128, 5, 2, 128], BF16) for i in range(NP_)]
        rPT = [S.res() for _ in range(NP_)]
        QBD = [g.sbt(es, "B_qbd%d" % i, [128, 2, 128], BF16) for i in range(2)]
        rQBD = [S.res() for _ in range(2)]
        for i in range(2):
            S.op("pool", lambda e, i=i: e.memset(QBD[i][:], 0.0), [], [rQBD[i]])
        YTM = [g.sbt(es, "B_ytm%d" % i, [128, 128], BF16) for i in range(2)]
        rYTM = [S.res() for _ in range(2)]
        RC = [g.sbt(es, "B_rc%d" % i, [128, 2], F32) for i in range(2)]
        rRC = [S.res() for _ in range(2)]
        PSS = [g.pst(es, "B_pss%d" % i, [128, 1536], F32) for i in range(2)]
        rPSS = [S.pres() for _ in range(2)]
        PSO = g.pst(es, "B_pso", [128, 512], F32)
        rPSO = S.pres()
        PSTr = g.pst(es, "B_pstr", [128, 512], F32)
        rPSTr = S.pres()
        rdram = S.res()
        it = 0
        cnt = 0
        units = [(si_, hp_) for si_ in range(len(cfg.seqs)) for hp_ in range(4)]

        def load_unit(u):
            si_, hp_ = units[u]
            T_ = cfg.seqs[si_][2]
            c0_ = cfg.offs[si_]
            b_ = u % NB
            DMA(g, "sp", QT[b_][:, 0:T_], g.W["QKN"][hp_ * 128:(hp_ + 1) * 128, c0_:c0_ + T_], "B_q%d" % b_, [], [rQT[b_]])
            DMA(g, "sp", KT[b_][:, 0:T_], g.W["QKN"][512 + hp_ * 128:512 + (hp_ + 1) * 128, c0_:c0_ + T_], "B_k%d" % b_, [], [rKT[b_]])
            vsrc = g.W["VN"][c0_:c0_ + T_, hp_ * 128:(hp_ + 1) * 128].rearrange("(t p) (h e) -> p t h e", p=128, h=2)
            for hh_ in range(2):
                DMA(g, "pool", V[b_][:, 0:T_ // 128, hh_, 0:64], vsrc[:, :, hh_, :], "B_v%d%d" % (b_, hh_), [], [rV[b_]])

        load_unit(0)
        for si, (grp, sidx, T) in enumerate(cfg.seqs):
            pairs = _na_pairs(T)
            c0 = cfg.offs[si]
            ntl = T // 128
            for hp in range(4):
                b = it % NB
                it += 1
                if it < len(units):
                    load_unit(it)
                for i in range(ntl):
                    yb = i % 2
                    lst = pairs[i]
                    nt = len(lst)
                    sp_ = cnt % 2
                    pp = cnt % NP_
                    cnt += 1
                    qs = slice(i * 128, (i + 1) * 128)
                    for hh in range(2):
                        b_ = slice(64 * hh, 64 * hh + 64)
                        CP(g, "pool", QBD[sp_][b_, hh, :], QT[b][b_, qs], [rQT[b]], [rQBD[sp_]])
                    PSv = PSS[sp_][:].rearrange("p (t h q) -> p t h q", h=2, q=128)
                    for jt, (j, cls) in enumerate(lst):
                        MM(g, PSS[sp_][:, jt * 256:(jt + 1) * 256], KT[b][:, j * 128:(j + 1) * 128],
                           QBD[sp_][:].rearrange("p h q -> p (h q)"), True, False, [rKT[b], rQBD[sp_]], [rPSS[sp_]])
                        MM(g, PSv[:, jt], g.IDB[:], BIAS[:, 2 * hp:2 * hp + 2, cls, :], False, True,
                           [g.rconst, rBIAS], [rPSS[sp_]])
                    n1 = min(nt, 4)
                    ACT(g, PT[pp][:, 0:n1].rearrange("p t h q -> p (t h q)"), PSS[sp_][:, 0:n1 * 256], AF.Exp, [rPSS[sp_]], [rPT[pp]])
                    if nt > 4:
                        ACT(g, PT[pp][:, 4].rearrange("p h q -> p (h q)"), PSS[sp_][:, 1024:1280], AF.Exp, [rPSS[sp_]], [rPT[pp]])
                    for hh in range(2):
                        for jt, (j, cls) in enumerate(lst):
                            MM(g, PSO[:, hh * 128:hh * 128 + 65], PT[pp][:, jt, hh, :], V[b][:, j, hh, :], jt == 0, jt == nt - 1,
                               [rPT[pp], rV[b]], [rPSO])
                    PSOv = PSO[:, 0:256].rearrange("p (h e) -> p h e", h=2)
                    rc = sp_
                    S.op("dve", lambda e, rc=rc, PSOv=PSOv: e.reciprocal(out=RC[rc][:], in_=PSOv[:, :, 64]), [rPSO], [rRC[rc]])
                    TT_(g, "dve", YTM[yb][:].rearrange("p (h e) -> p h e", h=2), PSOv[:, :, 0:64],
                        RC[rc][:, :, None].to_broadcast([128, 2, 64]), ALU.mult, [rPSO, rRC[rc]], [rYTM[yb]])
                    MM(g, PSTr[:, 0:128], YTM[yb][:], g.IDB[:], True, True, [rYTM[yb], g.rconst], [rPSTr])
                    CP(g, "act", YT[b][:, i * 128:(i + 1) * 128], PSTr[:, 0:128], [rPSTr], [rYT[b]])
                DMA(g, "sp", g.W["YMIX"][hp * 128:(hp + 1) * 128, c0:c0 + T], YT[b][:, 0:T], "B_y%d" % b, [rYT[b]], [rdram])
        S.barrier()


def phaseC(g, l):
    if g.cfg.skip_gdn:
        nc, S, cfg = g.nc, g.S, g.cfg
        S.barrier()
        with ExitStack() as es:
            Z = g.sbt(es, "C_z", [128, 4, 2048], BF16)
            rZ, rd = S.res(), S.res()
            S.op("dve", lambda e: e.memset(Z[:], 0.0), [], [rZ])
            for c0 in range(0, cfg.NT, 2048):
                w = min(2048, cfg.NT - c0)
                DMA(g, "sp", g.W["YMIX"][512:1024, c0:c0 + w].rearrange("(m p) t -> p m t", p=128), Z[:, :, 0:w], "C_z", [rZ], [rd])
            S.barrier()
        return
    phaseC_real(g, l)


def phaseC_real(g, l):
    nc, S, cfg = g.nc, g.S, g.cfg
    S.barrier()
    Tmax = max(T for (_, _, T) in cfg.seqs)
    NTLmax = Tmax // 128
    NCHmax = Tmax // 64
    GB = 256
    CB = 1024 if all(T % 1024 == 0 for (_, _, T) in cfg.seqs) else 512
    I, W = g.I, g.W
    with ExitStack() as es:
        sb = lambda n, shp, dt=F32: g.sbt(es, "C_" + n, shp, dt)
        MASK = sb("mask", [128, 2, 3, 128], BF16)
        SELR = sb("selr", [128, 16, 128], BF16)
        SELP = sb("selp", [128, 8, 128], BF16)
        IDF = sb("idf", [128, 128], F32)
        RESET = sb("reset", [128, GB], F32)
        CWQ = sb("cwq", [128, 12, 3], F32)
        NORMO = sb("normo", [128, 1], F32)
        DTB = sb("dtb", [40, 1], F32)
        NEGA = sb("nega", [40, 1], F32)
        rC = S.res()
        DMA(g, "pool", MASK[:].rearrange("p a b c -> p (a b c)"), I["c_gmask"], "C_c0", [], [rC])
        DMA(g, "pool", SELR[:].rearrange("p a b -> p (a b)"), I["c_selr"], "C_c1", [], [rC])
        DMA(g, "pool", SELP[:].rearrange("p a b -> p (a b)"), I["c_selp"], "C_c0", [], [rC])
        DMA(g, "sp", IDF[:], I["c_ident"], "C_c2", [], [rC])
        DMA(g, "sp", RESET[:], I["c_reset"][:, 0:GB], "C_c2", [], [rC])
        for k in range(3):
            small_vec_load(g, "sp", CWQ[:, :, k], I["conv_qkv"][l, k].rearrange("(f p) -> p f", p=128), "C_c2", [rC])
        for hh in range(2):
            small_vec_load(g, "sp", NORMO[64 * hh:64 * hh + 64, :], I["norm_o"][l].rearrange("(p o) -> p o", o=1), "C_c2", [rC])
        S.op("dve", lambda e: e.memset(DTB[:], 0.0), [], [rC])
        S.op("dve", lambda e: e.memset(NEGA[:], 0.0), [], [rC])
        for d_ in range(2):
            small_vec_load(g, "sp", DTB[32 * d_:32 * d_ + 8, :], I["dt_bias"][l, 8 * d_:8 * d_ + 8].rearrange("(p o) -> p o", o=1), "C_c2", [rC])
            small_vec_load(g, "sp", NEGA[32 * d_:32 * d_ + 8, :], I["a_log"][l, 8 * d_:8 * d_ + 8].rearrange("(p o) -> p o", o=1), "C_c2", [rC])
        ACT(g, NEGA[:], NEGA[:], AF.Exp, [rC], [rC])
        TS_(g, "dve", NEGA[:], NEGA[:], -1.0, None, ALU.mult, None, [rC], [rC])

        GCS = sb("gcs", [128, Tmax], BF16)
        HS = sb("hs", [128, Tmax], BF16)
        GLS = sb("gls", [128, NCHmax], BF16)
        SCT = sb("sct", [128, NTLmax, 200], F32)
        rGCS, rHS, rGLS, rSCT = S.res(), S.res(), S.res(), S.res()
        for t_ in (GCS, HS, GLS):
            S.op("pool", lambda e, t_=t_: e.memset(t_[:], 0.0), [], [rGCS, rHS, rGLS])
        AR = sb("ar", [40, GB]); BR = sb("br", [40, GB])
        rAR, rBR = S.res(), S.res()
        S.op("pool", lambda e: e.memset(AR[:], 0.0), [], [rAR])
        S.op("pool", lambda e: e.memset(BR[:], 0.0), [], [rBR])
        G_ = sb("g", [40, GB]); PRE = sb("pre", [40, GB]); GC = sb("gc", [40, GB]); LNB = sb("lnb", [40, GB])
        TMPG = sb("tmpg", [40, GB])
        RR = sb("rr", [40, 5, GB])
        RH = sb("rh", [40, 5, GB], BF16); RL = sb("rl", [40, 5, GB], BF16)
        GLF = sb("glf", [40, GB // 64])
        rG = S.res()
        S.op("dve", lambda e: e.memset(GC[:], 0.0), [], [rG])

        XR = sb("xr", [128, 3, CB + 2], BF16)
        ACC = [sb("acc%d" % i, [128, CB]) for i in range(2)]
        SQB = sb("sqb", [128, CB], BF16)
        RSB = sb("rsb", [128, 512])
        QT = sb("qt", [128, Tmax], BF16); KT = sb("kt", [128, Tmax], BF16); VT = sb("vt", [128, Tmax], BF16)
        KTZ = sb("ktz", [128, 2, Tmax], BF16)
        rKTZ = S.res()
        S.op("pool", lambda e: e.memset(KTZ[:], 0.0), [], [rKTZ])
        SZ = sb("sz", [128, Tmax], BF16)
        YT = sb("yt", [128, Tmax], BF16)
        OF = sb("of", [128, NTLmax, 128])
        GLB = sb("glb", [128, 2, NCHmax])
        rXR, rSQB, rRSB, rQT, rKT, rVT, rSZ, rYT, rOF, rGLB = [S.res() for _ in range(10)]
        rACC = [S.res() for _ in range(2)]
        KVTM = [sb("kvtm%d" % i, [128, 2, 128], BF16) for i in range(4)]
        E01 = [sb("e01%d" % i, [128, 2, 256]) for i in range(4)]
        E2 = [sb("e2%d" % i, [128, 2, 128]) for i in range(4)]
        EGC = [sb("egc%d" % i, [128, 128]) for i in range(4)]
        PB_A = [[sb("pb%d_%d" % (d_, i), [128, 2, 128], BF16) for i in range(2)] for d_ in range(2)]
        PTY_A = [[sb("pty%d_%d" % (d_, i), [128, 2, 2, 128], BF16) for i in range(2)] for d_ in range(2)]
        YF_A = [sb("yf%d" % d_, [128, 2, 128]) for d_ in range(2)]
        TB = [sb("tb%d" % i, [128, 2, 128], BF16) for i in range(4)]
        TBG = [sb("tbg%d" % i, [128, 2, 128], BF16) for i in range(4)]
        U = [sb("u%d" % i, [128, 128]) for i in range(4)]
        WT = [sb("wt%d" % i, [128, 2, 128], BF16) for i in range(4)]
        QGT = [sb("qgt%d" % i, [128, 2, 128], BF16) for i in range(4)]
        QKT = [sb("qkt%d" % i, [128, 2, 128], BF16) for i in range(4)]
        KGZ = [sb("kgz%d" % i, [128, 2, 128], BF16) for i in range(4)]
        rKVTM, rE01, rE2, rEGC, rTB, rTBG, rU, rWT, rQGT, rQKT, rKGZ = [[S.res() for _ in range(4)] for _ in range(11)]
        rPB_A = [[S.res() for _ in range(2)] for _ in range(2)]
        rPTY_A = [[S.res() for _ in range(2)] for _ in range(2)]
        rYF_A = [S.res() for _ in range(2)]
        for i in range(4):
            S.op("pool", lambda e, i=i: e.memset(KGZ[i][:], 0.0), [], [rKGZ[i]])
            S.op("pool", lambda e, i=i: e.memset(WT[i][:], 0.0), [], [rWT[i]])
            S.op("pool", lambda e, i=i: e.memset(QGT[i][:], 0.0), [], [rQGT[i]])
        SF_A = [sb("sf%d" % d_, [128, 64]) for d_ in range(2)]
        SBs_A = [sb("sbs%d" % d_, [128, 64], BF16) for d_ in range(2)]
        rSF_A = [S.res() for _ in range(2)]
        rSBs_A = [S.res() for _ in range(2)]
        VNEW_A = [[sb("vnew%d_%d" % (d_, i), [128, 128], BF16) for i in range(2)] for d_ in range(2)]
        rVNEW_A = [[S.res() for _ in range(2)] for _ in range(2)]
        OT = [sb("ot%d" % i, [128, 128]) for i in range(2)]
        ON = [sb("on%d" % i, [128, 128], BF16) for i in range(2)]
        SSQ = [sb("ssq%d" % i, [128, 4]) for i in range(2)]
        JUNK = sb("junk", [128, 64])
        rOT = [S.res() for _ in range(2)]
        rON = [S.res() for _ in range(2)]
        rSSQ = [S.res() for _ in range(2)]
        rJ = S.res()
        PD = g.pst(es, "C_pd", [128, 512]); rPD = S.pres()
        PG = g.pst(es, "C_pg", [128, 512]); rPG = S.pres()
        PX, rPX = PD, rPD
        PA_A = [g.pst(es, "C_pa%d" % d_, [128, 512]) for d_ in range(2)]
        rPA_A = [S.pres() for _ in range(2)]
        PBk_A = [g.pst(es, "C_pbk%d" % d_, [128, 512]) for d_ in range(2)]
        rPBk_A = [S.pres() for _ in range(2)]
        CH_A = [g.pst(es, "C_ch%d" % d_, [128, 512]) for d_ in range(2)]
        rCH_A = [S.pres() for _ in range(2)]
        rdram = S.res()
        IDB = g.IDB
        rI = g.rconst

        def row(d_, h):
            return 32 * d_ + h

        def gate_prep(si, T, c0):
            ntl = T // 128
            for gb in range(T // GB):
                cg = c0 + gb * GB
                for d_ in range(2):
                    DMA(g, "sp", AR[32 * d_:32 * d_ + 8, :], W["GR"][16 + 8 * d_:24 + 8 * d_, cg:cg + GB], "C_ga", [], [rAR])
                    DMA(g, "sp", BR[32 * d_:32 * d_ + 8, :], W["GR"][8 * d_:8 + 8 * d_, cg:cg + GB], "C_gb", [], [rBR])
                R = [rG, rC]
                TS_(g, "dve", G_[:], AR[:], DTB[:, 0:1], None, ALU.add, None, [rAR] + R, [rG])
                ACT(g, G_[:], G_[:], AF.Exp, R, [rG])
                ACT(g, G_[:], G_[:], AF.Ln, R, [rG], bias=1.0)
                TS_(g, "dve", G_[:], G_[:], NEGA[:, 0:1], None, ALU.mult, None, R, [rG])
                S.op("dve", lambda e: e.tensor_tensor_scan(out=PRE[:], data0=RESET[0:40, :], data1=G_[:], initial=0.0,
                                                           op0=ALU.mult, op1=ALU.add), R, [rG])
                totb = PRE[:].rearrange("p (n c) -> p n c", c=64)[:, :, 63:64].to_broadcast([40, GB // 64, 64])
                v3 = lambda t_: t_[:].rearrange("p (n c) -> p n c", c=64)
                CP(g, "dve", GC[0:8, :], PRE[0:8, :], R, [rG])
                TT_(g, "dve", v3(TMPG)[32:40], totb[32:40], v3(PRE)[32:40], ALU.subtract, R, [rG])
                TT_(g, "dve", GC[32:40, :], TMPG[32:40, :], G_[32:40, :], ALU.add, R, [rG])
                ACT(g, RR[:, 2, :], BR[:], AF.Sigmoid, [rBR] + R, [rG])
                ACT(g, LNB[:], RR[:, 2, :], AF.Ln, R, [rG])
                TS_(g, "dve", RR[:, 0, :], GC[:], -1.0, None, ALU.mult, None, R, [rG])
                TT_(g, "dve", RR[:, 1, :], GC[:], LNB[:], ALU.add, R, [rG])
                ACT(g, TMPG[:], GC[:], AF.Exp, R, [rG])
                TT_(g, "dve", RR[:, 3, :], RR[:, 2, :], TMPG[:], ALU.mult, R, [rG])
                TT_(g, "dve", v3(TMPG), totb, v3(GC), ALU.subtract, R, [rG])
                ACT(g, RR[:, 4, :], TMPG[:], AF.Exp, R, [rG])
                ACT(g, GLF[:], PRE[:].rearrange("p (n c) -> p n c", c=64)[:, :, 63], AF.Exp, R, [rG])
                CP(g, "act", RH[:], RR[:], R, [rG])
                TT_(g, "dve", RL[:], RR[:], RH[:], ALU.subtract, R, [rG])
                cs = slice(gb * GB, (gb + 1) * GB)
                CP(g, "act", GCS[0:40, cs], GC[:], R, [rGCS])
                TT_(g, "dve", GCS[64:104, cs], GC[:], GCS[0:40, cs], ALU.subtract, R + [rGCS], [rGCS])
                CP(g, "act", HS[0:40, cs], RR[:, 1, :], R, [rHS])
                TT_(g, "dve", HS[64:104, cs], RR[:, 1, :], HS[0:40, cs], ALU.subtract, R + [rHS], [rHS])
                ns = slice(gb * (GB // 64), (gb + 1) * (GB // 64))
                CP(g, "act", GLS[0:40, ns], GLF[:], R, [rGLS])
                TT_(g, "dve", GLS[64:104, ns], GLF[:], GLS[0:40, ns], ALU.subtract, R + [rGLS], [rGLS])
                for t4 in range(GB // 128):
                    m = gb * (GB // 128) + t4
                    for k in range(5):
                        MM(g, PD[:, k * 40:(k + 1) * 40], RH[:, k, t4 * 128:(t4 + 1) * 128], IDB[0:40, 0:40], True, False, [rG, rI], [rPD])
                        MM(g, PD[:, k * 40:(k + 1) * 40], RL[:, k, t4 * 128:(t4 + 1) * 128], IDB[0:40, 0:40], False, True, [rG, rI], [rPD])
                    CP(g, "dve", SCT[:, m, :], PD[:, 0:200], [rPD], [rSCT])

        def sc(m, k, d_, h):
            c = k * 40 + row(d_, h)
            return SCT[:, m, c:c + 1]

        def qkv_prep(si, T, c0, hp):
            for cb in range(T // CB):
                cc = c0 + cb * CB
                lo = 0 if cb > 0 else 1
                hi = CB + 2 if cb < T // CB - 1 else CB + 1
                if lo == 1:
                    S.op("pool", lambda e: e.memset(XR[:, :, 0:1], 0.0), [], [rXR])
                if hi == CB + 1:
                    S.op("pool", lambda e: e.memset(XR[:, :, CB + 1:CB + 2], 0.0), [], [rXR])
                for j in range(3):
                    r0 = j * 512 + hp * 128
                    DMA(g, "sp" if j != 1 else "act", XR[:, j, lo:hi], W["QKVD"][r0:r0 + 128, cc - 1 + lo:cc - 1 + hi], "C_xr%d" % j, [], [rXR])
                cs = slice(cb * CB, (cb + 1) * CB)
                for j in range(3):
                    f = j * 4 + hp
                    a = j % 2
                    A_, rA = ACC[a], rACC[a]
                    ACT(g, A_[:], XR[:, j, 0:CB], AF.Copy, [rXR, rC], [rA], scale=CWQ[:, f, 0:1])
                    STT(g, A_[:], XR[:, j, 1:CB + 1], CWQ[:, f, 1:2], A_[:], ALU.mult, ALU.add, [rXR, rC, rA], [rA])
                    STT(g, A_[:], XR[:, j, 2:CB + 2], CWQ[:, f, 2:3], A_[:], ALU.mult, ALU.add, [rXR, rC, rA], [rA])
                    if j == 2:
                        ACT(g, VT[:, cs], A_[:], AF.Silu, [rA], [rVT])
                        continue
                    ACT(g, A_[:], A_[:], AF.Silu, [rA], [rA])
                    ACT(g, SQB[:], A_[:], AF.Square, [rA], [rSQB])
                    dst, rdst = (QT, rQT) if j == 0 else (KT, rKT)
                    for hf in range(CB // 512):
                        MM(g, PG[:], g.ONESBD[:], SQB[:, hf * 512:(hf + 1) * 512], True, True, [rSQB, rI], [rPG])
                        ACT(g, RSB[:], PG[:], AF.Sqrt, [rPG], [rRSB], bias=1e-6, scale=1.0)
                        S.op("dve", lambda e: e.reciprocal(out=RSB[:], in_=RSB[:]), [rRSB], [rRSB])
                        dsl = slice(cb * CB + hf * 512, cb * CB + (hf + 1) * 512)
                        if j == 0:
                            STT(g, dst[:, dsl], A_[:, hf * 512:(hf + 1) * 512], 0.125, RSB[:], ALU.mult, ALU.mult, [rA, rRSB], [rdst])
                        else:
                            TT_(g, "dve", dst[:, dsl], A_[:, hf * 512:(hf + 1) * 512], RSB[:], ALU.mult, [rA, rRSB], [rdst])
                            for hh in range(2):
                                b_ = slice(64 * hh, 64 * hh + 64)
                                TT_(g, "pool", KTZ[b_, hh, dsl], A_[b_, hf * 512:(hf + 1) * 512], RSB[b_, :], ALU.mult, [rA, rRSB], [rKTZ])
            DMA(g, "act", SZ[:, 0:T], W["ZT"][hp * 128:(hp + 1) * 128, c0:c0 + T], "C_z", [], [rSZ])
            ACT(g, SZ[:, 0:T], SZ[:, 0:T], AF.Silu, [rSZ], [rSZ])
            nch = T // 64
            for d_ in range(2):
                MM(g, PX[:, 0:nch], SELP[:, d_ * 4 + hp, :], GLS[:, 0:nch], True, True, [rC, rGLS], [rPX])
                CP(g, "dve", GLB[:, d_, 0:nch], PX[:, 0:nch], [rPX], [rGLB])

        def pre(m, d_, hp, pb):
            ts_ = slice(m * 128, (m + 1) * 128)
            PB_, PTY, YF, rPB_, rPTY, rYF = PB_A[d_], PTY_A[d_], YF_A[d_], rPB_A[d_], rPTY_A[d_], rYF_A[d_]
            PA, PBk, rPA, rPBk = PA_A[d_], PBk_A[d_], rPA_A[d_], rPBk_A[d_]
            S.begin_atomic()
            MM(g, PX[:, 0:128], KT[:, ts_], IDB[:], True, True, [rKT, rI], [rPX])
            MM(g, PX[:, 128:256], VT[:, ts_], IDB[:], True, True, [rVT, rI], [rPX])
            CP(g, "act", KVTM[pb][:].rearrange("p a b -> p (a b)"), PX[:, 0:256], [rPX], [rKVTM[pb]])
            if cfg.gdn_stage < 3.07:
                S.end_atomic()
                return
            for hh in range(2):
                b_ = slice(64 * hh, 64 * hh + 64)
                MM(g, PG[:, hh * 256:hh * 256 + 128], KTZ[:, hh, ts_], KT[:, ts_], True, True, [rKTZ, rKT], [rPG])
                MM(g, PG[:, hh * 256 + 128:hh * 256 + 256], KTZ[:, hh, ts_], QT[:, ts_], True, True, [rKTZ, rQT], [rPG])
            if cfg.gdn_stage < 3.15:
                S.end_atomic()
                return
            for hh in range(2):
                h = 2 * hp + hh
                sel = SELR[:, d_ * 8 + h, :]
                MM(g, PD[:, 0:128], sel, HS[:, ts_], True, False, [rC, rHS], [rPD])
                MM(g, PD[:, 128:512].rearrange("p (a b) -> p a b", a=3), sel, GCS[:, None, ts_].to_broadcast([128, 3, 128]),
                   False, False, [rC, rGCS], [rPD])
                MM(g, PD[:, 0:384], IDB[:], MASK[:, d_, :, :].rearrange("p a b -> p (a b)"), False, True, [rI, rC], [rPD])
                ACT(g, E01[pb][:, hh, :], PD[:, 0:256], AF.Exp, [rPD, rSCT], [rE01[pb]], bias=sc(m, 0, d_, h), scale=1.0)
                ACT(g, E2[pb][:, hh, :], PD[:, 256:384], AF.Exp, [rPD, rSCT], [rE2[pb]], bias=sc(m, 1, d_, h), scale=-1.0)
                b_ = slice(64 * hh, 64 * hh + 64)
                ACT(g, EGC[pb][b_, :], PD[b_, 384:512], AF.Exp, [rPD], [rEGC[pb]])
            if cfg.gdn_stage < 3.25:
                S.end_atomic()
                return
            PGv = PG[:].rearrange("p (a b) -> p a b", a=2)
            STT(g, PTY[0][:, :, 0, :], PGv[:, :, 0:128], -1.0, E01[pb][:, :, 0:128], ALU.mult, ALU.mult, [rPG, rE01[pb]], [rPTY[0]])
            TT_(g, "dve", PTY[1][:, :, 1, :], PTY[0][:, :, 0, :], IDF[:, None, :].to_broadcast([128, 2, 128]), ALU.add,
                [rPTY[0], rC], [rPTY[1]])
            STT(g, PB_[0][:], PGv[:, :, 0:128], -1.0, E2[pb][:], ALU.mult, ALU.mult, [rPG, rE2[pb]], [rPB_[0]])
            TT_(g, "dve", QKT[pb][:], PGv[:, :, 128:256], E01[pb][:, :, 128:256], ALU.mult, [rPG, rE01[pb]], [rQKT[pb]])
            S.end_atomic()
            for hh in range(2):
                b_ = slice(64 * hh, 64 * hh + 64)
                TT_(g, "pool", QGT[pb][b_, hh, :], QT[b_, ts_], EGC[pb][b_, :], ALU.mult, [rQT, rEGC[pb]], [rQGT[pb]])
            for hh in range(2):
                h = 2 * hp + hh
                b_ = slice(64 * hh, 64 * hh + 64)
                ACT(g, KGZ[pb][:, hh, b_], KVTM[pb][:, 0, b_], AF.Copy, [rKVTM[pb], rSCT], [rKGZ[pb]], scale=sc(m, 4, d_, h))
            if cfg.gdn_stage < 3.35:
                return
            PAv = PA[:].rearrange("p (a b) -> p a b", a=2)
            PBv = PBk[:, 0:256].rearrange("p (a b) -> p a b", a=2)
            for k in range(6):
                cur, nxt = k % 2, (k + 1) % 2
                for hh in range(2):
                    if k == 0:
                        MM(g, PAv[:, hh, 0:128], PB_[cur][:, hh, :], PTY[cur][:, hh, 0, :], True, True, [rPB_[cur], rPTY[cur]], [rPA])
                    elif k < 4:
                        MM(g, PAv[:, hh, :], PB_[cur][:, hh, :], PTY[cur][:, hh].rearrange("p a b -> p (a b)"), True, True,
                           [rPB_[cur], rPTY[cur]], [rPA])
                    elif k == 4:
                        MM(g, PAv[:, hh, 128:256], PB_[cur][:, hh, :], PTY[cur][:, hh, 1, :], True, True, [rPB_[cur], rPTY[cur]], [rPA])
                    else:
                        MM(g, PAv[:, hh, 128:256], PB_[cur][:, hh, :], PTY[cur][:, hh, 1, :], True, True, [rPB_[cur], rPTY[cur]], [rPA])
                    if k < 5:
                        MM(g, PBv[:, hh, :], PTY[cur][:, hh, 0, :], PB_[cur][:, hh, :], True, True, [rPB_[cur], rPTY[cur]], [rPBk])
                if k < 5:
                    if k < 4:
                        CP(g, "act", PTY[nxt][:, :, 0, :], PAv[:, :, 0:128], [rPA], [rPTY[nxt]])
                    if k == 0:
                        pass
                    else:
                        TT_(g, "dve", PTY[nxt][:, :, 1, :], PAv[:, :, 128:256], PTY[cur][:, :, 1, :], ALU.add, [rPA, rPTY[cur]], [rPTY[nxt]])
                    CP(g, "act" if k % 2 else "dve", PB_[nxt][:], PBv, [rPBk], [rPB_[nxt]])
                else:
                    TT_(g, "dve", YF[:], PAv[:, :, 128:256], PTY[cur][:, :, 1, :], ALU.add, [rPA, rPTY[cur]], [rYF])
            if cfg.gdn_stage < 3.45:
                return
            for hh in range(2):
                h = 2 * hp + hh
                TS_(g, "dve", TB[pb][:, hh, :], YF[:, hh, :], sc(m, 2, d_, h), None, ALU.mult, None, [rYF, rSCT], [rTB[pb]])
                ACT(g, TBG[pb][:, hh, :], YF[:, hh, :], AF.Copy, [rYF, rSCT], [rTBG[pb]], scale=sc(m, 3, d_, h))
            S.begin_atomic()
            for hh in range(2):
                MM(g, PX[:, hh * 64:(hh + 1) * 64], TB[pb][:, hh, :], KVTM[pb][:, 1, hh * 64:(hh + 1) * 64], True, True,
                   [rTB[pb], rKVTM[pb]], [rPX])
                MM(g, PX[:, 128 + hh * 128:256 + hh * 128], KVTM[pb][:, 0, :], TBG[pb][:, hh, :], True, True,
                   [rTBG[pb], rKVTM[pb]], [rPX])
            CP(g, "act", U[pb][:], PX[:, 0:128], [rPX], [rU[pb]])
            for hh in range(2):
                b_ = slice(64 * hh, 64 * hh + 64)
                CP(g, "dve", WT[pb][b_, hh, :], PX[b_, 128 + hh * 128:256 + hh * 128], [rPX], [rWT[pb]])
            S.end_atomic()

        vc = [0]
        oc = [0]

        def chain(m, d_, hp, pb, T, second):
            SF, SBs, rSF, rSBs = SF_A[d_], SBs_A[d_], rSF_A[d_], rSBs_A[d_]
            VNEW, rVNEW = VNEW_A[d_], rVNEW_A[d_]
            CH, rCH = CH_A[d_], rCH_A[d_]
            PW, PO, PSt = CH[:, 0:128], CH[:, 128:256], CH[:, 256:320]
            rPW = rPO = rPSt = rCH
            order = (0, 1) if d_ == 0 else (1, 0)
            for j in order:
                n = 2 * m + j
                tj = slice(64 * j, 64 * j + 64)
                v = vc[0] % 2
                vc[0] += 1
                for hh in range(2):
                    b_ = slice(64 * hh, 64 * hh + 64)
                    MM(g, PW[:, hh * 64:(hh + 1) * 64], WT[pb][:, hh, :], SBs[:, :], True, True, [rWT[pb], rSBs], [rPW])
                TT_(g, "dve", VNEW[v][tj, :], U[pb][tj, :], PW[tj, :], ALU.subtract, [rU[pb], rPW], [rVNEW[v]])
                for hh in range(2):
                    b_ = slice(64 * hh, 64 * hh + 64)
                    MM(g, PO[:, hh * 64:(hh + 1) * 64], QGT[pb][:, hh, :], SBs[:, :], True, False, [rQGT[pb], rSBs], [rPO])
                    MM(g, PO[:, hh * 64:(hh + 1) * 64], QKT[pb][tj, hh, :], VNEW[v][tj, hh * 64:(hh + 1) * 64], False, True,
                       [rQKT[pb], rVNEW[v]], [rPO])
                if not second:
                    CP(g, "act", OF[tj, m, :], PO[tj, :], [rPO], [rOF])
                else:
                    ob = oc[0] % 2
                    TT_(g, "dve", OT[ob][tj, :], PO[tj, :], OF[tj, m, :], ALU.add, [rPO, rOF], [rOT[ob]])
                for hh in range(2):
                    MM(g, PSt[:, :], KGZ[pb][tj, hh, :], VNEW[v][tj, hh * 64:(hh + 1) * 64], hh == 0, hh == 1,
                       [rKGZ[pb], rVNEW[v]], [rPSt])
                STT(g, SF[:], SF[:], GLB[:, d_, n:n + 1], PSt[:, :], ALU.mult, ALU.add, [rSF, rGLB, rPSt], [rSF])
                CP(g, "act", SBs[:], SF[:], [rSF], [rSBs])
            if second:
                ob = oc[0] % 2
                oc[0] += 1
                for hh in range(2):
                    ACT(g, JUNK[:], OT[ob][:, hh * 64:(hh + 1) * 64], AF.Square, [rOT[ob]], [rJ, rSSQ[ob]], accum_out=SSQ[ob][:, hh:hh + 1])
                ACT(g, SSQ[ob][:, 2:4], SSQ[ob][:, 0:2], AF.Sqrt, [rSSQ[ob]], [rSSQ[ob]], scale=1.0 / 64, bias=EPS)
                S.op("dve", lambda e, ob=ob: e.reciprocal(out=SSQ[ob][:, 2:4], in_=SSQ[ob][:, 2:4]), [rSSQ[ob]], [rSSQ[ob]])
                for hh in range(2):
                    TS_(g, "dve", ON[ob][:, hh * 64:(hh + 1) * 64], OT[ob][:, hh * 64:(hh + 1) * 64], SSQ[ob][:, 2 + hh:3 + hh], None,
                        ALU.mult, None, [rOT[ob], rSSQ[ob]], [rON[ob]])
                S.begin_atomic()
                MM(g, PX[:, 384:512], ON[ob][:], IDB[:], True, True, [rON[ob], rI], [rPX])
                ts_ = slice(m * 128, (m + 1) * 128)
                STT(g, YT[:, ts_], PX[:, 384:512], NORMO[:, 0:1], SZ[:, ts_], ALU.mult, ALU.mult, [rPX, rC, rSZ], [rYT])
                S.end_atomic()

        for si, (grp, sidx, T) in enumerate(cfg.seqs):
            c0 = cfg.offs[si]
            ntl = T // 128
            gate_prep(si, T, c0)
            if cfg.gdn_stage <= 1:
                continue
            for hp in range(4):
                qkv_prep(si, T, c0, hp)
                if cfg.gdn_stage <= 2:
                    continue
                for d_ in range(2):
                    S.op("dve", lambda e, d_=d_: e.memset(SF_A[d_][:], 0.0), [], [rSF_A[d_]])
                    S.op("pool", lambda e, d_=d_: e.memset(SBs_A[d_][:], 0.0), [], [rSBs_A[d_]])
                tl = [list(range(ntl)), list(range(ntl - 1, -1, -1))]
                seen = set()

                def cap(fn, *a):
                    S.capture()
                    fn(*a)
                    return S.end_capture()

                S.extend(interleave(cap(pre, tl[0][0], 0, hp, 0), cap(pre, tl[1][0], 1, hp, 2)))
                for ix in range(ntl):
                    lists = []
                    for d_ in range(2):
                        m = tl[d_][ix]
                        if cfg.gdn_stage >= 4:
                            lists.append(cap(chain, m, d_, hp, 2 * d_ + ix % 2, T, m in seen))
                        seen.add(m)
                    if ix + 1 < ntl:
                        for d_ in range(2):
                            lists.append(cap(pre, tl[d_][ix + 1], d_, hp, 2 * d_ + (ix + 1) % 2))
                    S.extend(interleave(*lists))
                DMA(g, "sp", W["YMIX"][512 + hp * 128:512 + (hp + 1) * 128, c0:c0 + T], YT[:, 0:T], "C_y", [rYT], [rdram])
        S.barrier()


def phaseM(g, l):
    pass


def small_vec_load(g, q, dst, src, semkey, writes):
    return g.S.dma(q, lambda e: e.dma_start(out=dst, in_=src, allow_slow_non_contiguous=True), semkey, [], writes)


def phaseD(g, l):
    nc, S, cfg = g.nc, g.S, g.cfg
    NS = cfg.NS
    S.barrier()
    with ExitStack() as es:
        KMT = g.sbt(es, "D_kmt", [128, 8, NS, NMEM], BF16)
        VM = g.sbt(es, "D_vm", [128, NS, 2, D], BF16)
        rKMT, rVM = S.res(), S.res()
        WOUT = g.sbt(es, "D_wout", [128, KC, D], BF16)
        WXQ = g.sbt(es, "D_wxq", [128, KC, D], BF16)
        WXO = g.sbt(es, "D_wxo", [128, KC, D], BF16)
        rWD = S.res()
        for nm, Wt in (("w_out", WOUT), ("w_xq", WXQ), ("w_xo", WXO)):
            wsrc = g.I[nm][l].rearrange("(kc p) n -> p kc n", p=128)
            for kc in range(KC):
                DMA(g, "pool", Wt[:, kc, :], wsrc[:, kc, :], "D_w%d" % (kc % 4), [], [rWD])
        with ExitStack() as esm:
            WKV = g.sbt(esm, "M_wkv", [128, KC, 2 * D], BF16)
            rWKV = S.res()
            wsrc = g.I["w_xkv"][l].rearrange("(kc p) n -> p kc n", p=128)
            for kc in range(KC):
                DMA(g, "pool", WKV[:, kc, :], wsrc[:, kc, :], "M_w%d" % (kc % 4), [], [rWKV])
            MHF = [g.sbt(esm, "M_mhf%d" % i, [128, KC, NMEM], F32) for i in range(2)]
            MN = [g.sbt(esm, "M_mn%d" % i, [128, KC, NMEM], BF16) for i in range(2)]
            rMHF = [S.res() for _ in range(2)]
            rMN = [S.res() for _ in range(2)]
            PSm = [g.pst(esm, "M_ps%d" % i, [128, 512], F32) for i in range(4)]
            rPSm = [S.pres() for _ in range(4)]
            MHv = g.W["MH"].rearrange("(kc p) t -> p kc t", p=128)
            pc = 0
            for si in range(NS):
                b = si % 2
                DMA(g, "sp", MHF[b][:], MHv[:, :, si * NMEM:(si + 1) * NMEM], "M_m%d" % b, [], [rMHF[b]])
                for kc in range(KC):
                    TS_(g, "dve" if kc % 2 == 0 else "pool", MN[b][:, kc, :], MHF[b][:, kc, :], g.NMEMG[:, l, kc:kc + 1], None,
                        ALU.mult, None, [rMHF[b], g.rconst], [rMN[b]])
                for m in range(8):
                    p = pc % 4
                    pc += 1
                    for kc in range(KC):
                        MM(g, PSm[p][:, 0:NMEM], WKV[:, kc, m * 128:(m + 1) * 128], MN[b][:, kc, :], kc == 0, kc == KC - 1,
                           [rWKV, rMN[b]], [rPSm[p]])
                    CP(g, "act" if m % 2 == 0 else "dve", KMT[:, m, si, :], PSm[p][:, 0:NMEM], [rPSm[p]], [rKMT])
                for mt in range(2):
                    for nh in range(2):
                        p = pc % 4
                        pc += 1
                        for kc in range(KC):
                            MM(g, PSm[p][:], MN[b][:, kc, mt * 128:(mt + 1) * 128], WKV[:, kc, D + nh * 512:D + (nh + 1) * 512],
                               kc == 0, kc == KC - 1, [rWKV, rMN[b]], [rPSm[p]])
                        CP(g, "act" if nh == 0 else "dve", VM[:, si, mt, nh * 512:(nh + 1) * 512], PSm[p][:], [rPSm[p]], [rVM])
            S.barrier()
        YM = g.sbt(es, "D_ym", [128, KC, TT], BF16)
        XF_A = [g.sbt(es, "D_xf%d" % i, [128, KC, TT], F32) for i in range(2)]
        SQ = g.sbt(es, "D_sq", [128, KC, TT], BF16)
        RSTD = g.sbt(es, "D_rstd", [128, TT], F32)
        XN = g.sbt(es, "D_xn", [128, KC, TT], BF16)
        QX_A = [g.sbt(es, "D_qx%d" % i, [128, KC, TT], BF16) for i in range(2)]
        PT = [g.sbt(es, "D_pt%d" % i, [128, 2, TT], BF16) for i in range(2)]
        RCP = [g.sbt(es, "D_rcp%d" % i, [128, TT], F32) for i in range(2)]
        ATT = g.sbt(es, "D_att", [128, KC, TT], BF16)
        XN3 = g.sbt(es, "D_xn3", [128, KC, TT], BF16)
        rYM, rSQ, rRSTD, rXN, rATT, rXN3 = [S.res() for _ in range(6)]
        rXF_A = [S.res() for _ in range(2)]
        rQX_A = [S.res() for _ in range(2)]
        rPT = [S.res() for _ in range(2)]
        rRCP = [S.res() for _ in range(2)]
        PSs = g.pst(es, "D_pss", [128, TT], F32)
        rPSs = S.pres()
        PS2 = g.pst(es, "D_pssc", [128, 2 * TT], F32)
        rPS2 = S.pres()
        PSd = g.pst(es, "D_psd", [128, TT], F32)
        rPSd = S.pres()
        PS = [g.pst(es, "D_ps%d" % i, [128, TT], F32) for i in range(4)]
        rPS = [S.pres() for _ in range(4)]
        rdram = S.res()
        XTv = g.W["XT"].rearrange("(kc p) t -> p kc t", p=128)
        YMv = g.W["YMIX"].rearrange("(kc p) t -> p kc t", p=128)
        X3v = g.W["XN3"].rearrange("(kc p) t -> p kc t", p=128)
        pcs = [0]
        hcs = [0, 0]
        tiles = []
        for si, (grp, sidx, T) in enumerate(cfg.seqs):
            for t4 in range(T // TT):
                tiles.append((si, cfg.offs[si] + t4 * TT))

        def part1(ix):
            si, c0 = tiles[ix]
            XF, rXF, QX, rQX = XF_A[ix % 2], rXF_A[ix % 2], QX_A[ix % 2], rQX_A[ix % 2]
            DMA(g, "sp", YM[:], YMv[:, :, c0:c0 + TT], "D_ym", [], [rYM])
            DMA(g, "act", XF[:], XTv[:, :, c0:c0 + TT], "D_xf%d" % (ix % 2), [], [rXF])
            for m in range(KC):
                p = pcs[0] % 2
                pcs[0] += 1
                for kc in range(KC):
                    MM(g, PS[p][:], WOUT[:, kc, m * 128:(m + 1) * 128], YM[:, kc, :], kc == 0, kc == KC - 1, [rWD, rYM], [rPS[p]])
                TT_(g, "dve", XF[:, m, :], PS[p][:], XF[:, m, :], ALU.add, [rPS[p], rXF], [rXF])
            S.begin_atomic()
            rmsnorm_fm(g, XF, rXF, XN, rXN, SQ, rSQ, RSTD, rRSTD, PSs, rPSs, 3 * l + 1, TT)
            S.end_atomic()
            for m in range(KC):
                p = pcs[0] % 2
                pcs[0] += 1
                for kc in range(KC):
                    MM(g, PS[p][:], WXQ[:, kc, m * 128:(m + 1) * 128], XN[:, kc, :], kc == 0, kc == KC - 1, [rWD, rXN], [rPS[p]])
                ACT(g, QX[:, m, :], PS[p][:], AF.Copy, [rPS[p]], [rQX], scale=1.0 / 16.0)

        def part2(ix):
            si, c0 = tiles[ix]
            XF, rXF, QX, rQX = XF_A[ix % 2], rXF_A[ix % 2], QX_A[ix % 2], rQX_A[ix % 2]
            for hx in range(4):
                hb = hcs[0] % 2
                hcs[0] += 1
                for mt in range(2):
                    for c in range(2):
                        MM(g, PS2[:, mt * TT:(mt + 1) * TT], KMT[:, hx * 2 + c, si, mt * 128:(mt + 1) * 128], QX[:, hx * 2 + c, :],
                           c == 0, c == 1, [rKMT, rQX], [rPS2])
                ACT(g, PT[hb][:], PS2[:].rearrange("p (a b) -> p a b", a=2), AF.Exp, [rPS2], [rPT[hb]])
                for mt in range(2):
                    MM(g, PSd[:], g.ONESB[:], PT[hb][:, mt, :], mt == 0, mt == 1, [g.rconst, rPT[hb]], [rPSd])
                S.op("dve", lambda e, hb=hb: e.reciprocal(out=RCP[hb][:], in_=PSd[:]), [rPSd], [rRCP[hb]])
                for c in range(2):
                    p = 2 + hcs[1] % 2
                    hcs[1] += 1
                    for mt in range(2):
                        MM(g, PS[p][:], VM[:, si, mt, hx * 256 + c * 128:hx * 256 + (c + 1) * 128], PT[hb][:, mt, :],
                           mt == 0, mt == 1, [rVM, rPT[hb]], [rPS[p]])
                    TT_(g, "dve", ATT[:, hx * 2 + c, :], PS[p][:], RCP[hb][:], ALU.mult, [rPS[p], rRCP[hb]], [rATT])
            for m in range(KC):
                p = 2 + hcs[1] % 2
                hcs[1] += 1
                for kc in range(KC):
                    MM(g, PS[p][:], WXO[:, kc, m * 128:(m + 1) * 128], ATT[:, kc, :], kc == 0, kc == KC - 1, [rWD, rATT], [rPS[p]])
                TT_(g, "dve", XF[:, m, :], PS[p][:], XF[:, m, :], ALU.add, [rPS[p], rXF], [rXF])
            DMA(g, "sp", XTv[:, :, c0:c0 + TT], XF[:], "D_xs%d" % (ix % 2), [rXF], [rdram])
            S.begin_atomic()
            rmsnorm_fm(g, XF, rXF, XN3, rXN3, SQ, rSQ, RSTD, rRSTD, PSs, rPSs, 3 * l + 2, TT)
            S.end_atomic()
            DMA(g, "act", X3v[:, :, c0:c0 + TT], XN3[:], "D_x3", [rXN3], [rdram])

        def cap(fn, *a):
            S.capture()
            fn(*a)
            return S.end_capture()

        S.extend(cap(part1, 0))
        for ix in range(len(tiles)):
            lists = [cap(part2, ix)]
            if ix + 1 < len(tiles):
                lists.append(cap(part1, ix + 1))
            S.extend(interleave(*lists))
        S.barrier()


def phaseE(g, l):
    nc, S, cfg = g.nc, g.S, g.cfg
    S.barrier()
    NF = 2 * DFF // 128
    NP2 = DFF // 128
    with ExitStack() as es:
        WUP = g.sbt(es, "E_wup", [128, KC, 2 * DFF], BF16)
        WDN = g.sbt(es, "E_wdn", [128, NP2, D], BF16)
        rWU, rWDn = S.res(), S.res()
        wsrc = g.I["w_up"][l].rearrange("(kc p) n -> p kc n", p=128)
        for kc in range(KC):
            DMA(g, "pool", WUP[:, kc, :], wsrc[:, kc, :], "E_w%d" % (kc % 4), [], [rWU])
        wsrc = g.I["w_down"][l].rearrange("(k p) n -> p k n", p=128)
        for k in range(NP2):
            DMA(g, "pool", WDN[:, k, :], wsrc[:, k, :], "E_w%d" % (k % 4), [], [rWDn])
        CW = g.sbt(es, "E_cw", [128, NF, 4], F32)
        rCW = S.res()
        for k in range(3):
            small_vec_load(g, "sp", CW[:, :, k], g.I["conv_ffn"][l, k].rearrange("(f p) -> p f", p=128), "E_c", [rCW])
        small_vec_load(g, "sp", CW[:, :, 3], g.I["conv_ffn_b"][l].rearrange("(f p) -> p f", p=128), "E_c", [rCW])
        XH = [g.sbt(es, "E_xh%d" % i, [128, KC, TT + 2], BF16) for i in range(2)]
        rXH = [S.res() for _ in range(2)]
        ACTT = g.sbt(es, "E_act", [128, NP2, TT], BF16)
        rACTT = S.res()
        XF = g.sbt(es, "E_xf", [128, KC, TT], F32)
        rXF = S.res()
        T1 = [g.sbt(es, "E_t1%d" % i, [128, TT], F32) for i in range(2)]
        T2 = [g.sbt(es, "E_t2%d" % i, [128, TT], F32) for i in range(2)]
        T3 = [g.sbt(es, "E_t3%d" % i, [128, TT], F32) for i in range(2)]
        rT1 = [S.res() for _ in range(2)]
        rT2 = [S.res() for _ in range(2)]
        rT3 = [S.res() for _ in range(2)]
        PSH = g.pst(es, "E_psh", [128, 6 * 512], F32)
        rSLOT = [S.pres() for _ in range(3)]
        PSO = [g.pst(es, "E_pso%d" % i, [128, TT], F32) for i in range(2)]
        rPSO = [S.pres() for _ in range(2)]
        rdram = S.res()
        XTv = g.W["XT"].rearrange("(kc p) t -> p kc t", p=128)
        X3v = g.W["XN3"].rearrange("(kc p) t -> p kc t", p=128)
        W_ = TT + 2

        def slot_pieces(k):
            a = 1024 * k
            return [(a, a + 512, 0, 512), (a + 512, a + W_, 512, W_)]

        ti = 0
        pcount = 0
        oc = 0
        slotc = [0]
        etiles = []
        for si, (grp, sidx, T) in enumerate(cfg.seqs):
            for t4 in range(T // TT):
                etiles.append((si, t4, T // TT))

        def load_xh(ix):
            si, t4, nt4 = etiles[ix]
            c0 = cfg.offs[si] + t4 * TT
            b = ix % 2
            lo = 0 if t4 > 0 else 1
            hi = W_ if t4 < nt4 - 1 else W_ - 1
            if lo == 1:
                S.op("pool", lambda e, b=b: e.memset(XH[b][:, :, 0:1], 0.0), [], [rXH[b]])
            if hi == W_ - 1:
                S.op("pool", lambda e, b=b: e.memset(XH[b][:, :, W_ - 1:W_], 0.0), [], [rXH[b]])
            DMA(g, "sp", XH[b][:, :, lo:hi], X3v[:, :, c0 - 1 + lo:c0 - 1 + hi], "E_xh%d" % b, [], [rXH[b]])

        load_xh(0)
        for ix_, (si, t4, nt4) in enumerate(etiles):
            if True:
                c0 = cfg.offs[si] + t4 * TT
                b = ti % 2
                ti += 1
                if ix_ + 1 < len(etiles):
                    load_xh(ix_ + 1)
                DMA(g, "act", XF[:], XTv[:, :, c0:c0 + TT], "E_xf", [], [rXF])
                for i in range(NP2):
                    pr = pcount % 2
                    pcount += 1
                    for which, f in ((0, i), (1, NP2 + i)):
                        k = slotc[0] % 3
                        slotc[0] += 1
                        for kc in range(KC):
                            for (pa, pb, ra, rb) in slot_pieces(k):
                                MM(g, PSH[:, pa:pb], WUP[:, kc, f * 128:(f + 1) * 128], XH[b][:, kc, ra:rb], kc == 0, kc == KC - 1,
                                   [rWU, rXH[b]], [rSLOT[k]])
                        a0 = 1024 * k
                        Tt, rTt = (T1[pr], rT1[pr]) if which == 0 else (T2[pr], rT2[pr])
                        ACT(g, Tt[:], PSH[:, a0 + 1:a0 + 1 + TT], AF.Identity, [rSLOT[k], rCW], [rTt],
                            scale=CW[:, f, 1:2], bias=CW[:, f, 3:4])
                        STT(g, Tt[:], PSH[:, a0:a0 + TT], CW[:, f, 0:1], Tt[:], ALU.mult, ALU.add, [rSLOT[k], rCW, rTt], [rTt])
                        STT(g, Tt[:], PSH[:, a0 + 2:a0 + 2 + TT], CW[:, f, 2:3], Tt[:], ALU.mult, ALU.add, [rSLOT[k], rCW, rTt], [rTt])
                    ACT(g, T3[pr][:], T2[pr][:], AF.Silu, [rT2[pr]], [rT3[pr]])
                    TT_(g, "pool", ACTT[:, i, :], T3[pr][:], T1[pr][:], ALU.mult, [rT3[pr], rT1[pr]], [rACTT])
                for m in range(KC):
                    p = oc % 2
                    oc += 1
                    for k in range(NP2):
                        MM(g, PSO[p][:], WDN[:, k, m * 128:(m + 1) * 128], ACTT[:, k, :], k == 0, k == NP2 - 1, [rWDn, rACTT], [rPSO[p]])
                    TT_(g, "dve", XF[:, m, :], PSO[p][:], XF[:, m, :], ALU.add, [rPSO[p], rXF], [rXF])
                DMA(g, "sp", XTv[:, :, c0:c0 + TT], XF[:], "E_xs", [rXF], [rdram])
        S.barrier()


def phaseF(g):
    nc, S, cfg = g.nc, g.S, g.cfg
    L = cfg.depth
    S.barrier()
    with ExitStack() as es:
        XF = [g.sbt(es, "F_xf%d" % i, [128, KC, TT], F32) for i in range(2)]
        rXF = [S.res() for _ in range(2)]
        SQ = g.sbt(es, "F_sq", [128, KC, TT], BF16)
        RSTD = g.sbt(es, "F_rstd", [128, TT], F32)
        R1 = g.sbt(es, "F_r1", [128, KC, TT], F32)
        HI = g.sbt(es, "F_hi", [128, 3, KC, TT], BF16)
        OUT = [g.sbt(es, "F_out%d" % i, [128, 4, D], F32) for i in range(2)]
        rSQ, rRSTD, rR1, rHI = [S.res() for _ in range(4)]
        rOUT = [S.res() for _ in range(2)]
        PSs = g.pst(es, "F_pss", [128, TT], F32)
        rPSs = S.pres()
        PS = [g.pst(es, "F_ps%d" % i, [128, TT], F32) for i in range(4)]
        rPS = [S.pres() for _ in range(4)]
        rdram = S.res()
        XTv = g.W["XT"].rearrange("(kc p) t -> p kc t", p=128)
        ti = 0
        pc = 0
        for si, (grp, sidx, T) in enumerate(cfg.seqs):
            ydst = y_dst(g, si)
            for t4 in range(T // TT):
                c0 = cfg.offs[si] + t4 * TT
                b = ti % 2
                ti += 1
                DMA(g, "sp", XF[b][:], XTv[:, :, c0:c0 + TT], "F_x%d" % b, [], [rXF[b]])
                ACT(g, SQ[:], XF[b][:], AF.Square, [rXF[b]], [rSQ])
                for kc in range(KC):
                    MM(g, PSs[:], g.ONESB[:], SQ[:, kc, :], kc == 0, kc == KC - 1, [rSQ, g.rconst], [rPSs])
                ACT(g, RSTD[:], PSs[:], AF.Sqrt, [rPSs], [rRSTD], scale=1.0 / D, bias=EPS)
                S.op("dve", lambda e: e.reciprocal(out=RSTD[:], in_=RSTD[:]), [rRSTD], [rRSTD])
                for kc in range(KC):
                    STT(g, XF[b][:, kc, :], XF[b][:, kc, :], g.NORMS[:, 3 * L, kc:kc + 1], RSTD[:], ALU.mult, ALU.mult,
                        [rXF[b], rRSTD, g.rconst], [rXF[b]])
                CP(g, "act", HI[:, 0], XF[b][:], [rXF[b]], [rHI])
                TT_(g, "dve", R1[:], XF[b][:], HI[:, 0], ALU.subtract, [rXF[b], rHI], [rR1])
                CP(g, "act", HI[:, 1], R1[:], [rR1], [rHI])
                TT_(g, "pool", R1[:], R1[:], HI[:, 1], ALU.subtract, [rR1, rHI], [rR1])
                CP(g, "act", HI[:, 2], R1[:], [rR1], [rHI])
                for ts in range(4):
                    for half in range(2):
                        p = pc % 4
                        pc += 1
                        for q in range(4):
                            kc = half * 4 + q
                            for part in range(3):
                                MM(g, PS[p][:, q * 128:(q + 1) * 128], HI[:, part, kc, ts * 128:(ts + 1) * 128], g.IDB[:],
                                   part == 0, part == 2, [rHI, g.rconst], [rPS[p]])
                        CP(g, "act" if half == 0 else "dve", OUT[b][:, ts, half * 512:(half + 1) * 512], PS[p][:], [rPS[p]], [rOUT[b]])
                t0 = t4 * TT
                DMA(g, "act", ydst[t0:t0 + TT, :].rearrange("(s p) f -> p s f", p=128), OUT[b][:], "F_o%d" % b, [rOUT[b]], [rdram])
        S.barrier()


N_CORES = 8
_FULL = dict(n_p=4, T_p=2048, n_s=2, T_s=4096, depth=4)
SKIP_GDN = False


def _device_inputs(inputs, L):
    m = {}
    f = lambda a: np.ascontiguousarray(np.asarray(a, dtype=np.float32))
    for k in ("norm_mix", "w_in", "conv_qkv", "norm_o", "w_out", "norm_x", "norm_mem", "w_xq", "w_xkv", "w_xo",
              "norm_ffn", "w_up", "conv_ffn", "conv_ffn_b", "w_down", "norm_final"):
        m[k] = f(inputs[k])
    m["a_log"] = f(inputs["a_log"]).reshape(L, 16)
    m["dt_bias"] = f(inputs["dt_bias"]).reshape(L, 16)
    m["na_bias"] = _na_bias_table(f(inputs["rpb"])).reshape(L, 128, 8 * 9 * 128)
    m.update(_consts())
    return m


def kernel(**inputs):
    cfg = Cfg(_FULL["n_p"], _FULL["T_p"], _FULL["n_s"], _FULL["T_s"], _FULL["depth"], skip_gdn=SKIP_GDN)
    nc, g = build_program(cfg)
    shared = _device_inputs(inputs, cfg.depth)
    xp = np.asarray(inputs["x_prompt"], dtype=np.float32)
    xs = np.asarray(inputs["x_sample"], dtype=np.float32)
    mp = np.asarray(inputs["mem_prompt"], dtype=np.float32)
    ms = np.asarray(inputs["mem_sample"], dtype=np.float32)
    in_maps = []
    for c in range(N_CORES):
        m = dict(shared)
        m["x_p"] = np.ascontiguousarray(xp[cfg.n_p * c:cfg.n_p * (c + 1)])
        m["x_s"] = np.ascontiguousarray(xs[cfg.n_s * c:cfg.n_s * (c + 1)])
        m["m_p"] = np.ascontiguousarray(mp[cfg.n_p * c:cfg.n_p * (c + 1)])
        m["m_s"] = np.ascontiguousarray(ms[cfg.n_s * c:cfg.n_s * (c + 1)])
        in_maps.append(m)
    res = run_bass_kernel_spmd(nc, in_maps, core_ids=list(range(N_CORES)))
    y_p = np.concatenate([np.asarray(r["y_p"], dtype=np.float32) for r in res.results], axis=0)
    y_s = np.concatenate([np.asarray(r["y_s"], dtype=np.float32) for r in res.results], axis=0)
    return (y_p, y_s)
```

```python
import numpy as np
from contextlib import ExitStack

import concourse.bass as bass
import concourse.mybir as mybir
from concourse.bass_utils import run_bass_kernel_spmd

F32 = mybir.dt.float32
BF16 = mybir.dt.bfloat16
AF = mybir.ActivationFunctionType
ALU = mybir.AluOpType

D = 1024
KC = 8
DIN = 3616
DFF = 2816
NMEM = 256
NEG = -30000.0
EPS = 1e-6
TT = 512

SAME_ENGINE_SYNC = True
ATTACH_WAIT = True


class Res:
    __slots__ = ("name", "last_w", "readers", "excl")

    def __init__(self, name="", last_w=None, excl=False):
        self.name = name
        self.last_w = last_w
        self.readers = []
        self.excl = excl


class Op:
    __slots__ = ("eng", "fn", "reads", "writes", "dma", "semkey", "val", "needs_inc", "idx", "barrier", "mm_stop", "atom")


class Sched:
    def __init__(self, nc):
        self.nc = nc
        self.ops = []
        self.eng_obj = {"pe": nc.tensor, "act": nc.scalar, "dve": nc.vector, "pool": nc.gpsimd, "sp": nc.sync}
        self.last_barrier = None
        self.all_res = []
        self.cap = None
        self.atom = None
        self.atom_ctr = 0

    def begin_atomic(self):
        self.atom_ctr += 1
        self.atom = self.atom_ctr

    def end_atomic(self):
        self.atom = None

    def res(self, name=""):
        r = Res(name, self.last_barrier)
        self.all_res.append(r)
        return r

    def pres(self, name=""):
        r = Res(name, self.last_barrier, excl=True)
        self.all_res.append(r)
        return r

    def _new(self, eng, fn, reads, writes):
        o = Op()
        ex = [r for r in reads if r.excl]
        if ex:
            reads = [r for r in reads if not r.excl]
            writes = list(writes) + [r for r in ex if r not in writes]
        o.eng, o.fn, o.reads, o.writes = eng, fn, tuple(reads), tuple(writes)
        o.dma, o.semkey, o.val, o.needs_inc, o.barrier = False, None, None, False, False
        o.mm_stop = None
        o.atom = self.atom
        if self.cap is not None:
            self.cap.append(o)
        else:
            self.ops.append(o)
        return o

    def op(self, eng, fn, reads=(), writes=()):
        return self._new(eng, fn, reads, writes)

    def dma(self, queue, fn, semkey, reads=(), writes=()):
        o = self._new(queue, fn, reads, writes)
        o.dma, o.semkey = True, semkey
        return o

    def capture(self):
        assert self.cap is None
        self.cap = []

    def end_capture(self):
        c, self.cap = self.cap, None
        return c

    def extend(self, ops):
        assert self.cap is None
        self.ops.extend(ops)

    def barrier(self):
        o = self._new("sp", lambda e: e.nop(), [], [])
        o.barrier = True
        return o

    def emit(self):
        nc = self.nc
        ops = self.ops
        n = len(ops)
        for i, o in enumerate(ops):
            o.idx = i
        deps_all = [None] * n
        last_dma_on_sem = {}
        last_on_eng = {}
        outstanding_dma = set()
        cur_barrier = None
        for o in ops:
            deps = set()
            if o.barrier:
                for e, d in last_on_eng.items():
                    deps.add(d)
                deps.update(outstanding_dma)
                outstanding_dma = set()
                cur_barrier = o.idx
            else:
                for r in o.reads:
                    lw = r.last_w
                    if lw is not None:
                        deps.add(lw.idx if isinstance(lw, Op) else lw)
                for w in o.writes:
                    lw = w.last_w
                    if lw is not None:
                        deps.add(lw.idx if isinstance(lw, Op) else lw)
                    deps.update(w.readers)
                if o.dma:
                    p = last_dma_on_sem.get(o.semkey)
                    if p is not None:
                        deps.add(p)
                    last_dma_on_sem[o.semkey] = o.idx
                    outstanding_dma.add(o.idx)
                if cur_barrier is not None:
                    deps.add(cur_barrier)
            deps.discard(o.idx)
            deps_all[o.idx] = deps
            for r in o.reads:
                r.readers.append(o.idx)
            for w in o.writes:
                w.last_w = o.idx
                w.readers = []
            last_on_eng[o.eng] = o.idx
        known = {e: {} for e in self.eng_obj}
        waits = [None] * n
        for o in ops:
            best = {}
            dmadeps = []
            for d in deps_all[o.idx]:
                p = ops[d]
                if p.dma:
                    dmadeps.append(d)
                else:
                    if p.eng == o.eng and not o.dma:
                        if p.eng in ("pe", "sp") or not SAME_ENGINE_SYNC:
                            continue
                    if p.eng not in best or best[p.eng] < d:
                        best[p.eng] = d
            w = []
            for e, d in best.items():
                kn = known[o.eng].get(("c", e), -1)
                if d > kn:
                    known[o.eng][("c", e)] = d
                    w.append(d)
                    ops[d].needs_inc = True
            for d in dmadeps:
                key = ("d", ops[d].semkey)
                kn = known[o.eng].get(key, -1)
                if d > kn:
                    known[o.eng][key] = d
                    w.append(d)
            waits[o.idx] = w
        sems = {}
        self._stack = []

        def get_sem(key):
            if key not in sems:
                cm = nc.semaphore("s%d" % len(sems))
                s = cm.__enter__()
                self._stack.append(cm)
                sems[key] = [s, 0]
            return sems[key]

        self.n_inst = {e: 0 for e in self.eng_obj}
        for o in ops:
            eng = self.eng_obj[o.eng]
            wl = waits[o.idx]
            attach = None
            if ATTACH_WAIT and wl and not o.barrier:
                attach = wl[-1]
                wl = wl[:-1]
            for d in wl:
                p = ops[d]
                key = ("dma", p.semkey) if p.dma else ("eng", p.eng)
                s = get_sem(key)[0]
                assert p.val is not None
                eng.wait_ge(s, p.val)
                self.n_inst[o.eng] += 1
            inst = o.fn(eng)
            if attach is not None:
                p = ops[attach]
                key = ("dma", p.semkey) if p.dma else ("eng", p.eng)
                inst.wait_op(get_sem(key)[0], p.val, "sem-ge")
            self.n_inst[o.eng] += 1
            if o.dma:
                sv = get_sem(("dma", o.semkey))
                sv[1] += 16
                inst.then_inc(sv[0], 16)
                o.val = sv[1]
            elif o.needs_inc:
                sv = get_sem(("eng", o.eng))
                sv[1] += 1
                inst.then_inc(sv[0], 1)
                o.val = sv[1]
        feng = self.eng_obj["sp"]
        for key, (s, v) in sems.items():
            if key[0] == "dma" and v > 0:
                feng.wait_ge(s, v)
        self.n_sems = len(sems)

    def close(self):
        for cm in reversed(self._stack):
            cm.__exit__(None, None, None)


def interleave(*lists):
    out = []
    its = [list(l) for l in lists]
    pos = [0] * len(its)
    tot = sum(len(l) for l in its)
    while len(out) < tot:
        best, bi = None, -1
        for i, l in enumerate(its):
            if pos[i] < len(l):
                frac = pos[i] / len(l)
                if best is None or frac < best:
                    best, bi = frac, i
        out.append(its[bi][pos[bi]])
        pos[bi] += 1
        if out[-1].atom is not None:
            a = out[-1].atom
            while pos[bi] < len(its[bi]) and its[bi][pos[bi]].atom == a:
                out.append(its[bi][pos[bi]])
                pos[bi] += 1
        elif out[-1].mm_stop is False:
            while pos[bi] < len(its[bi]):
                o = its[bi][pos[bi]]
                out.append(o)
                pos[bi] += 1
                if o.mm_stop is True:
                    break
    return out


def _na_pairs(T):
    rows = T // 64
    kh = 8
    npair = rows // 2
    out = []
    for i in range(npair):
        lst = []
        for j in range(npair):
            V = np.zeros((2, 2), bool)
            for b in range(2):
                r = 2 * i + b
                rs = min(max(r - kh // 2, 0), rows - kh)
                for a in range(2):
                    kr = 2 * j + a
                    V[a, b] = (rs <= kr < rs + kh)
            if not V.any():
                continue
            delta = j - i
            if V.all():
                assert -3 <= delta <= 3
                cls = delta + 3
            elif delta == -2 and V[0, 0] and V[1, 0] and V[1, 1] and not V[0, 1]:
                cls = 7
            elif delta == 2 and V[0, 1] and not V[0, 0] and not V[1, 0] and not V[1, 1]:
                cls = 8
            else:
                raise AssertionError("unexpected NA window pattern %s %d %d" % (V, i, j))
            lst.append((j, cls))
        out.append(lst)
    return out


def _na_bias_table(rpb):
    L = rpb.shape[0]
    a = np.arange(2)[:, None, None, None]
    kc = np.arange(64)[None, :, None, None]
    b = np.arange(2)[None, None, :, None]
    qc = np.arange(64)[None, None, None, :]
    cs = np.clip(qc - 8, 0, 48)
    colok = (kc >= cs) & (kc < cs + 16)
    coff = np.clip(kc - qc + 15, 0, 30)
    tab = np.full((L, 8, 9, 2, 64, 2, 64), NEG, np.float32)
    for cls in range(9):
        if cls < 7:
            delta = cls - 3
            rowok = np.ones((2, 1, 2, 1), bool)
        elif cls == 7:
            delta = -2
            rowok = ~((a == 0) & (b == 1))
        else:
            delta = 2
            rowok = (a == 0) & (b == 1)
        dr = 2 * delta + a - b + 7
        ok = np.broadcast_to(colok & rowok & (dr >= 0) & (dr <= 14), (2, 64, 2, 64))
        drc = np.broadcast_to(np.clip(dr, 0, 14), (2, 64, 2, 64))
        cof = np.broadcast_to(coff, (2, 64, 2, 64))
        g = rpb[:, :, drc, cof]
        tab[:, :, cls] = np.where(ok[None, None], g, np.float32(NEG))
    tab = tab.reshape(L, 8, 9, 128, 128)
    return np.ascontiguousarray(tab.transpose(0, 3, 1, 2, 4))


def _gdn_masks():
    p = np.arange(128)
    same = (p[:, None] // 64) == (p[None, :] // 64)
    r, c = p[:, None], p[None, :]
    m = np.zeros((128, 2, 3, 128), np.float32)
    m[:, 0, 0] = np.where(same & (r < c), 0.0, NEG)
    m[:, 0, 1] = np.where(same & (r <= c), 0.0, NEG)
    m[:, 0, 2] = np.where(same & (c < r), 0.0, -NEG)
    m[:, 1, 0] = np.where(same & (r > c), 0.0, NEG)
    m[:, 1, 1] = np.where(same & (r >= c), 0.0, NEG)
    m[:, 1, 2] = np.where(same & (c > r), 0.0, -NEG)
    return m


def _consts():
    c = {}
    c["c_ident"] = np.eye(128, dtype=np.float32)
    p = np.arange(128)
    c["c_onesbd"] = ((p[:, None] // 64) == (p[None, :] // 64)).astype(np.float32)
    c["c_gmask"] = _gdn_masks().reshape(128, 6 * 128)
    sel = np.zeros((48, 16, 128), np.float32)
    for i in range(16):
        sel[i, i, :] = 1.0
        sel[32 + i, i, :] = 1.0
    c["c_sel"] = sel.reshape(48, 16 * 128)
    selr = np.zeros((128, 16, 128), np.float32)
    for d_ in range(2):
        for h in range(8):
            r = 32 * d_ + h
            selr[r, d_ * 8 + h, :] = 1.0
            selr[64 + r, d_ * 8 + h, :] = 1.0
    c["c_selr"] = selr.reshape(128, 16 * 128)
    selp = np.zeros((128, 8, 128), np.float32)
    for d_ in range(2):
        for hp in range(4):
            for hh in range(2):
                r = 32 * d_ + 2 * hp + hh
                selp[r, d_ * 4 + hp, 64 * hh:64 * hh + 64] = 1.0
                selp[64 + r, d_ * 4 + hp, 64 * hh:64 * hh + 64] = 1.0
    c["c_selp"] = selp.reshape(128, 8 * 128)
    rst = np.ones((128, 512), np.float32)
    rst[:, ::64] = 0.0
    c["c_reset"] = rst
    return c


class Cfg:
    def __init__(self, n_p, T_p, n_s, T_s, depth, taps=(), stop_after=None, skip_gdn=False, gdn_stage=9):
        self.gdn_stage = gdn_stage
        self.n_p, self.T_p, self.n_s, self.T_s, self.depth = n_p, T_p, n_s, T_s, depth
        self.seqs = [("p", i, T_p) for i in range(n_p)] + [("s", i, T_s) for i in range(n_s)]
        self.offs = []
        o = 0
        for (_, _, T) in self.seqs:
            self.offs.append(o)
            o += T
        self.NT = o
        self.NS = len(self.seqs)
        self.taps = tuple(taps)
        self.stop_after = stop_after
        self.skip_gdn = skip_gdn


class B:
    pass


def build_program(cfg):
    nc = bass.Bass("TRN2", target_bir_lowering=False)
    L = cfg.depth
    NT, NS = cfg.NT, cfg.NS
    g = B()
    g.nc, g.cfg = nc, cfg

    def din(name, shape, dt=F32):
        return nc.dram_tensor(name, list(shape), dt, kind="ExternalInput").ap()

    def dscr(name, shape, dt):
        kind = "ExternalOutput" if name in cfg.taps else "Internal"
        return nc.dram_tensor(name, list(shape), dt, kind=kind).ap()

    I = {}
    I["x_p"] = din("x_p", [max(cfg.n_p, 1), cfg.T_p, D])
    I["x_s"] = din("x_s", [max(cfg.n_s, 1), cfg.T_s, D])
    I["m_p"] = din("m_p", [max(cfg.n_p, 1), NMEM, D])
    I["m_s"] = din("m_s", [max(cfg.n_s, 1), NMEM, D])
    for nm, shp in [("norm_mix", [L, D]), ("w_in", [L, D, DIN]), ("conv_qkv", [L, 3, 1536]), ("a_log", [L, 16]),
                    ("dt_bias", [L, 16]), ("norm_o", [L, 64]), ("w_out", [L, D, D]), ("norm_x", [L, D]),
                    ("norm_mem", [L, D]), ("w_xq", [L, D, D]), ("w_xkv", [L, D, 2 * D]), ("w_xo", [L, D, D]),
                    ("norm_ffn", [L, D]), ("w_up", [L, D, 2 * DFF]), ("conv_ffn", [L, 3, 2 * DFF]),
                    ("conv_ffn_b", [L, 2 * DFF]), ("w_down", [L, DFF, D]), ("norm_final", [D]),
                    ("na_bias", [L, 128, 8 * 9 * 128]), ("c_ident", [128, 128]), ("c_onesbd", [128, 128]),
                    ("c_gmask", [128, 768]), ("c_sel", [48, 2048]), ("c_selr", [128, 2048]),
                    ("c_selp", [128, 1024]), ("c_reset", [128, 512])]:
        I[nm] = din(nm, shp)
    O = {}
    O["y_p"] = nc.dram_tensor("y_p", [max(cfg.n_p, 1), cfg.T_p, D], F32, kind="ExternalOutput").ap()
    O["y_s"] = nc.dram_tensor("y_s", [max(cfg.n_s, 1), cfg.T_s, D], F32, kind="ExternalOutput").ap()
    W = {}
    W["XT"] = dscr("XT", [D, NT], F32)
    W["MH"] = dscr("MH", [D, NS * NMEM], F32)
    W["QKN"] = dscr("QKN", [1024, NT], BF16)
    W["VN"] = dscr("VN", [NT, 512], BF16)
    W["QKVD"] = dscr("QKVD", [1536, NT], BF16)
    W["ZT"] = dscr("ZT", [512, NT], BF16)
    W["GR"] = dscr("GR", [32, NT], F32)
    W["YMIX"] = dscr("YMIX", [1024, NT], BF16)
    W["XN3"] = dscr("XN3", [1024, NT], BF16)
    g.I, g.O, g.W = I, O, W

    S = Sched(nc)
    g.S = S
    with ExitStack() as top:
        uniq = [0]

        def sbt(es, name, shape, dt=F32):
            uniq[0] += 1
            return es.enter_context(nc.sbuf_tensor("%s_%d" % (name, uniq[0]), list(shape), dt))

        def pst(es, name, shape, dt=F32):
            uniq[0] += 1
            return es.enter_context(nc.psum_tensor("%s_%d" % (name, uniq[0]), list(shape), dt))

        g.sbt, g.pst = sbt, pst
        g.IDB = sbt(top, "IDB", [128, 128], BF16)
        g.ONESB = sbt(top, "ONESB", [128, 128], BF16)
        g.ONESBD = sbt(top, "ONESBD", [128, 128], BF16)
        g.NORMS = sbt(top, "NORMS", [128, 3 * L + 1, KC], F32)
        g.NMEMG = sbt(top, "NMEMG", [128, L, KC], F32)
        g.rconst = S.res("const")
        S.dma("pool", lambda e: e.dma_start(out=g.IDB[:], in_=I["c_ident"]), "c0", writes=[g.rconst])
        S.dma("pool", lambda e: e.dma_start(out=g.ONESBD[:], in_=I["c_onesbd"]), "c1", writes=[g.rconst])
        S.op("dve", lambda e: e.memset(g.ONESB[:], 1.0), writes=[g.rconst])
        with nc.allow_non_contiguous_dma("tiny per-partition gain vectors"):
            k = 0
            for l in range(L):
                for j, nm in enumerate(("norm_mix", "norm_x", "norm_ffn")):
                    src = I[nm][l].rearrange("(kc p) -> p kc", p=128)
                    S.dma("sp", lambda e, src=src, idx=3 * l + j: e.dma_start(out=g.NORMS[:, idx, :], in_=src, allow_slow_non_contiguous=True),
                          "c2", writes=[g.rconst])
                src = I["norm_mem"][l].rearrange("(kc p) -> p kc", p=128)
                S.dma("sp", lambda e, src=src, l=l: e.dma_start(out=g.NMEMG[:, l, :], in_=src, allow_slow_non_contiguous=True), "c2", writes=[g.rconst])
            src = I["norm_final"].rearrange("(kc p) -> p kc", p=128)
            S.dma("sp", lambda e, src=src: e.dma_start(out=g.NORMS[:, 3 * L, :], in_=src, allow_slow_non_contiguous=True), "c2", writes=[g.rconst])

        phase0(g)
        if cfg.stop_after != "0":
            for l in range(L):
                phaseA(g, l)
                if cfg.stop_after == "A":
                    break
                phaseB(g, l)
                if cfg.stop_after == "B":
                    break
                phaseC(g, l)
                if cfg.stop_after == "C":
                    break
                phaseM(g, l)
                phaseD(g, l)
                if cfg.stop_after == "D":
                    break
                phaseE(g, l)
            if cfg.stop_after is None:
                phaseF(g)
        S.barrier()
        S.emit()
        S.close()
    g.n_inst = S.n_inst
    return nc, g


def MM(g, out, lhsT, rhs, start, stop, reads, writes):
    o = g.S.op("pe", lambda e: e.matmul(out, lhsT, rhs, start=start, stop=stop), reads, writes)
    o.mm_stop = bool(stop)
    return o


def ACT(g, out, in_, func, reads, writes, **kw):
    return g.S.op("act", lambda e: e.activation(out=out, in_=in_, func=func, **kw), reads, writes)


def TT_(g, eng, out, in0, in1, op, reads, writes):
    return g.S.op(eng, lambda e: e.tensor_tensor(out=out, in0=in0, in1=in1, op=op), reads, writes)


def TS_(g, eng, out, in0, s1, s2, op0, op1, reads, writes, **kw):
    if op1 is None and eng == "pool" and op0 == ALU.mult:
        op1, s2 = ALU.add, 0.0
    if op1 is None:
        return g.S.op(eng, lambda e: e.tensor_scalar(out=out, in0=in0, scalar1=s1, scalar2=None, op0=op0, **kw), reads, writes)
    return g.S.op(eng, lambda e: e.tensor_scalar(out=out, in0=in0, scalar1=s1, scalar2=s2, op0=op0, op1=op1, **kw), reads, writes)


def STT(g, out, in0, scalar, in1, op0, op1, reads, writes):
    return g.S.op("dve", lambda e: e.scalar_tensor_tensor(out=out, in0=in0, scalar=scalar, in1=in1, op0=op0, op1=op1), reads, writes)


def CP(g, eng, out, in_, reads, writes):
    if eng == "act":
        return g.S.op("act", lambda e: e.activation(out=out, in_=in_, func=AF.Copy), reads, writes)
    return g.S.op(eng, lambda e: e.tensor_copy(out=out, in_=in_), reads, writes)


def DMA(g, q, out, in_, semkey, reads, writes):
    return g.S.dma(q, lambda e: e.dma_start(out=out, in_=in_), semkey, reads, writes)


def x_src(g, si):
    grp, i, T = g.cfg.seqs[si]
    return (g.I["x_p"] if grp == "p" else g.I["x_s"])[i]


def y_dst(g, si):
    grp, i, T = g.cfg.seqs[si]
    return (g.O["y_p"] if grp == "p" else g.O["y_s"])[i]


def m_src(g, si):
    grp, i, T = g.cfg.seqs[si]
    return (g.I["m_p"] if grp == "p" else g.I["m_s"])[i]


def phase0(g):
    nc, S, cfg = g.nc, g.S, g.cfg
    S.barrier()
    with ExitStack() as es:
        NB = 2
        XL = [g.sbt(es, "p0_xl%d" % i, [128, D], F32) for i in range(NB)]
        HI = [g.sbt(es, "p0_hi%d" % i, [128, 3, D], BF16) for i in range(NB)]
        R1 = [g.sbt(es, "p0_r%d" % i, [128, D], F32) for i in range(NB)]
        ST = [g.sbt(es, "p0_st%d" % i, [128, KC, 512], F32) for i in range(2)]
        SS = g.sbt(es, "p0_ss", [128, 4], F32)
        PS = [g.pst(es, "p0_ps%d" % i, [128, 512], F32) for i in range(4)]
        rXL = [S.res() for _ in range(NB)]
        rHI = [S.res() for _ in range(NB)]
        rR1 = [S.res() for _ in range(NB)]
        rST = [S.res() for _ in range(2)]
        rSS = S.res()
        rPS = [S.pres() for _ in range(4)]
        rdram = S.res()
        cnt = [0]

        def do_tile(src_rows, dst, dst_col0, slot_in_group, stg, is_mem):
            b = cnt[0] % NB
            cnt[0] += 1
            DMA(g, "sp", XL[b][:], src_rows, "p0l%d" % b, [], [rXL[b]])
            xin = XL[b]
            if is_mem:
                ACT(g, R1[b][:], XL[b][:], AF.Square, [rXL[b]], [rR1[b], rSS], accum_out=SS[:, 0:1])
                ACT(g, SS[:, 1:2], SS[:, 0:1], AF.Sqrt, [rSS], [rSS], scale=1.0 / D, bias=EPS)
                S.op("dve", lambda e: e.reciprocal(out=SS[:, 2:3], in_=SS[:, 1:2]), [rSS], [rSS])
                TS_(g, "dve", XL[b][:], XL[b][:], SS[:, 2:3], None, ALU.mult, None, [rXL[b], rSS], [rXL[b]])
            CP(g, "act", HI[b][:, 0, :], xin[:], [rXL[b]], [rHI[b]])
            TT_(g, "dve", R1[b][:], xin[:], HI[b][:, 0, :], ALU.subtract, [rXL[b], rHI[b]], [rR1[b]])
            CP(g, "act", HI[b][:, 1, :], R1[b][:], [rR1[b]], [rHI[b]])
            TT_(g, "dve", R1[b][:], R1[b][:], HI[b][:, 1, :], ALU.subtract, [rR1[b], rHI[b]], [rR1[b]])
            CP(g, "act", HI[b][:, 2, :], R1[b][:], [rR1[b]], [rHI[b]])
            for half in range(2):
                pi = (2 * cnt[0] + half) % 4
                for q in range(4):
                    fc = half * 4 + q
                    for part in range(3):
                        MM(g, PS[pi][:, q * 128:(q + 1) * 128], HI[b][:, part, fc * 128:(fc + 1) * 128], g.IDB[:],
                           part == 0, part == 2, [rHI[b], g.rconst], [rPS[pi]])
                outv = ST[stg][:, half * 4:(half + 1) * 4, slot_in_group * 128:(slot_in_group + 1) * 128]
                inv = PS[pi][:].rearrange("p (q t) -> p q t", q=4)
                if half == 0:
                    CP(g, "act", outv, inv, [rPS[pi]], [rST[stg]])
                else:
                    CP(g, "dve", outv, inv, [rPS[pi]], [rST[stg]])

        grp_i = 0
        for si, (grp, i, T) in enumerate(cfg.seqs):
            xs = x_src(g, si)
            for t4 in range(T // 512):
                stg = grp_i % 2
                grp_i += 1
                for s in range(4):
                    r0 = t4 * 512 + s * 128
                    do_tile(xs[r0:r0 + 128, :], None, None, s, stg, False)
                col0 = cfg.offs[si] + t4 * 512
                dst = g.W["XT"].rearrange("(kc p) t -> p kc t", p=128)[:, :, col0:col0 + 512]
                DMA(g, "act", dst, ST[stg][:], "p0s%d" % stg, [rST[stg]], [rdram])
        for si in range(cfg.NS):
            ms = m_src(g, si)
            stg = grp_i % 2
            grp_i += 1
            for s in range(2):
                do_tile(ms[s * 128:(s + 1) * 128, :], None, None, s, stg, True)
            col0 = si * NMEM
            dst = g.W["MH"].rearrange("(kc p) t -> p kc t", p=128)[:, :, col0:col0 + 256]
            DMA(g, "act", dst, ST[stg][:, :, 0:256], "p0s%d" % stg, [rST[stg]], [rdram])
        S.barrier()


def rmsnorm_fm(g, XF, rXF, XN, rXN, SQ, rSQ, RSTD, rRSTD, PSs, rPSs, gain_idx, ncols, col0=0, engs=("dve", "pool")):
    ACT(g, SQ[:, :, 0:ncols], XF[:, :, 0:ncols], AF.Square, [rXF], [rSQ])
    for kc in range(KC):
        MM(g, PSs[:, 0:ncols], g.ONESB[:], SQ[:, kc, 0:ncols], kc == 0, kc == KC - 1, [rSQ, g.rconst], [rPSs])
    ACT(g, RSTD[:, 0:ncols], PSs[:, 0:ncols], AF.Ln, [rPSs], [rRSTD], scale=1.0 / D, bias=EPS)
    ACT(g, RSTD[:, 0:ncols], RSTD[:, 0:ncols], AF.Exp, [rRSTD], [rRSTD], scale=-0.5)
    for kc in range(KC):
        STT(g, XN[:, kc, col0:col0 + ncols], XF[:, kc, 0:ncols], g.NORMS[:, gain_idx, kc:kc + 1], RSTD[:, 0:ncols],
            ALU.mult, ALU.mult, [rXF, rRSTD, g.rconst], [rXN])


def phaseA(g, l):
    nc, S, cfg = g.nc, g.S, g.cfg
    S.barrier()
    with ExitStack() as es:
        WIN = g.sbt(es, "A_win", [128, KC, DIN], BF16)
        rW = S.res()
        wsrc = g.I["w_in"][l].rearrange("(kc p) n -> p kc n", p=128)
        for kc in range(KC):
            DMA(g, "pool", WIN[:, kc, :], wsrc[:, kc, :], "A_w%d" % (kc % 4), [], [rW])
        XF = [g.sbt(es, "A_xf%d" % i, [128, KC, TT], F32) for i in range(2)]
        rXF = [S.res() for _ in range(2)]
        SQ = g.sbt(es, "A_sq", [128, KC, TT], BF16)
        rSQ = S.res()
        RSTD = g.sbt(es, "A_rstd", [128, TT], F32)
        rRSTD = S.res()
        XN = [g.sbt(es, "A_xn%d" % i, [128, KC, TT], BF16) for i in range(2)]
        rXN = [S.res() for _ in range(2)]
        STG = [g.sbt(es, "A_stg%d" % i, [128, 4, TT], BF16) for i in range(3)]
        rSTG = [S.res() for _ in range(3)]
        GST = [g.sbt(es, "A_gst%d" % i, [32, TT], F32) for i in range(2)]
        rGST = [S.res() for _ in range(2)]
        PSs = g.pst(es, "A_pss", [128, TT], F32)
        rPSs = S.pres()
        PS = [g.pst(es, "A_ps%d" % i, [128, TT], F32) for i in range(6)]
        rPS = [S.pres() for _ in range(6)]
        rdram = S.res()
        ntile = cfg.NT // TT
        psi = [0]
        stgi = [0]
        XTv = g.W["XT"].rearrange("(kc p) t -> p kc t", p=128)
        def prepart(ti):
            c0 = ti * TT
            b = ti % 2
            DMA(g, "sp", XF[b][:], XTv[:, :, c0:c0 + TT], "A_x%d" % b, [], [rXF[b]])
            rmsnorm_fm(g, XF[b], rXF[b], XN[b], rXN[b], SQ, rSQ, RSTD, rRSTD, PSs, rPSs, 3 * l + 0, TT)

        def mainpart(ti):
            c0 = ti * TT
            b = ti % 2
            groups = [(g.W["QKN"], 0, 0, 4, 0.125), (g.W["QKN"], 512, 512, 4, None),
                      (g.W["QKVD"], 0, 1536, 4, None), (g.W["QKVD"], 512, 2048, 4, None), (g.W["QKVD"], 1024, 2560, 4, None),
                      (g.W["ZT"], 0, 3072, 4, None)]
            for (dst, r0, wc0, nm, scale) in groups:
                sg = stgi[0] % 3
                stgi[0] += 1
                for m in range(nm):
                    p = psi[0] % 6
                    psi[0] += 1
                    for kc in range(KC):
                        MM(g, PS[p][:], WIN[:, kc, wc0 + m * 128: wc0 + (m + 1) * 128], XN[b][:, kc, :], kc == 0, kc == KC - 1,
                           [rW, rXN[b]], [rPS[p]])
                    if scale is not None:
                        ACT(g, STG[sg][:, m, :], PS[p][:], AF.Copy, [rPS[p]], [rSTG[sg]], scale=scale)
                    elif m % 2 == 0:
                        CP(g, "act", STG[sg][:, m, :], PS[p][:], [rPS[p]], [rSTG[sg]])
                    else:
                        CP(g, "dve", STG[sg][:, m, :], PS[p][:], [rPS[p]], [rSTG[sg]])
                d = dst[r0:r0 + 512, c0:c0 + TT].rearrange("(m p) t -> p m t", p=128)
                DMA(g, "sp", d, STG[sg][:], "A_st%d" % sg, [rSTG[sg]], [rdram])
            p = psi[0] % 6
            psi[0] += 1
            for kc in range(KC):
                MM(g, PS[p][0:32, :], WIN[:, kc, 3584:3616], XN[b][:, kc, :], kc == 0, kc == KC - 1, [rW, rXN[b]], [rPS[p]])
            gb = ti % 2
            CP(g, "dve", GST[gb][:], PS[p][0:32, :], [rPS[p]], [rGST[gb]])
            DMA(g, "sp", g.W["GR"][:, c0:c0 + TT], GST[gb][:], "A_g%d" % gb, [rGST[gb]], [rdram])
            sg = stgi[0] % 3
            stgi[0] += 1
            for ts in range(4):
                p = psi[0] % 6
                psi[0] += 1
                for kc in range(KC):
                    MM(g, PS[p][:], XN[b][:, kc, ts * 128:(ts + 1) * 128], WIN[:, kc, 1024:1536], kc == 0, kc == KC - 1,
                       [rW, rXN[b]], [rPS[p]])
                CP(g, "act" if ts % 2 == 0 else "dve", STG[sg][:, ts, :], PS[p][:], [rPS[p]], [rSTG[sg]])
            d = g.W["VN"][c0:c0 + TT, :].rearrange("(s p) f -> p s f", p=128)
            DMA(g, "sp", d, STG[sg][:], "A_st%d" % sg, [rSTG[sg]], [rdram])

        def cap(fn, *a):
            S.capture()
            fn(*a)
            return S.end_capture()

        S.extend(cap(prepart, 0))
        for ti in range(ntile):
            lists = [cap(mainpart, ti)]
            if ti + 1 < ntile:
                lists.append(cap(prepart, ti + 1))
            S.extend(interleave(*lists))
        S.barrier()


def phaseB(g, l):
    nc, S, cfg = g.nc, g.S, g.cfg
    S.barrier()
    Tmax = max(T for (_, _, T) in cfg.seqs)
    with ExitStack() as es:
        BIAS = g.sbt(es, "B_bias", [128, 8, 9, 128], BF16)
        rBIAS = S.res()
        bsrc = g.I["na_bias"][l].rearrange("p (h c q) -> p h c q", h=8, c=9)
        for h in range(8):
            DMA(g, "pool", BIAS[:, h], bsrc[:, h], "B_b%d" % (h % 2), [], [rBIAS])
        NB = 2
        QT = [g.sbt(es, "B_qt%d" % i, [128, Tmax], BF16) for i in range(NB)]
        KT = [g.sbt(es, "B_kt%d" % i, [128, Tmax], BF16) for i in range(NB)]
        V = [g.sbt(es, "B_v%d" % i, [128, Tmax // 128, 2, 65], BF16) for i in range(NB)]
        YT = [g.sbt(es, "B_yt%d" % i, [128, Tmax], BF16) for i in range(NB)]
        rQT = [S.res() for _ in range(NB)]
        rKT = [S.res() for _ in range(NB)]
        rV = [S.res() for _ in range(NB)]
        rYT = [S.res() for _ in range(NB)]
        for i in range(NB):
            S.op("pool", lambda e, i=i: e.memset(V[i][:], 1.0), [], [rV[i]])
        NP_ = 2
        PT = [g.sbt(es, "B_pt%d" % i, [128, 5, 2, 128], BF16) for i in range(NP_)]
        rPT = [S.res() for _ in range(NP_)]
        QBD = [g.sbt(es, "B_qbd%d" % i, [128, 2, 128], BF16) for i in range(2)]
        rQBD = [S.res() for _ in range(2)]
        for i in range(2):
            S.op("pool", lambda e, i=i: e.memset(QBD[i][:], 0.0), [], [rQBD[i]])
        YTM = [g.sbt(es, "B_ytm%d" % i, [128, 128], BF16) for i in range(2)]
        rYTM = [S.res() for _ in range(2)]
        RC = [g.sbt(es, "B_rc%d" % i, [128, 2], F32) for i in range(2)]
        rRC = [S.res() for _ in range(2)]
        PSS = [g.pst(es, "B_pss%d" % i, [128, 1536], F32) for i in range(2)]
        rPSS = [S.pres() for _ in range(2)]
        PSO = g.pst(es, "B_pso", [128, 512], F32)
        rPSO = S.pres()
        PSTr = g.pst(es, "B_pstr", [128, 512], F32)
        rPSTr = S.pres()
        rdram = S.res()
        it = 0
        cnt = 0
        units = [(si_, hp_) for si_ in range(len(cfg.seqs)) for hp_ in range(4)]

        def load_unit(u):
            si_, hp_ = units[u]
            T_ = cfg.seqs[si_][2]
            c0_ = cfg.offs[si_]
            b_ = u % NB
            DMA(g, "sp", QT[b_][:, 0:T_], g.W["QKN"][hp_ * 128:(hp_ + 1) * 128, c0_:c0_ + T_], "B_q%d" % b_, [], [rQT[b_]])
            DMA(g, "sp", KT[b_][:, 0:T_], g.W["QKN"][512 + hp_ * 128:512 + (hp_ + 1) * 128, c0_:c0_ + T_], "B_k%d" % b_, [], [rKT[b_]])
            vsrc = g.W["VN"][c0_:c0_ + T_, hp_ * 128:(hp_ + 1) * 128].rearrange("(t p) (h e) -> p t h e", p=128, h=2)
            for hh_ in range(2):
                DMA(g, "pool", V[b_][:, 0:T_ // 128, hh_, 0:64], vsrc[:, :, hh_, :], "B_v%d%d" % (b_, hh_), [], [rV[b_]])

        load_unit(0)
        for si, (grp, sidx, T) in enumerate(cfg.seqs):
            pairs = _na_pairs(T)
            c0 = cfg.offs[si]
            ntl = T // 128
            for hp in range(4):
                b = it % NB
                it += 1
                if it < len(units):
                    load_unit(it)
                for i in range(ntl):
                    yb = i % 2
                    lst = pairs[i]
                    nt = len(lst)
                    sp_ = cnt % 2
                    pp = cnt % NP_
                    cnt += 1
                    qs = slice(i * 128, (i + 1) * 128)
                    for hh in range(2):
                        b_ = slice(64 * hh, 64 * hh + 64)
                        CP(g, "pool", QBD[sp_][b_, hh, :], QT[b][b_, qs], [rQT[b]], [rQBD[sp_]])
                    PSv = PSS[sp_][:].rearrange("p (t h q) -> p t h q", h=2, q=128)
                    for jt, (j, cls) in enumerate(lst):
                        MM(g, PSS[sp_][:, jt * 256:(jt + 1) * 256], KT[b][:, j * 128:(j + 1) * 128],
                           QBD[sp_][:].rearrange("p h q -> p (h q)"), True, False, [rKT[b], rQBD[sp_]], [rPSS[sp_]])
                        MM(g, PSv[:, jt], g.IDB[:], BIAS[:, 2 * hp:2 * hp + 2, cls, :], False, True,
                           [g.rconst, rBIAS], [rPSS[sp_]])
                    n1 = min(nt, 4)
                    ACT(g, PT[pp][:, 0:n1].rearrange("p t h q -> p (t h q)"), PSS[sp_][:, 0:n1 * 256], AF.Exp, [rPSS[sp_]], [rPT[pp]])
                    if nt > 4:
                        ACT(g, PT[pp][:, 4].rearrange("p h q -> p (h q)"), PSS[sp_][:, 1024:1280], AF.Exp, [rPSS[sp_]], [rPT[pp]])
                    for hh in range(2):
                        for jt, (j, cls) in enumerate(lst):
                            MM(g, PSO[:, hh * 128:hh * 128 + 65], PT[pp][:, jt, hh, :], V[b][:, j, hh, :], jt == 0, jt == nt - 1,
                               [rPT[pp], rV[b]], [rPSO])
                    PSOv = PSO[:, 0:256].rearrange("p (h e) -> p h e", h=2)
                    rc = sp_
                    S.op("dve", lambda e, rc=rc, PSOv=PSOv: e.reciprocal(out=RC[rc][:], in_=PSOv[:, :, 64]), [rPSO], [rRC[rc]])
                    TT_(g, "dve", YTM[yb][:].rearrange("p (h e) -> p h e", h=2), PSOv[:, :, 0:64],
                        RC[rc][:, :, None].to_broadcast([128, 2, 64]), ALU.mult, [rPSO, rRC[rc]], [rYTM[yb]])
                    MM(g, PSTr[:, 0:128], YTM[yb][:], g.IDB[:], True, True, [rYTM[yb], g.rconst], [rPSTr])
                    CP(g, "act", YT[b][:, i * 128:(i + 1) * 128], PSTr[:, 0:128], [rPSTr], [rYT[b]])
                DMA(g, "sp", g.W["YMIX"][hp * 128:(hp + 1) * 128, c0:c0 + T], YT[b][:, 0:T], "B_y%d" % b, [rYT[b]], [rdram])
        S.barrier()


def phaseC(g, l):
    if g.cfg.skip_gdn:
        nc, S, cfg = g.nc, g.S, g.cfg
        S.barrier()
        with ExitStack() as es:
            Z = g.sbt(es, "C_z", [128, 4, 2048], BF16)
            rZ, rd = S.res(), S.res()
            S.op("dve", lambda e: e.memset(Z[:], 0.0), [], [rZ])
            for c0 in range(0, cfg.NT, 2048):
                w = min(2048, cfg.NT - c0)
                DMA(g, "sp", g.W["YMIX"][512:1024, c0:c0 + w].rearrange("(m p) t -> p m t", p=128), Z[:, :, 0:w], "C_z", [rZ], [rd])
            S.barrier()
        return
    phaseC_real(g, l)


def phaseC_real(g, l):
    nc, S, cfg = g.nc, g.S, g.cfg
    S.barrier()
    Tmax = max(T for (_, _, T) in cfg.seqs)
    NTLmax = Tmax // 128
    NCHmax = Tmax // 64
    GB = 256
    CB = 512
    I, W = g.I, g.W
    with ExitStack() as es:
        sb = lambda n, shp, dt=F32: g.sbt(es, "C_" + n, shp, dt)
        MASK = sb("mask", [128, 2, 3, 128], BF16)
        SELR = sb("selr", [128, 16, 128], BF16)
        SELP = sb("selp", [128, 8, 128], BF16)
        IDF = sb("idf", [128, 128], F32)
        RESET = sb("reset", [128, GB], F32)
        CWQ = sb("cwq", [128, 12, 3], F32)
        NORMO = sb("normo", [128, 1], F32)
        DTB = sb("dtb", [40, 1], F32)
        NEGA = sb("nega", [40, 1], F32)
        rC = S.res()
        DMA(g, "pool", MASK[:].rearrange("p a b c -> p (a b c)"), I["c_gmask"], "C_c0", [], [rC])
        DMA(g, "pool", SELR[:].rearrange("p a b -> p (a b)"), I["c_selr"], "C_c1", [], [rC])
        DMA(g, "pool", SELP[:].rearrange("p a b -> p (a b)"), I["c_selp"], "C_c0", [], [rC])
        DMA(g, "sp", IDF[:], I["c_ident"], "C_c2", [], [rC])
        DMA(g, "sp", RESET[:], I["c_reset"][:, 0:GB], "C_c2", [], [rC])
        for k in range(3):
            small_vec_load(g, "sp", CWQ[:, :, k], I["conv_qkv"][l, k].rearrange("(f p) -> p f", p=128), "C_c2", [rC])
        for hh in range(2):
            small_vec_load(g, "sp", NORMO[64 * hh:64 * hh + 64, :], I["norm_o"][l].rearrange("(p o) -> p o", o=1), "C_c2", [rC])
        S.op("dve", lambda e: e.memset(DTB[:], 0.0), [], [rC])
        S.op("dve", lambda e: e.memset(NEGA[:], 0.0), [], [rC])
        for d_ in range(2):
            small_vec_load(g, "sp", DTB[32 * d_:32 * d_ + 8, :], I["dt_bias"][l, 8 * d_:8 * d_ + 8].rearrange("(p o) -> p o", o=1), "C_c2", [rC])
            small_vec_load(g, "sp", NEGA[32 * d_:32 * d_ + 8, :], I["a_log"][l, 8 * d_:8 * d_ + 8].rearrange("(p o) -> p o", o=1), "C_c2", [rC])
        ACT(g, NEGA[:], NEGA[:], AF.Exp, [rC], [rC])
        TS_(g, "dve", NEGA[:], NEGA[:], -1.0, None, ALU.mult, None, [rC], [rC])

        GCS = sb("gcs", [128, Tmax], BF16)
        HS = sb("hs", [128, Tmax], BF16)
        GLS = sb("gls", [128, NCHmax], BF16)
        SCT = sb("sct", [128, NTLmax, 5, 16], F32)
        rGCS, rHS, rGLS, rSCT = S.res(), S.res(), S.res(), S.res()
        for t_ in (GCS, HS, GLS):
            S.op("pool", lambda e, t_=t_: e.memset(t_[:], 0.0), [], [rGCS, rHS, rGLS])
        AR = sb("ar", [40, GB]); BR = sb("br", [40, GB])
        rAR, rBR = S.res(), S.res()
        S.op("pool", lambda e: e.memset(AR[:], 0.0), [], [rAR])
        S.op("pool", lambda e: e.memset(BR[:], 0.0), [], [rBR])
        G_ = sb("g", [40, GB]); PRE = sb("pre", [40, GB]); GC = sb("gc", [40, GB]); LNB = sb("lnb", [40, GB])
        TMPG = sb("tmpg", [40, GB])
        RR = sb("rr", [40, 5, GB])
        RH = sb("rh", [40, 5, GB], BF16); RL = sb("rl", [40, 5, GB], BF16)
        GLF = sb("glf", [40, GB // 64])
        rG = S.res()
        S.op("dve", lambda e: e.memset(GC[:], 0.0), [], [rG])

        XR_A = [sb("xr%d" % i, [128, 3, CB + 2], BF16) for i in range(2)]
        ACC_A = [[sb("acc%d_%d" % (p_, i), [128, CB]) for i in range(3)] for p_ in range(2)]
        SQB_A = [[sb("sqb%d_%d" % (p_, i), [128, CB], BF16) for i in range(2)] for p_ in range(2)]
        RSB_A = [[sb("rsb%d_%d" % (p_, i), [128, CB]) for i in range(2)] for p_ in range(2)]
        QT = sb("qt", [128, Tmax], BF16); KT = sb("kt", [128, Tmax], BF16); VT = sb("vt", [128, Tmax], BF16)
        KTZ = sb("ktz", [128, 2, Tmax], BF16)
        rKTZ = S.res()
        S.op("pool", lambda e: e.memset(KTZ[:], 0.0), [], [rKTZ])
        SZ = sb("sz", [128, Tmax], BF16)
        YT = sb("yt", [128, Tmax], BF16)
        OF = sb("of", [128, NTLmax, 128])
        GLB = sb("glb", [128, 2, NCHmax])
        rQT, rKT, rVT, rSZ, rYT, rOF, rGLB = [S.res() for _ in range(7)]
        rXR_A = [S.res() for _ in range(2)]
        rACC_A = [[S.res() for _ in range(3)] for _ in range(2)]
        rSQB_A = [[S.res() for _ in range(2)] for _ in range(2)]
        rRSB_A = [[S.res() for _ in range(2)] for _ in range(2)]
        qcnt = [0]
        KVTM = [sb("kvtm%d" % i, [128, 2, 128], BF16) for i in range(4)]
        E01 = [sb("e01%d" % i, [128, 2, 256]) for i in range(4)]
        E2 = [sb("e2%d" % i, [128, 2, 128]) for i in range(4)]
        EGC = [sb("egc%d" % i, [128, 128]) for i in range(4)]
        PB_A = [[sb("pb%d_%d" % (d_, i), [128, 2, 128], BF16) for i in range(2)] for d_ in range(2)]
        PTY_A = [[sb("pty%d_%d" % (d_, i), [128, 2, 2, 128], BF16) for i in range(2)] for d_ in range(2)]
        YF_A = [sb("yf%d" % d_, [128, 2, 128]) for d_ in range(2)]
        TB = [sb("tb%d" % i, [128, 2, 128], BF16) for i in range(4)]
        TBG = [sb("tbg%d" % i, [128, 2, 128], BF16) for i in range(4)]
        U = [sb("u%d" % i, [128, 128]) for i in range(4)]
        WT = [sb("wt%d" % i, [128, 2, 128], BF16) for i in range(4)]
        QGT = [sb("qgt%d" % i, [128, 2, 128], BF16) for i in range(4)]
        QKT = [sb("qkt%d" % i, [128, 2, 128], BF16) for i in range(4)]
        KGZ = [sb("kgz%d" % i, [128, 2, 128], BF16) for i in range(4)]
        rKVTM, rE01, rE2, rEGC, rTB, rTBG, rU, rWT, rQGT, rQKT, rKGZ = [[S.res() for _ in range(4)] for _ in range(11)]
        rPB_A = [[S.res() for _ in range(2)] for _ in range(2)]
        rPTY_A = [[S.res() for _ in range(2)] for _ in range(2)]
        rYF_A = [S.res() for _ in range(2)]
        for i in range(4):
            S.op("pool", lambda e, i=i: e.memset(KGZ[i][:], 0.0), [], [rKGZ[i]])
            S.op("pool", lambda e, i=i: e.memset(WT[i][:], 0.0), [], [rWT[i]])
            S.op("pool", lambda e, i=i: e.memset(QGT[i][:], 0.0), [], [rQGT[i]])
        SF_A = [sb("sf%d" % d_, [128, 64]) for d_ in range(2)]
        SBs_A = [sb("sbs%d" % d_, [128, 64], BF16) for d_ in range(2)]
        rSF_A = [S.res() for _ in range(2)]
        rSBs_A = [S.res() for _ in range(2)]
        VNEW_A = [[sb("vnew%d_%d" % (d_, i), [128, 128], BF16) for i in range(2)] for d_ in range(2)]
        rVNEW_A = [[S.res() for _ in range(2)] for _ in range(2)]
        OT = [sb("ot%d" % i, [128, 128]) for i in range(2)]
        ON = [sb("on%d" % i, [128, 128], BF16) for i in range(2)]
        SSQ = [sb("ssq%d" % i, [128, 4]) for i in range(2)]
        JUNK = sb("junk", [128, 64])
        rOT = [S.res() for _ in range(2)]
        rON = [S.res() for _ in range(2)]
        rSSQ = [S.res() for _ in range(2)]
        rJ = S.res()
        PD = g.pst(es, "C_pd", [128, 512]); rPD = S.pres()
        PG = g.pst(es, "C_pg", [128, 512]); rPG = S.pres()
        PX, rPX = PD, rPD
        PA_A = [g.pst(es, "C_pa%d" % d_, [128, 512]) for d_ in range(2)]
        rPA_A = [S.pres() for _ in range(2)]
        PBk_A = [g.pst(es, "C_pbk%d" % d_, [128, 512]) for d_ in range(2)]
        rPBk_A = [S.pres() for _ in range(2)]
        CH_A = [g.pst(es, "C_ch%d" % d_, [128, 512]) for d_ in range(2)]
        rCH_A = [S.pres() for _ in range(2)]
        rdram = S.res()
        IDB = g.IDB
        rI = g.rconst

        def row(d_, h):
            return 32 * d_ + h

        def gate_prep(si, T, c0):
            ntl = T // 128
            for gb in range(T // GB):
                cg = c0 + gb * GB
                for d_ in range(2):
                    DMA(g, "sp", AR[32 * d_:32 * d_ + 8, :], W["GR"][16 + 8 * d_:24 + 8 * d_, cg:cg + GB], "C_ga", [], [rAR])
                    DMA(g, "sp", BR[32 * d_:32 * d_ + 8, :], W["GR"][8 * d_:8 + 8 * d_, cg:cg + GB], "C_gb", [], [rBR])
                R = [rG, rC]
                TS_(g, "dve", G_[:], AR[:], DTB[:, 0:1], None, ALU.add, None, [rAR] + R, [rG])
                ACT(g, G_[:], G_[:], AF.Exp, R, [rG])
                ACT(g, G_[:], G_[:], AF.Ln, R, [rG], bias=1.0)
                TS_(g, "dve", G_[:], G_[:], NEGA[:, 0:1], None, ALU.mult, None, R, [rG])
                S.op("dve", lambda e: e.tensor_tensor_scan(out=PRE[:], data0=RESET[0:40, :], data1=G_[:], initial=0.0,
                                                           op0=ALU.mult, op1=ALU.add), R, [rG])
                totb = PRE[:].rearrange("p (n c) -> p n c", c=64)[:, :, 63:64].to_broadcast([40, GB // 64, 64])
                v3 = lambda t_: t_[:].rearrange("p (n c) -> p n c", c=64)
                CP(g, "dve", GC[0:8, :], PRE[0:8, :], R, [rG])
                TT_(g, "dve", v3(TMPG)[32:40], totb[32:40], v3(PRE)[32:40], ALU.subtract, R, [rG])
                TT_(g, "dve", GC[32:40, :], TMPG[32:40, :], G_[32:40, :], ALU.add, R, [rG])
                ACT(g, RR[:, 2, :], BR[:], AF.Sigmoid, [rBR] + R, [rG])
                ACT(g, LNB[:], RR[:, 2, :], AF.Ln, R, [rG])
                TS_(g, "dve", RR[:, 0, :], GC[:], -1.0, None, ALU.mult, None, R, [rG])
                TT_(g, "dve", RR[:, 1, :], GC[:], LNB[:], ALU.add, R, [rG])
                ACT(g, TMPG[:], GC[:], AF.Exp, R, [rG])
                TT_(g, "dve", RR[:, 3, :], RR[:, 2, :], TMPG[:], ALU.mult, R, [rG])
                TT_(g, "dve", v3(TMPG), totb, v3(GC), ALU.subtract, R, [rG])
                ACT(g, RR[:, 4, :], TMPG[:], AF.Exp, R, [rG])
                ACT(g, GLF[:], PRE[:].rearrange("p (n c) -> p n c", c=64)[:, :, 63], AF.Exp, R, [rG])
                CP(g, "act", RH[:], RR[:], R, [rG])
                TT_(g, "dve", RL[:], RR[:], RH[:], ALU.subtract, R, [rG])
                cs = slice(gb * GB, (gb + 1) * GB)
                CP(g, "act", GCS[0:40, cs], GC[:], R, [rGCS])
                TT_(g, "dve", GCS[64:104, cs], GC[:], GCS[0:40, cs], ALU.subtract, R + [rGCS], [rGCS])
                CP(g, "act", HS[0:40, cs], RR[:, 1, :], R, [rHS])
                TT_(g, "dve", HS[64:104, cs], RR[:, 1, :], HS[0:40, cs], ALU.subtract, R + [rHS], [rHS])
                ns = slice(gb * (GB // 64), (gb + 1) * (GB // 64))
                CP(g, "act", GLS[0:40, ns], GLF[:], R, [rGLS])
                TT_(g, "dve", GLS[64:104, ns], GLF[:], GLS[0:40, ns], ALU.subtract, R + [rGLS], [rGLS])
                for t4 in range(GB // 128):
                    m = gb * (GB // 128) + t4
                    for k in range(5):
                        MM(g, PD[:, k * 40:(k + 1) * 40], RH[:, k, t4 * 128:(t4 + 1) * 128], IDB[0:40, 0:40], True, False, [rG, rI], [rPD])
                        MM(g, PD[:, k * 40:(k + 1) * 40], RL[:, k, t4 * 128:(t4 + 1) * 128], IDB[0:40, 0:40], False, True, [rG, rI], [rPD])
                    PDv = PD[:, 0:200].rearrange("p (k r) -> p k r", r=40)
                    CP(g, "dve", SCT[:, m, :, 0:8], PDv[:, :, 0:8], [rPD], [rSCT])
                    CP(g, "act", SCT[:, m, :, 8:16], PDv[:, :, 32:40], [rPD], [rSCT])

        def sc(m, k, d_, h):
            c = 8 * d_ + h
            return SCT[:, m, k, c:c + 1]

        def qkv_prep(si, T, c0, hp):
            nb = T // CB
            pars = []
            for cb in range(nb):
                pars.append(qcnt[0] % 2)
                qcnt[0] += 1

            def partA(cb):
                par = pars[cb]
                XR, rXR = XR_A[par], rXR_A[par]
                ACC, rACC = ACC_A[par], rACC_A[par]
                cc = c0 + cb * CB
                lo = 0 if cb > 0 else 1
                hi = CB + 2 if cb < nb - 1 else CB + 1
                if lo == 1:
                    S.op("pool", lambda e, XR=XR: e.memset(XR[:, :, 0:1], 0.0), [], [rXR])
                if hi == CB + 1:
                    S.op("pool", lambda e, XR=XR: e.memset(XR[:, :, CB + 1:CB + 2], 0.0), [], [rXR])
                for j in range(3):
                    r0 = j * 512 + hp * 128
                    DMA(g, "sp" if j != 1 else "act", XR[:, j, lo:hi], W["QKVD"][r0:r0 + 128, cc - 1 + lo:cc - 1 + hi], "C_xr%d_%d" % (j, par), [], [rXR])
                for j in range(3):
                    f = j * 4 + hp
                    ACT(g, ACC[j][:], XR[:, j, 0:CB], AF.Copy, [rXR, rC], [rACC[j]], scale=CWQ[:, f, 0:1])
                for tap in (1, 2):
                    for j in range(3):
                        f = j * 4 + hp
                        STT(g, ACC[j][:], XR[:, j, tap:CB + tap], CWQ[:, f, tap:tap + 1], ACC[j][:], ALU.mult, ALU.add,
                            [rXR, rC, rACC[j]], [rACC[j]])

            def partB(cb):
                par = pars[cb]
                ACC, rACC, SQB, rSQB, RSB, rRSB = ACC_A[par], rACC_A[par], SQB_A[par], rSQB_A[par], RSB_A[par], rRSB_A[par]
                cs = slice(cb * CB, (cb + 1) * CB)
                ACT(g, VT[:, cs], ACC[2][:], AF.Silu, [rACC[2]], [rVT])
                for j in range(2):
                    ACT(g, ACC[j][:], ACC[j][:], AF.Silu, [rACC[j]], [rACC[j]])
                for j in range(2):
                    ACT(g, SQB[j][:], ACC[j][:], AF.Square, [rACC[j]], [rSQB[j]])
                PSj = [(PG, rPG), (PD, rPD)]
                for j in range(2):
                    S.begin_atomic()
                    MM(g, PSj[j][0][:, 0:CB], g.ONESBD[:], SQB[j][:], True, True, [rSQB[j], rI], [PSj[j][1]])
                    ACT(g, RSB[j][:], PSj[j][0][:, 0:CB], AF.Ln, [PSj[j][1]], [rRSB[j]], bias=1e-6, scale=1.0)
                    S.end_atomic()
                for j in range(2):
                    ACT(g, RSB[j][:], RSB[j][:], AF.Exp, [rRSB[j]], [rRSB[j]], scale=-0.5)
                STT(g, QT[:, cs], ACC[0][:], 0.125, RSB[0][:], ALU.mult, ALU.mult, [rACC[0], rRSB[0]], [rQT])
                TT_(g, "dve", KT[:, cs], ACC[1][:], RSB[1][:], ALU.mult, [rACC[1], rRSB[1]], [rKT])
                for hh in range(2):
                    b_ = slice(64 * hh, 64 * hh + 64)
                    TT_(g, "pool", KTZ[b_, hh, cs], ACC[1][b_, :], RSB[1][b_, :], ALU.mult, [rACC[1], rRSB[1]], [rKTZ])

            def capq(fn, *a):
                S.capture()
                fn(*a)
                return S.end_capture()

            S.extend(capq(partA, 0))
            for cb in range(nb):
                lists = [capq(partB, cb)]
                if cb + 1 < nb:
                    lists.append(capq(partA, cb + 1))
                S.extend(interleave(*lists))
            DMA(g, "act", SZ[:, 0:T], W["ZT"][hp * 128:(hp + 1) * 128, c0:c0 + T], "C_z", [], [rSZ])
            ACT(g, SZ[:, 0:T], SZ[:, 0:T], AF.Silu, [rSZ], [rSZ])
            nch = T // 64
            for d_ in range(2):
                MM(g, PX[:, 0:nch], SELP[:, d_ * 4 + hp, :], GLS[:, 0:nch], True, True, [rC, rGLS], [rPX])
                CP(g, "dve", GLB[:, d_, 0:nch], PX[:, 0:nch], [rPX], [rGLB])

        def pre(m, d_, hp, pb):
            ts_ = slice(m * 128, (m + 1) * 128)
            PB_, PTY, YF, rPB_, rPTY, rYF = PB_A[d_], PTY_A[d_], YF_A[d_], rPB_A[d_], rPTY_A[d_], rYF_A[d_]
            PA, PBk, rPA, rPBk = PA_A[d_], PBk_A[d_], rPA_A[d_], rPBk_A[d_]
            S.begin_atomic()
            MM(g, PX[:, 0:128], KT[:, ts_], IDB[:], True, True, [rKT, rI], [rPX])
            MM(g, PX[:, 128:256], VT[:, ts_], IDB[:], True, True, [rVT, rI], [rPX])
            CP(g, "act", KVTM[pb][:].rearrange("p a b -> p (a b)"), PX[:, 0:256], [rPX], [rKVTM[pb]])
            if cfg.gdn_stage < 3.07:
                S.end_atomic()
                return
            for hh in range(2):
                b_ = slice(64 * hh, 64 * hh + 64)
                MM(g, PG[:, hh * 256:hh * 256 + 128], KTZ[:, hh, ts_], KT[:, ts_], True, True, [rKTZ, rKT], [rPG])
                MM(g, PG[:, hh * 256 + 128:hh * 256 + 256], KTZ[:, hh, ts_], QT[:, ts_], True, True, [rKTZ, rQT], [rPG])
            if cfg.gdn_stage < 3.15:
                S.end_atomic()
                return
            for hh in range(2):
                h = 2 * hp + hh
                sel = SELR[:, d_ * 8 + h, :]
                MM(g, PD[:, 0:128], sel, HS[:, ts_], True, False, [rC, rHS], [rPD])
                MM(g, PD[:, 128:512].rearrange("p (a b) -> p a b", a=3), sel, GCS[:, None, ts_].to_broadcast([128, 3, 128]),
                   False, False, [rC, rGCS], [rPD])
                MM(g, PD[:, 0:384], IDB[:], MASK[:, d_, :, :].rearrange("p a b -> p (a b)"), False, True, [rI, rC], [rPD])
                ACT(g, E01[pb][:, hh, :], PD[:, 0:256], AF.Exp, [rPD, rSCT], [rE01[pb]], bias=sc(m, 0, d_, h), scale=1.0)
                ACT(g, E2[pb][:, hh, :], PD[:, 256:384], AF.Exp, [rPD, rSCT], [rE2[pb]], bias=sc(m, 1, d_, h), scale=-1.0)
                b_ = slice(64 * hh, 64 * hh + 64)
                ACT(g, EGC[pb][b_, :], PD[b_, 384:512], AF.Exp, [rPD], [rEGC[pb]])
            if cfg.gdn_stage < 3.25:
                S.end_atomic()
                return
            PGv = PG[:].rearrange("p (a b) -> p a b", a=2)
            STT(g, PTY[0][:, :, 0, :], PGv[:, :, 0:128], -1.0, E01[pb][:, :, 0:128], ALU.mult, ALU.mult, [rPG, rE01[pb]], [rPTY[0]])
            TT_(g, "dve", PTY[1][:, :, 1, :], PTY[0][:, :, 0, :], IDF[:, None, :].to_broadcast([128, 2, 128]), ALU.add,
                [rPTY[0], rC], [rPTY[1]])
            STT(g, PB_[0][:], PGv[:, :, 0:128], -1.0, E2[pb][:], ALU.mult, ALU.mult, [rPG, rE2[pb]], [rPB_[0]])
            TT_(g, "dve", QKT[pb][:], PGv[:, :, 128:256], E01[pb][:, :, 128:256], ALU.mult, [rPG, rE01[pb]], [rQKT[pb]])
            S.end_atomic()
            for hh in range(2):
                b_ = slice(64 * hh, 64 * hh + 64)
                TT_(g, "pool", QGT[pb][b_, hh, :], QT[b_, ts_], EGC[pb][b_, :], ALU.mult, [rQT, rEGC[pb]], [rQGT[pb]])
            for hh in range(2):
                h = 2 * hp + hh
                b_ = slice(64 * hh, 64 * hh + 64)
                ACT(g, KGZ[pb][:, hh, b_], KVTM[pb][:, 0, b_], AF.Copy, [rKVTM[pb], rSCT], [rKGZ[pb]], scale=sc(m, 4, d_, h))
            if cfg.gdn_stage < 3.35:
                return
            PAv = PA[:].rearrange("p (a b) -> p a b", a=2)
            PBv = PBk[:, 0:256].rearrange("p (a b) -> p a b", a=2)
            for k in range(6):
                cur, nxt = k % 2, (k + 1) % 2
                for hh in range(2):
                    if k == 0:
                        MM(g, PAv[:, hh, 0:128], PB_[cur][:, hh, :], PTY[cur][:, hh, 0, :], True, True, [rPB_[cur], rPTY[cur]], [rPA])
                    elif k < 4:
                        MM(g, PAv[:, hh, :], PB_[cur][:, hh, :], PTY[cur][:, hh].rearrange("p a b -> p (a b)"), True, True,
                           [rPB_[cur], rPTY[cur]], [rPA])
                    elif k == 4:
                        MM(g, PAv[:, hh, 128:256], PB_[cur][:, hh, :], PTY[cur][:, hh, 1, :], True, True, [rPB_[cur], rPTY[cur]], [rPA])
                    else:
                        MM(g, PAv[:, hh, 128:256], PB_[cur][:, hh, :], PTY[cur][:, hh, 1, :], True, True, [rPB_[cur], rPTY[cur]], [rPA])
                    if k < 5:
                        MM(g, PBv[:, hh, :], PTY[cur][:, hh, 0, :], PB_[cur][:, hh, :], True, True, [rPB_[cur], rPTY[cur]], [rPBk])
                if k < 5:
                    if k < 4:
                        CP(g, "act", PTY[nxt][:, :, 0, :], PAv[:, :, 0:128], [rPA], [rPTY[nxt]])
                    if k == 0:
                        pass
                    else:
                        TT_(g, "dve", PTY[nxt][:, :, 1, :], PAv[:, :, 128:256], PTY[cur][:, :, 1, :], ALU.add, [rPA, rPTY[cur]], [rPTY[nxt]])
                    CP(g, "act" if k % 2 else "dve", PB_[nxt][:], PBv, [rPBk], [rPB_[nxt]])
                else:
                    TT_(g, "dve", YF[:], PAv[:, :, 128:256], PTY[cur][:, :, 1, :], ALU.add, [rPA, rPTY[cur]], [rYF])
            if cfg.gdn_stage < 3.45:
                return
            for hh in range(2):
                h = 2 * hp + hh
                TS_(g, "dve", TB[pb][:, hh, :], YF[:, hh, :], sc(m, 2, d_, h), None, ALU.mult, None, [rYF, rSCT], [rTB[pb]])
                ACT(g, TBG[pb][:, hh, :], YF[:, hh, :], AF.Copy, [rYF, rSCT], [rTBG[pb]], scale=sc(m, 3, d_, h))
            S.begin_atomic()
            for hh in range(2):
                MM(g, PX[:, hh * 64:(hh + 1) * 64], TB[pb][:, hh, :], KVTM[pb][:, 1, hh * 64:(hh + 1) * 64], True, True,
                   [rTB[pb], rKVTM[pb]], [rPX])
                MM(g, PX[:, 128 + hh * 128:256 + hh * 128], KVTM[pb][:, 0, :], TBG[pb][:, hh, :], True, True,
                   [rTBG[pb], rKVTM[pb]], [rPX])
            CP(g, "act", U[pb][:], PX[:, 0:128], [rPX], [rU[pb]])
            for hh in range(2):
                b_ = slice(64 * hh, 64 * hh + 64)
                CP(g, "dve", WT[pb][b_, hh, :], PX[b_, 128 + hh * 128:256 + hh * 128], [rPX], [rWT[pb]])
            S.end_atomic()

        vc = [0]
        oc = [0]

        def chain(m, d_, hp, pb, T, second):
            SF, SBs, rSF, rSBs = SF_A[d_], SBs_A[d_], rSF_A[d_], rSBs_A[d_]
            VNEW, rVNEW = VNEW_A[d_], rVNEW_A[d_]
            CH, rCH = CH_A[d_], rCH_A[d_]
            PW, PO, PSt = CH[:, 0:128], CH[:, 128:256], CH[:, 256:320]
            rPW = rPO = rPSt = rCH
            order = (0, 1) if d_ == 0 else (1, 0)
            for j in order:
                n = 2 * m + j
                tj = slice(64 * j, 64 * j + 64)
                v = vc[0] % 2
                vc[0] += 1
                for hh in range(2):
                    b_ = slice(64 * hh, 64 * hh + 64)
                    MM(g, PW[:, hh * 64:(hh + 1) * 64], WT[pb][:, hh, :], SBs[:, :], True, True, [rWT[pb], rSBs], [rPW])
                TT_(g, "dve", VNEW[v][tj, :], U[pb][tj, :], PW[tj, :], ALU.subtract, [rU[pb], rPW], [rVNEW[v]])
                for hh in range(2):
                    b_ = slice(64 * hh, 64 * hh + 64)
                    MM(g, PO[:, hh * 64:(hh + 1) * 64], QGT[pb][:, hh, :], SBs[:, :], True, False, [rQGT[pb], rSBs], [rPO])
                    MM(g, PO[:, hh * 64:(hh + 1) * 64], QKT[pb][tj, hh, :], VNEW[v][tj, hh * 64:(hh + 1) * 64], False, True,
                       [rQKT[pb], rVNEW[v]], [rPO])
                if not second:
                    CP(g, "act", OF[tj, m, :], PO[tj, :], [rPO], [rOF])
                else:
                    ob = oc[0] % 2
                    TT_(g, "dve", OT[ob][tj, :], PO[tj, :], OF[tj, m, :], ALU.add, [rPO, rOF], [rOT[ob]])
                for hh in range(2):
                    MM(g, PSt[:, :], KGZ[pb][tj, hh, :], VNEW[v][tj, hh * 64:(hh + 1) * 64], hh == 0, hh == 1,
                       [rKGZ[pb], rVNEW[v]], [rPSt])
                STT(g, SF[:], SF[:], GLB[:, d_, n:n + 1], PSt[:, :], ALU.mult, ALU.add, [rSF, rGLB, rPSt], [rSF])
                CP(g, "act", SBs[:], SF[:], [rSF], [rSBs])
            if second:
                ob = oc[0] % 2
                oc[0] += 1
                for hh in range(2):
                    ACT(g, JUNK[:], OT[ob][:, hh * 64:(hh + 1) * 64], AF.Square, [rOT[ob]], [rJ, rSSQ[ob]], accum_out=SSQ[ob][:, hh:hh + 1])
                ACT(g, SSQ[ob][:, 2:4], SSQ[ob][:, 0:2], AF.Sqrt, [rSSQ[ob]], [rSSQ[ob]], scale=1.0 / 64, bias=EPS)
                S.op("dve", lambda e, ob=ob: e.reciprocal(out=SSQ[ob][:, 2:4], in_=SSQ[ob][:, 2:4]), [rSSQ[ob]], [rSSQ[ob]])
                for hh in range(2):
                    TS_(g, "dve", ON[ob][:, hh * 64:(hh + 1) * 64], OT[ob][:, hh * 64:(hh + 1) * 64], SSQ[ob][:, 2 + hh:3 + hh], None,
                        ALU.mult, None, [rOT[ob], rSSQ[ob]], [rON[ob]])
                S.begin_atomic()
                MM(g, PX[:, 384:512], ON[ob][:], IDB[:], True, True, [rON[ob], rI], [rPX])
                ts_ = slice(m * 128, (m + 1) * 128)
                STT(g, YT[:, ts_], PX[:, 384:512], NORMO[:, 0:1], SZ[:, ts_], ALU.mult, ALU.mult, [rPX, rC, rSZ], [rYT])
                S.end_atomic()

        for si, (grp, sidx, T) in enumerate(cfg.seqs):
            c0 = cfg.offs[si]
            ntl = T // 128
            gate_prep(si, T, c0)
            if cfg.gdn_stage <= 1:
                continue
            for hp in range(4):
                qkv_prep(si, T, c0, hp)
                if cfg.gdn_stage <= 2:
                    continue
                for d_ in range(2):
                    S.op("dve", lambda e, d_=d_: e.memset(SF_A[d_][:], 0.0), [], [rSF_A[d_]])
                    S.op("pool", lambda e, d_=d_: e.memset(SBs_A[d_][:], 0.0), [], [rSBs_A[d_]])
                tl = [list(range(ntl)), list(range(ntl - 1, -1, -1))]
                seen = set()

                def cap(fn, *a):
                    S.capture()
                    fn(*a)
                    return S.end_capture()

                S.extend(interleave(cap(pre, tl[0][0], 0, hp, 0), cap(pre, tl[1][0], 1, hp, 2)))
                for ix in range(ntl):
                    lists = []
                    for d_ in range(2):
                        m = tl[d_][ix]
                        if cfg.gdn_stage >= 4:
                            lists.append(cap(chain, m, d_, hp, 2 * d_ + ix % 2, T, m in seen))
                        seen.add(m)
                    if ix + 1 < ntl:
                        for d_ in range(2):
                            lists.append(cap(pre, tl[d_][ix + 1], d_, hp, 2 * d_ + (ix + 1) % 2))
                    S.extend(interleave(*lists))
                DMA(g, "sp", W["YMIX"][512 + hp * 128:512 + (hp + 1) * 128, c0:c0 + T], YT[:, 0:T], "C_y", [rYT], [rdram])
        S.barrier()


def phaseM(g, l):
    pass


def small_vec_load(g, q, dst, src, semkey, writes):
    return g.S.dma(q, lambda e: e.dma_start(out=dst, in_=src, allow_slow_non_contiguous=True), semkey, [], writes)


def phaseD(g, l):
    nc, S, cfg = g.nc, g.S, g.cfg
    NS = cfg.NS
    S.barrier()
    with ExitStack() as es:
        KMT = g.sbt(es, "D_kmt", [128, 8, NS, NMEM], BF16)
        VM = g.sbt(es, "D_vm", [128, NS, 2, D], BF16)
        rKMT, rVM = S.res(), S.res()
        WOUT = g.sbt(es, "D_wout", [128, KC, D], BF16)
        WXQ = g.sbt(es, "D_wxq", [128, KC, D], BF16)
        WXO = g.sbt(es, "D_wxo", [128, KC, D], BF16)
        rWD = S.res()
        for nm, Wt in (("w_out", WOUT), ("w_xq", WXQ), ("w_xo", WXO)):
            wsrc = g.I[nm][l].rearrange("(kc p) n -> p kc n", p=128)
            for kc in range(KC):
                DMA(g, "pool", Wt[:, kc, :], wsrc[:, kc, :], "D_w%d" % (kc % 4), [], [rWD])
        with ExitStack() as esm:
            WKV = g.sbt(esm, "M_wkv", [128, KC, 2 * D], BF16)
            rWKV = S.res()
            wsrc = g.I["w_xkv"][l].rearrange("(kc p) n -> p kc n", p=128)
            for kc in range(KC):
                DMA(g, "pool", WKV[:, kc, :], wsrc[:, kc, :], "M_w%d" % (kc % 4), [], [rWKV])
            MHF = [g.sbt(esm, "M_mhf%d" % i, [128, KC, NMEM], F32) for i in range(2)]
            MN = [g.sbt(esm, "M_mn%d" % i, [128, KC, NMEM], BF16) for i in range(2)]
            rMHF = [S.res() for _ in range(2)]
            rMN = [S.res() for _ in range(2)]
            PSm = [g.pst(esm, "M_ps%d" % i, [128, 512], F32) for i in range(4)]
            rPSm = [S.pres() for _ in range(4)]
            MHv = g.W["MH"].rearrange("(kc p) t -> p kc t", p=128)
            pc = 0
            for si in range(NS):
                b = si % 2
                DMA(g, "sp", MHF[b][:], MHv[:, :, si * NMEM:(si + 1) * NMEM], "M_m%d" % b, [], [rMHF[b]])
                for kc in range(KC):
                    TS_(g, "dve" if kc % 2 == 0 else "pool", MN[b][:, kc, :], MHF[b][:, kc, :], g.NMEMG[:, l, kc:kc + 1], None,
                        ALU.mult, None, [rMHF[b], g.rconst], [rMN[b]])
                for m in range(8):
                    p = pc % 4
                    pc += 1
                    for kc in range(KC):
                        MM(g, PSm[p][:, 0:NMEM], WKV[:, kc, m * 128:(m + 1) * 128], MN[b][:, kc, :], kc == 0, kc == KC - 1,
                           [rWKV, rMN[b]], [rPSm[p]])
                    CP(g, "act" if m % 2 == 0 else "dve", KMT[:, m, si, :], PSm[p][:, 0:NMEM], [rPSm[p]], [rKMT])
                for mt in range(2):
                    for nh in range(2):
                        p = pc % 4
                        pc += 1
                        for kc in range(KC):
                            MM(g, PSm[p][:], MN[b][:, kc, mt * 128:(mt + 1) * 128], WKV[:, kc, D + nh * 512:D + (nh + 1) * 512],
                               kc == 0, kc == KC - 1, [rWKV, rMN[b]], [rPSm[p]])
                        CP(g, "act" if nh == 0 else "dve", VM[:, si, mt, nh * 512:(nh + 1) * 512], PSm[p][:], [rPSm[p]], [rVM])
            S.barrier()
        YM = g.sbt(es, "D_ym", [128, KC, TT], BF16)
        XF_A = [g.sbt(es, "D_xf%d" % i, [128, KC, TT], F32) for i in range(2)]
        SQ = g.sbt(es, "D_sq", [128, KC, TT], BF16)
        RSTD = g.sbt(es, "D_rstd", [128, TT], F32)
        XN = g.sbt(es, "D_xn", [128, KC, TT], BF16)
        QX_A = [g.sbt(es, "D_qx%d" % i, [128, KC, TT], BF16) for i in range(2)]
        PT = [g.sbt(es, "D_pt%d" % i, [128, 2, TT], BF16) for i in range(2)]
        RCP = [g.sbt(es, "D_rcp%d" % i, [128, TT], F32) for i in range(2)]
        ATT = g.sbt(es, "D_att", [128, KC, TT], BF16)
        XN3 = g.sbt(es, "D_xn3", [128, KC, TT], BF16)
        rYM, rSQ, rRSTD, rXN, rATT, rXN3 = [S.res() for _ in range(6)]
        rXF_A = [S.res() for _ in range(2)]
        rQX_A = [S.res() for _ in range(2)]
        rPT = [S.res() for _ in range(2)]
        rRCP = [S.res() for _ in range(2)]
        PSs = g.pst(es, "D_pss", [128, TT], F32)
        rPSs = S.pres()
        PS2 = g.pst(es, "D_pssc", [128, 2 * TT], F32)
        rPS2 = S.pres()
        PSd = g.pst(es, "D_psd", [128, TT], F32)
        rPSd = S.pres()
        PS = [g.pst(es, "D_ps%d" % i, [128, TT], F32) for i in range(4)]
        rPS = [S.pres() for _ in range(4)]
        rdram = S.res()
        XTv = g.W["XT"].rearrange("(kc p) t -> p kc t", p=128)
        YMv = g.W["YMIX"].rearrange("(kc p) t -> p kc t", p=128)
        X3v = g.W["XN3"].rearrange("(kc p) t -> p kc t", p=128)
        pcs = [0]
        hcs = [0, 0]
        tiles = []
        for si, (grp, sidx, T) in enumerate(cfg.seqs):
            for t4 in range(T // TT):
                tiles.append((si, cfg.offs[si] + t4 * TT))

        def part1(ix):
            si, c0 = tiles[ix]
            XF, rXF, QX, rQX = XF_A[ix % 2], rXF_A[ix % 2], QX_A[ix % 2], rQX_A[ix % 2]
            DMA(g, "sp", YM[:], YMv[:, :, c0:c0 + TT], "D_ym", [], [rYM])
            DMA(g, "act", XF[:], XTv[:, :, c0:c0 + TT], "D_xf%d" % (ix % 2), [], [rXF])
            for m in range(KC):
                p = pcs[0] % 2
                pcs[0] += 1
                for kc in range(KC):
                    MM(g, PS[p][:], WOUT[:, kc, m * 128:(m + 1) * 128], YM[:, kc, :], kc == 0, kc == KC - 1, [rWD, rYM], [rPS[p]])
                TT_(g, "dve", XF[:, m, :], PS[p][:], XF[:, m, :], ALU.add, [rPS[p], rXF], [rXF])
            S.begin_atomic()
            rmsnorm_fm(g, XF, rXF, XN, rXN, SQ, rSQ, RSTD, rRSTD, PSs, rPSs, 3 * l + 1, TT)
            S.end_atomic()
            for m in range(KC):
                p = pcs[0] % 2
                pcs[0] += 1
                for kc in range(KC):
                    MM(g, PS[p][:], WXQ[:, kc, m * 128:(m + 1) * 128], XN[:, kc, :], kc == 0, kc == KC - 1, [rWD, rXN], [rPS[p]])
                ACT(g, QX[:, m, :], PS[p][:], AF.Copy, [rPS[p]], [rQX], scale=1.0 / 16.0)

        def part2(ix):
            si, c0 = tiles[ix]
            XF, rXF, QX, rQX = XF_A[ix % 2], rXF_A[ix % 2], QX_A[ix % 2], rQX_A[ix % 2]
            for hx in range(4):
                hb = hcs[0] % 2
                hcs[0] += 1
                for mt in range(2):
                    for c in range(2):
                        MM(g, PS2[:, mt * TT:(mt + 1) * TT], KMT[:, hx * 2 + c, si, mt * 128:(mt + 1) * 128], QX[:, hx * 2 + c, :],
                           c == 0, c == 1, [rKMT, rQX], [rPS2])
                ACT(g, PT[hb][:], PS2[:].rearrange("p (a b) -> p a b", a=2), AF.Exp, [rPS2], [rPT[hb]])
                for mt in range(2):
                    MM(g, PSd[:], g.ONESB[:], PT[hb][:, mt, :], mt == 0, mt == 1, [g.rconst, rPT[hb]], [rPSd])
                ACT(g, RCP[hb][:], PSd[:], AF.Ln, [rPSd], [rRCP[hb]])
                ACT(g, RCP[hb][:], RCP[hb][:], AF.Exp, [rRCP[hb]], [rRCP[hb]], scale=-1.0)
                for c in range(2):
                    p = 2 + hcs[1] % 2
                    hcs[1] += 1
                    for mt in range(2):
                        MM(g, PS[p][:], VM[:, si, mt, hx * 256 + c * 128:hx * 256 + (c + 1) * 128], PT[hb][:, mt, :],
                           mt == 0, mt == 1, [rVM, rPT[hb]], [rPS[p]])
                    TT_(g, "dve", ATT[:, hx * 2 + c, :], PS[p][:], RCP[hb][:], ALU.mult, [rPS[p], rRCP[hb]], [rATT])
            for m in range(KC):
                p = 2 + hcs[1] % 2
                hcs[1] += 1
                for kc in range(KC):
                    MM(g, PS[p][:], WXO[:, kc, m * 128:(m + 1) * 128], ATT[:, kc, :], kc == 0, kc == KC - 1, [rWD, rATT], [rPS[p]])
                TT_(g, "dve", XF[:, m, :], PS[p][:], XF[:, m, :], ALU.add, [rPS[p], rXF], [rXF])
            DMA(g, "sp", XTv[:, :, c0:c0 + TT], XF[:], "D_xs%d" % (ix % 2), [rXF], [rdram])
            S.begin_atomic()
            rmsnorm_fm(g, XF, rXF, XN3, rXN3, SQ, rSQ, RSTD, rRSTD, PSs, rPSs, 3 * l + 2, TT)
            S.end_atomic()
            DMA(g, "act", X3v[:, :, c0:c0 + TT], XN3[:], "D_x3", [rXN3], [rdram])

        def cap(fn, *a):
            S.capture()
            fn(*a)
            return S.end_capture()

        S.extend(cap(part1, 0))
        for ix in range(len(tiles)):
            lists = [cap(part2, ix)]
            if ix + 1 < len(tiles):
                lists.append(cap(part1, ix + 1))
            S.extend(interleave(*lists))
        S.barrier()


def phaseE(g, l):
    nc, S, cfg = g.nc, g.S, g.cfg
    S.barrier()
    NF = 2 * DFF // 128
    NP2 = DFF // 128
    with ExitStack() as es:
        WUP = g.sbt(es, "E_wup", [128, KC, 2 * DFF], BF16)
        WDN = g.sbt(es, "E_wdn", [128, NP2, D], BF16)
        rWU, rWDn = S.res(), S.res()
        wsrc = g.I["w_up"][l].rearrange("(kc p) n -> p kc n", p=128)
        for kc in range(KC):
            DMA(g, "pool", WUP[:, kc, :], wsrc[:, kc, :], "E_w%d" % (kc % 4), [], [rWU])
        wsrc = g.I["w_down"][l].rearrange("(k p) n -> p k n", p=128)
        for k in range(NP2):
            DMA(g, "pool", WDN[:, k, :], wsrc[:, k, :], "E_w%d" % (k % 4), [], [rWDn])
        CW = g.sbt(es, "E_cw", [128, NF, 4], F32)
        rCW = S.res()
        for k in range(3):
            small_vec_load(g, "sp", CW[:, :, k], g.I["conv_ffn"][l, k].rearrange("(f p) -> p f", p=128), "E_c", [rCW])
        small_vec_load(g, "sp", CW[:, :, 3], g.I["conv_ffn_b"][l].rearrange("(f p) -> p f", p=128), "E_c", [rCW])
        XH = [g.sbt(es, "E_xh%d" % i, [128, KC, TT + 2], BF16) for i in range(2)]
        rXH = [S.res() for _ in range(2)]
        ACTT = g.sbt(es, "E_act", [128, NP2, TT], BF16)
        rACTT = S.res()
        XF = g.sbt(es, "E_xf", [128, KC, TT], F32)
        rXF = S.res()
        T1 = [g.sbt(es, "E_t1%d" % i, [128, TT], F32) for i in range(2)]
        T2 = [g.sbt(es, "E_t2%d" % i, [128, TT], F32) for i in range(2)]
        T3 = [g.sbt(es, "E_t3%d" % i, [128, TT], F32) for i in range(2)]
        rT1 = [S.res() for _ in range(2)]
        rT2 = [S.res() for _ in range(2)]
        rT3 = [S.res() for _ in range(2)]
        PSH = g.pst(es, "E_psh", [128, 6 * 512], F32)
        rSLOT = [S.pres() for _ in range(3)]
        PSO = [g.pst(es, "E_pso%d" % i, [128, TT], F32) for i in range(2)]
        rPSO = [S.pres() for _ in range(2)]
        rdram = S.res()
        XTv = g.W["XT"].rearrange("(kc p) t -> p kc t", p=128)
        X3v = g.W["XN3"].rearrange("(kc p) t -> p kc t", p=128)
        W_ = TT + 2

        def slot_pieces(k):
            a = 1024 * k
            return [(a, a + 512, 0, 512), (a + 512, a + W_, 512, W_)]

        ti = 0
        pcount = 0
        oc = 0
        slotc = [0]
        etiles = []
        for si, (grp, sidx, T) in enumerate(cfg.seqs):
            for t4 in range(T // TT):
                etiles.append((si, t4, T // TT))

        def load_xh(ix):
            si, t4, nt4 = etiles[ix]
            c0 = cfg.offs[si] + t4 * TT
            b = ix % 2
            lo = 0 if t4 > 0 else 1
            hi = W_ if t4 < nt4 - 1 else W_ - 1
            if lo == 1:
                S.op("pool", lambda e, b=b: e.memset(XH[b][:, :, 0:1], 0.0), [], [rXH[b]])
            if hi == W_ - 1:
                S.op("pool", lambda e, b=b: e.memset(XH[b][:, :, W_ - 1:W_], 0.0), [], [rXH[b]])
            DMA(g, "sp", XH[b][:, :, lo:hi], X3v[:, :, c0 - 1 + lo:c0 - 1 + hi], "E_xh%d" % b, [], [rXH[b]])

        load_xh(0)
        for ix_, (si, t4, nt4) in enumerate(etiles):
            if True:
                c0 = cfg.offs[si] + t4 * TT
                b = ti % 2
                ti += 1
                if ix_ + 1 < len(etiles):
                    load_xh(ix_ + 1)
                DMA(g, "act", XF[:], XTv[:, :, c0:c0 + TT], "E_xf", [], [rXF])
                for i in range(NP2):
                    pr = pcount % 2
                    pcount += 1
                    for which, f in ((0, i), (1, NP2 + i)):
                        k = slotc[0] % 3
                        slotc[0] += 1
                        for kc in range(KC):
                            for (pa, pb, ra, rb) in slot_pieces(k):
                                MM(g, PSH[:, pa:pb], WUP[:, kc, f * 128:(f + 1) * 128], XH[b][:, kc, ra:rb], kc == 0, kc == KC - 1,
                                   [rWU, rXH[b]], [rSLOT[k]])
                        a0 = 1024 * k
                        Tt, rTt = (T1[pr], rT1[pr]) if which == 0 else (T2[pr], rT2[pr])
                        ACT(g, Tt[:], PSH[:, a0 + 1:a0 + 1 + TT], AF.Identity, [rSLOT[k], rCW], [rTt],
                            scale=CW[:, f, 1:2], bias=CW[:, f, 3:4])
                        STT(g, Tt[:], PSH[:, a0:a0 + TT], CW[:, f, 0:1], Tt[:], ALU.mult, ALU.add, [rSLOT[k], rCW, rTt], [rTt])
                        STT(g, Tt[:], PSH[:, a0 + 2:a0 + 2 + TT], CW[:, f, 2:3], Tt[:], ALU.mult, ALU.add, [rSLOT[k], rCW, rTt], [rTt])
                    ACT(g, T3[pr][:], T2[pr][:], AF.Silu, [rT2[pr]], [rT3[pr]])
                    TT_(g, "pool", ACTT[:, i, :], T3[pr][:], T1[pr][:], ALU.mult, [rT3[pr], rT1[pr]], [rACTT])
                for m in range(KC):
                    p = oc % 2
                    oc += 1
                    for k in range(NP2):
                        MM(g, PSO[p][:], WDN[:, k, m * 128:(m + 1) * 128], ACTT[:, k, :], k == 0, k == NP2 - 1, [rWDn, rACTT], [rPSO[p]])
                    TT_(g, "dve", XF[:, m, :], PSO[p][:], XF[:, m, :], ALU.add, [rPSO[p], rXF], [rXF])
                DMA(g, "sp", XTv[:, :, c0:c0 + TT], XF[:], "E_xs", [rXF], [rdram])
        S.barrier()


def phaseF(g):
    nc, S, cfg = g.nc, g.S, g.cfg
    L = cfg.depth
    S.barrier()
    with ExitStack() as es:
        XF = [g.sbt(es, "F_xf%d" % i, [128, KC, TT], F32) for i in range(2)]
        rXF = [S.res() for _ in range(2)]
        SQ = g.sbt(es, "F_sq", [128, KC, TT], BF16)
        RSTD = g.sbt(es, "F_rstd", [128, TT], F32)
        R1 = g.sbt(es, "F_r1", [128, KC, TT], F32)
        HI = g.sbt(es, "F_hi", [128, 3, KC, TT], BF16)
        OUT = [g.sbt(es, "F_out%d" % i, [128, 4, D], F32) for i in range(2)]
        rSQ, rRSTD, rR1, rHI = [S.res() for _ in range(4)]
        rOUT = [S.res() for _ in range(2)]
        PSs = g.pst(es, "F_pss", [128, TT], F32)
        rPSs = S.pres()
        PS = [g.pst(es, "F_ps%d" % i, [128, TT], F32) for i in range(4)]
        rPS = [S.pres() for _ in range(4)]
        rdram = S.res()
        XTv = g.W["XT"].rearrange("(kc p) t -> p kc t", p=128)
        ti = 0
        pc = 0
        for si, (grp, sidx, T) in enumerate(cfg.seqs):
            ydst = y_dst(g, si)
            for t4 in range(T // TT):
                c0 = cfg.offs[si] + t4 * TT
                b = ti % 2
                ti += 1
                DMA(g, "sp", XF[b][:], XTv[:, :, c0:c0 + TT], "F_x%d" % b, [], [rXF[b]])
                ACT(g, SQ[:], XF[b][:], AF.Square, [rXF[b]], [rSQ])
                for kc in range(KC):
                    MM(g, PSs[:], g.ONESB[:], SQ[:, kc, :], kc == 0, kc == KC - 1, [rSQ, g.rconst], [rPSs])
                ACT(g, RSTD[:], PSs[:], AF.Ln, [rPSs], [rRSTD], scale=1.0 / D, bias=EPS)
                ACT(g, RSTD[:], RSTD[:], AF.Exp, [rRSTD], [rRSTD], scale=-0.5)
                for kc in range(KC):
                    STT(g, XF[b][:, kc, :], XF[b][:, kc, :], g.NORMS[:, 3 * L, kc:kc + 1], RSTD[:], ALU.mult, ALU.mult,
                        [rXF[b], rRSTD, g.rconst], [rXF[b]])
                CP(g, "act", HI[:, 0], XF[b][:], [rXF[b]], [rHI])
                TT_(g, "dve", R1[:], XF[b][:], HI[:, 0], ALU.subtract, [rXF[b], rHI], [rR1])
                CP(g, "act", HI[:, 1], R1[:], [rR1], [rHI])
                TT_(g, "pool", R1[:], R1[:], HI[:, 1], ALU.subtract, [rR1, rHI], [rR1])
                CP(g, "act", HI[:, 2], R1[:], [rR1], [rHI])
                for ts in range(4):
                    for half in range(2):
                        p = pc % 4
                        pc += 1
                        for q in range(4):
                            kc = half * 4 + q
                            for part in range(3):
                                MM(g, PS[p][:, q * 128:(q + 1) * 128], HI[:, part, kc, ts * 128:(ts + 1) * 128], g.IDB[:],
                                   part == 0, part == 2, [rHI, g.rconst], [rPS[p]])
                        CP(g, "act" if half == 0 else "dve", OUT[b][:, ts, half * 512:(half + 1) * 512], PS[p][:], [rPS[p]], [rOUT[b]])
                t0 = t4 * TT
                DMA(g, "act", ydst[t0:t0 + TT, :].rearrange("(s p) f -> p s f", p=128), OUT[b][:], "F_o%d" % b, [rOUT[b]], [rdram])
        S.barrier()


N_CORES = 8
_FULL = dict(n_p=4, T_p=2048, n_s=2, T_s=4096, depth=4)
SKIP_GDN = False


def _device_inputs(inputs, L):
    m = {}
    f = lambda a: np.ascontiguousarray(np.asarray(a, dtype=np.float32))
    for k in ("norm_mix", "w_in", "conv_qkv", "norm_o", "w_out", "norm_x", "norm_mem", "w_xq", "w_xkv", "w_xo",
              "norm_ffn", "w_up", "conv_ffn", "conv_ffn_b", "w_down", "norm_final"):
        m[k] = f(inputs[k])
    m["a_log"] = f(inputs["a_log"]).reshape(L, 16)
    m["dt_bias"] = f(inputs["dt_bias"]).reshape(L, 16)
    m["na_bias"] = _na_bias_table(f(inputs["rpb"])).reshape(L, 128, 8 * 9 * 128)
    m.update(_consts())
    return m


def kernel(**inputs):
    cfg = Cfg(_FULL["n_p"], _FULL["T_p"], _FULL["n_s"], _FULL["T_s"], _FULL["depth"], skip_gdn=SKIP_GDN)
    nc, g = build_program(cfg)
    shared = _device_inputs(inputs, cfg.depth)
    xp = np.asarray(inputs["x_prompt"], dtype=np.float32)
    xs = np.asarray(inputs["x_sample"], dtype=np.float32)
    mp = np.asarray(inputs["mem_prompt"], dtype=np.float32)
    ms = np.asarray(inputs["mem_sample"], dtype=np.float32)
    in_maps = []
    for c in range(N_CORES):
        m = dict(shared)
        m["x_p"] = np.ascontiguousarray(xp[cfg.n_p * c:cfg.n_p * (c + 1)])
        m["x_s"] = np.ascontiguousarray(xs[cfg.n_s * c:cfg.n_s * (c + 1)])
        m["m_p"] = np.ascontiguousarray(mp[cfg.n_p * c:cfg.n_p * (c + 1)])
        m["m_s"] = np.ascontiguousarray(ms[cfg.n_s * c:cfg.n_s * (c + 1)])
        in_maps.append(m)
    res = run_bass_kernel_spmd(nc, in_maps, core_ids=list(range(N_CORES)))
    y_p = np.concatenate([np.asarray(r["y_p"], dtype=np.float32) for r in res.results], axis=0)
    y_s = np.concatenate([np.asarray(r["y_s"], dtype=np.float32) for r in res.results], axis=0)
    return (y_p, y_s)
```

```python
import numpy as np
from contextlib import ExitStack

import concourse.bass as bass
import concourse.mybir as mybir
from concourse.bass_utils import run_bass_kernel_spmd

F32 = mybir.dt.float32
BF16 = mybir.dt.bfloat16
AF = mybir.ActivationFunctionType
ALU = mybir.AluOpType

D = 1024
KC = 8
DIN = 3616
DFF = 2816
NMEM = 256
NEG = -30000.0
EPS = 1e-6
TT = 512

SAME_ENGINE_SYNC = True
ATTACH_WAIT = True


class Res:
    __slots__ = ("name", "last_w", "readers", "excl")

    def __init__(self, name="", last_w=None, excl=False):
        self.name = name
        self.last_w = last_w
        self.readers = []
        self.excl = excl


class Op:
    __slots__ = ("eng", "fn", "reads", "writes", "dma", "semkey", "val", "needs_inc", "idx", "barrier", "mm_stop", "atom")


class Sched:
    def __init__(self, nc):
        self.nc = nc
        self.ops = []
        self.eng_obj = {"pe": nc.tensor, "act": nc.scalar, "dve": nc.vector, "pool": nc.gpsimd, "sp": nc.sync}
        self.last_barrier = None
        self.all_res = []
        self.cap = None
        self.atom = None
        self.atom_ctr = 0

    def begin_atomic(self):
        self.atom_ctr += 1
        self.atom = self.atom_ctr

    def end_atomic(self):
        self.atom = None

    def res(self, name=""):
        r = Res(name, self.last_barrier)
        self.all_res.append(r)
        return r

    def pres(self, name=""):
        r = Res(name, self.last_barrier, excl=True)
        self.all_res.append(r)
        return r

    def _new(self, eng, fn, reads, writes):
        o = Op()
        ex = [r for r in reads if r.excl]
        if ex:
            reads = [r for r in reads if not r.excl]
            writes = list(writes) + [r for r in ex if r not in writes]
        o.eng, o.fn, o.reads, o.writes = eng, fn, tuple(reads), tuple(writes)
        o.dma, o.semkey, o.val, o.needs_inc, o.barrier = False, None, None, False, False
        o.mm_stop = None
        o.atom = self.atom
        if self.cap is not None:
            self.cap.append(o)
        else:
            self.ops.append(o)
        return o

    def op(self, eng, fn, reads=(), writes=()):
        return self._new(eng, fn, reads, writes)

    def dma(self, queue, fn, semkey, reads=(), writes=()):
        o = self._new(queue, fn, reads, writes)
        o.dma, o.semkey = True, semkey
        return o

    def capture(self):
        assert self.cap is None
        self.cap = []

    def end_capture(self):
        c, self.cap = self.cap, None
        return c

    def extend(self, ops):
        assert self.cap is None
        self.ops.extend(ops)

    def barrier(self):
        o = self._new("sp", lambda e: e.nop(), [], [])
        o.barrier = True
        return o

    def emit(self):
        nc = self.nc
        ops = self.ops
        n = len(ops)
        for i, o in enumerate(ops):
            o.idx = i
        deps_all = [None] * n
        last_dma_on_sem = {}
        last_on_eng = {}
        outstanding_dma = set()
        cur_barrier = None
        for o in ops:
            deps = set()
            if o.barrier:
                for e, d in last_on_eng.items():
                    deps.add(d)
                deps.update(outstanding_dma)
                outstanding_dma = set()
                cur_barrier = o.idx
            else:
                for r in o.reads:
                    lw = r.last_w
                    if lw is not None:
                        deps.add(lw.idx if isinstance(lw, Op) else lw)
                for w in o.writes:
                    lw = w.last_w
                    if lw is not None:
                        deps.add(lw.idx if isinstance(lw, Op) else lw)
                    deps.update(w.readers)
                if o.dma:
                    p = last_dma_on_sem.get(o.semkey)
                    if p is not None:
                        deps.add(p)
                    last_dma_on_sem[o.semkey] = o.idx
                    outstanding_dma.add(o.idx)
                if cur_barrier is not None:
                    deps.add(cur_barrier)
            deps.discard(o.idx)
            deps_all[o.idx] = deps
            for r in o.reads:
                r.readers.append(o.idx)
            for w in o.writes:
                w.last_w = o.idx
                w.readers = []
            last_on_eng[o.eng] = o.idx
        known = {e: {} for e in self.eng_obj}
        waits = [None] * n
        for o in ops:
            best = {}
            dmadeps = []
            for d in deps_all[o.idx]:
                p = ops[d]
                if p.dma:
                    dmadeps.append(d)
                else:
                    if p.eng == o.eng and not o.dma:
                        if p.eng in ("pe", "sp") or not SAME_ENGINE_SYNC:
                            continue
                    if p.eng not in best or best[p.eng] < d:
                        best[p.eng] = d
            w = []
            for e, d in best.items():
                kn = known[o.eng].get(("c", e), -1)
                if d > kn:
                    known[o.eng][("c", e)] = d
                    w.append(d)
                    ops[d].needs_inc = True
            for d in dmadeps:
                key = ("d", ops[d].semkey)
                kn = known[o.eng].get(key, -1)
                if d > kn:
                    known[o.eng][key] = d
                    w.append(d)
            waits[o.idx] = w
        sems = {}
        self._stack = []

        def get_sem(key):
            if key not in sems:
                cm = nc.semaphore("s%d" % len(sems))
                s = cm.__enter__()
                self._stack.append(cm)
                sems[key] = [s, 0]
            return sems[key]

        self.n_inst = {e: 0 for e in self.eng_obj}
        for o in ops:
            eng = self.eng_obj[o.eng]
            wl = waits[o.idx]
            attach = None
            if ATTACH_WAIT and wl and not o.barrier:
                attach = wl[-1]
                wl = wl[:-1]
            for d in wl:
                p = ops[d]
                key = ("dma", p.semkey) if p.dma else ("eng", p.eng)
                s = get_sem(key)[0]
                assert p.val is not None
                eng.wait_ge(s, p.val)
                self.n_inst[o.eng] += 1
            inst = o.fn(eng)
            if attach is not None:
                p = ops[attach]
                key = ("dma", p.semkey) if p.dma else ("eng", p.eng)
                inst.wait_op(get_sem(key)[0], p.val, "sem-ge")
            self.n_inst[o.eng] += 1
            if o.dma:
                sv = get_sem(("dma", o.semkey))
                sv[1] += 16
                inst.then_inc(sv[0], 16)
                o.val = sv[1]
            elif o.needs_inc:
                sv = get_sem(("eng", o.eng))
                sv[1] += 1
                inst.then_inc(sv[0], 1)
                o.val = sv[1]
        feng = self.eng_obj["sp"]
        for key, (s, v) in sems.items():
            if key[0] == "dma" and v > 0:
                feng.wait_ge(s, v)
        self.n_sems = len(sems)

    def close(self):
        for cm in reversed(self._stack):
            cm.__exit__(None, None, None)


def interleave(*lists):
    out = []
    its = [list(l) for l in lists]
    pos = [0] * len(its)
    tot = sum(len(l) for l in its)
    while len(out) < tot:
        best, bi = None, -1
        for i, l in enumerate(its):
            if pos[i] < len(l):
                frac = pos[i] / len(l)
                if best is None or frac < best:
                    best, bi = frac, i
        out.append(its[bi][pos[bi]])
        pos[bi] += 1
        if out[-1].atom is not None:
            a = out[-1].atom
            while pos[bi] < len(its[bi]) and its[bi][pos[bi]].atom == a:
                out.append(its[bi][pos[bi]])
                pos[bi] += 1
        elif out[-1].mm_stop is False:
            while pos[bi] < len(its[bi]):
                o = its[bi][pos[bi]]
                out.append(o)
                pos[bi] += 1
                if o.mm_stop is True:
                    break
    return out


def _na_pairs(T):
    rows = T // 64
    kh = 8
    npair = rows // 2
    out = []
    for i in range(npair):
        lst = []
        for j in range(npair):
            V = np.zeros((2, 2), bool)
            for b in range(2):
                r = 2 * i + b
                rs = min(max(r - kh // 2, 0), rows - kh)
                for a in range(2):
                    kr = 2 * j + a
                    V[a, b] = (rs <= kr < rs + kh)
            if not V.any():
                continue
            delta = j - i
            if V.all():
                assert -3 <= delta <= 3
                cls = delta + 3
            elif delta == -2 and V[0, 0] and V[1, 0] and V[1, 1] and not V[0, 1]:
                cls = 7
            elif delta == 2 and V[0, 1] and not V[0, 0] and not V[1, 0] and not V[1, 1]:
                cls = 8
            else:
                raise AssertionError("unexpected NA window pattern %s %d %d" % (V, i, j))
            lst.append((j, cls))
        out.append(lst)
    return out


def _na_bias_table(rpb):
    L = rpb.shape[0]
    a = np.arange(2)[:, None, None, None]
    kc = np.arange(64)[None, :, None, None]
    b = np.arange(2)[None, None, :, None]
    qc = np.arange(64)[None, None, None, :]
    cs = np.clip(qc - 8, 0, 48)
    colok = (kc >= cs) & (kc < cs + 16)
    coff = np.clip(kc - qc + 15, 0, 30)
    tab = np.full((L, 8, 9, 2, 64, 2, 64), NEG, np.float32)
    for cls in range(9):
        if cls < 7:
            delta = cls - 3
            rowok = np.ones((2, 1, 2, 1), bool)
        elif cls == 7:
            delta = -2
            rowok = ~((a == 0) & (b == 1))
        else:
            delta = 2
            rowok = (a == 0) & (b == 1)
        dr = 2 * delta + a - b + 7
        ok = np.broadcast_to(colok & rowok & (dr >= 0) & (dr <= 14), (2, 64, 2, 64))
        drc = np.broadcast_to(np.clip(dr, 0, 14), (2, 64, 2, 64))
        cof = np.broadcast_to(coff, (2, 64, 2, 64))
        g = rpb[:, :, drc, cof]
        tab[:, :, cls] = np.where(ok[None, None], g, np.float32(NEG))
    tab = tab.reshape(L, 8, 9, 128, 128)
    return np.ascontiguousarray(tab.transpose(0, 3, 1, 2, 4))


def _gdn_masks():
    p = np.arange(128)
    same = (p[:, None] // 64) == (p[None, :] // 64)
    r, c = p[:, None], p[None, :]
    m = np.zeros((128, 2, 3, 128), np.float32)
    m[:, 0, 0] = np.where(same & (r < c), 0.0, NEG)
    m[:, 0, 1] = np.where(same & (r <= c), 0.0, NEG)
    m[:, 0, 2] = np.where(same & (c < r), 0.0, -NEG)
    m[:, 1, 0] = np.where(same & (r > c), 0.0, NEG)
    m[:, 1, 1] = np.where(same & (r >= c), 0.0, NEG)
    m[:, 1, 2] = np.where(same & (c > r), 0.0, -NEG)
    return m


def _consts():
    c = {}
    c["c_ident"] = np.eye(128, dtype=np.float32)
    p = np.arange(128)
    c["c_onesbd"] = ((p[:, None] // 64) == (p[None, :] // 64)).astype(np.float32)
    c["c_gmask"] = _gdn_masks().reshape(128, 6 * 128)
    sel = np.zeros((48, 16, 128), np.float32)
    for i in range(16):
        sel[i, i, :] = 1.0
        sel[32 + i, i, :] = 1.0
    c["c_sel"] = sel.reshape(48, 16 * 128)
    selr = np.zeros((128, 16, 128), np.float32)
    for d_ in range(2):
        for h in range(8):
            r = 32 * d_ + h
            selr[r, d_ * 8 + h, :] = 1.0
            selr[64 + r, d_ * 8 + h, :] = 1.0
    c["c_selr"] = selr.reshape(128, 16 * 128)
    selp = np.zeros((128, 8, 128), np.float32)
    for d_ in range(2):
        for hp in range(4):
            for hh in range(2):
                r = 32 * d_ + 2 * hp + hh
                selp[r, d_ * 4 + hp, 64 * hh:64 * hh + 64] = 1.0
                selp[64 + r, d_ * 4 + hp, 64 * hh:64 * hh + 64] = 1.0
    c["c_selp"] = selp.reshape(128, 8 * 128)
    rst = np.ones((128, 512), np.float32)
    rst[:, ::64] = 0.0
    c["c_reset"] = rst
    return c


class Cfg:
    def __init__(self, n_p, T_p, n_s, T_s, depth, taps=(), stop_after=None, skip_gdn=False, gdn_stage=9):
        self.gdn_stage = gdn_stage
        self.n_p, self.T_p, self.n_s, self.T_s, self.depth = n_p, T_p, n_s, T_s, depth
        self.seqs = [("p", i, T_p) for i in range(n_p)] + [("s", i, T_s) for i in range(n_s)]
        self.offs = []
        o = 0
        for (_, _, T) in self.seqs:
            self.offs.append(o)
            o += T
        self.NT = o
        self.NS = len(self.seqs)
        self.taps = tuple(taps)
        self.stop_after = stop_after
        self.skip_gdn = skip_gdn


class B:
    pass


def build_program(cfg):
    nc = bass.Bass("TRN2", target_bir_lowering=False)
    L = cfg.depth
    NT, NS = cfg.NT, cfg.NS
    g = B()
    g.nc, g.cfg = nc, cfg

    def din(name, shape, dt=F32):
        return nc.dram_tensor(name, list(shape), dt, kind="ExternalInput").ap()

    def dscr(name, shape, dt):
        kind = "ExternalOutput" if name in cfg.taps else "Internal"
        return nc.dram_tensor(name, list(shape), dt, kind=kind).ap()

    I = {}
    I["x_p"] = din("x_p", [max(cfg.n_p, 1), cfg.T_p, D])
    I["x_s"] = din("x_s", [max(cfg.n_s, 1), cfg.T_s, D])
    I["m_p"] = din("m_p", [max(cfg.n_p, 1), NMEM, D])
    I["m_s"] = din("m_s", [max(cfg.n_s, 1), NMEM, D])
    for nm, shp in [("norm_mix", [L, D]), ("w_in", [L, D, DIN]), ("conv_qkv", [L, 3, 1536]), ("a_log", [L, 16]),
                    ("dt_bias", [L, 16]), ("norm_o", [L, 64]), ("w_out", [L, D, D]), ("norm_x", [L, D]),
                    ("norm_mem", [L, D]), ("w_xq", [L, D, D]), ("w_xkv", [L, D, 2 * D]), ("w_xo", [L, D, D]),
                    ("norm_ffn", [L, D]), ("w_up", [L, D, 2 * DFF]), ("conv_ffn", [L, 3, 2 * DFF]),
                    ("conv_ffn_b", [L, 2 * DFF]), ("w_down", [L, DFF, D]), ("norm_final", [D]),
                    ("na_bias", [L, 128, 8 * 9 * 128]), ("c_ident", [128, 128]), ("c_onesbd", [128, 128]),
                    ("c_gmask", [128, 768]), ("c_sel", [48, 2048]), ("c_selr", [128, 2048]),
                    ("c_selp", [128, 1024]), ("c_reset", [128, 512])]:
        I[nm] = din(nm, shp)
    O = {}
    O["y_p"] = nc.dram_tensor("y_p", [max(cfg.n_p, 1), cfg.T_p, D], F32, kind="ExternalOutput").ap()
    O["y_s"] = nc.dram_tensor("y_s", [max(cfg.n_s, 1), cfg.T_s, D], F32, kind="ExternalOutput").ap()
    W = {}
    W["XT"] = dscr("XT", [D, NT], F32)
    W["MH"] = dscr("MH", [D, NS * NMEM], F32)
    W["QKN"] = dscr("QKN", [1024, NT], BF16)
    W["VN"] = dscr("VN", [NT, 512], BF16)
    W["QKVD"] = dscr("QKVD", [1536, NT], BF16)
    W["ZT"] = dscr("ZT", [512, NT], BF16)
    W["GR"] = dscr("GR", [32, NT], F32)
    W["YMIX"] = dscr("YMIX", [1024, NT], BF16)
    W["XN3"] = dscr("XN3", [1024, NT], BF16)
    g.I, g.O, g.W = I, O, W

    S = Sched(nc)
    g.S = S
    with ExitStack() as top:
        uniq = [0]

        def sbt(es, name, shape, dt=F32):
            uniq[0] += 1
            return es.enter_context(nc.sbuf_tensor("%s_%d" % (name, uniq[0]), list(shape), dt))

        def pst(es, name, shape, dt=F32):
            uniq[0] += 1
            return es.enter_context(nc.psum_tensor("%s_%d" % (name, uniq[0]), list(shape), dt))

        g.sbt, g.pst = sbt, pst
        g.IDB = sbt(top, "IDB", [128, 128], BF16)
        g.ONESB = sbt(top, "ONESB", [128, 128], BF16)
        g.ONESBD = sbt(top, "ONESBD", [128, 128], BF16)
        g.NORMS = sbt(top, "NORMS", [128, 3 * L + 1, KC], F32)
        g.NMEMG = sbt(top, "NMEMG", [128, L, KC], F32)
        g.rconst = S.res("const")
        S.dma("pool", lambda e: e.dma_start(out=g.IDB[:], in_=I["c_ident"]), "c0", writes=[g.rconst])
        S.dma("pool", lambda e: e.dma_start(out=g.ONESBD[:], in_=I["c_onesbd"]), "c1", writes=[g.rconst])
        S.op("dve", lambda e: e.memset(g.ONESB[:], 1.0), writes=[g.rconst])
        with nc.allow_non_contiguous_dma("tiny per-partition gain vectors"):
            k = 0
            for l in range(L):
                for j, nm in enumerate(("norm_mix", "norm_x", "norm_ffn")):
                    src = I[nm][l].rearrange("(kc p) -> p kc", p=128)
                    S.dma("sp", lambda e, src=src, idx=3 * l + j: e.dma_start(out=g.NORMS[:, idx, :], in_=src, allow_slow_non_contiguous=True),
                          "c2", writes=[g.rconst])
                src = I["norm_mem"][l].rearrange("(kc p) -> p kc", p=128)
                S.dma("sp", lambda e, src=src, l=l: e.dma_start(out=g.NMEMG[:, l, :], in_=src, allow_slow_non_contiguous=True), "c2", writes=[g.rconst])
            src = I["norm_final"].rearrange("(kc p) -> p kc", p=128)
            S.dma("sp", lambda e, src=src: e.dma_start(out=g.NORMS[:, 3 * L, :], in_=src, allow_slow_non_contiguous=True), "c2", writes=[g.rconst])

        phase0(g)
        if cfg.stop_after != "0":
            for l in range(L):
                phaseA(g, l)
                if cfg.stop_after == "A":
                    break
                phaseB(g, l)
                if cfg.stop_after == "B":
                    break
                phaseC(g, l)
                if cfg.stop_after == "C":
                    break
                phaseM(g, l)
                phaseD(g, l)
                if cfg.stop_after == "D":
                    break
                phaseE(g, l)
            if cfg.stop_after is None:
                phaseF(g)
        S.barrier()
        S.emit()
        S.close()
    g.n_inst = S.n_inst
    return nc, g


def MM(g, out, lhsT, rhs, start, stop, reads, writes):
    o = g.S.op("pe", lambda e: e.matmul(out, lhsT, rhs, start=start, stop=stop), reads, writes)
    o.mm_stop = bool(stop)
    return o


def ACT(g, out, in_, func, reads, writes, **kw):
    return g.S.op("act", lambda e: e.activation(out=out, in_=in_, func=func, **kw), reads, writes)


def TT_(g, eng, out, in0, in1, op, reads, writes):
    return g.S.op(eng, lambda e: e.tensor_tensor(out=out, in0=in0, in1=in1, op=op), reads, writes)


def TS_(g, eng, out, in0, s1, s2, op0, op1, reads, writes, **kw):
    if op1 is None and eng == "pool" and op0 == ALU.mult:
        op1, s2 = ALU.add, 0.0
    if op1 is None:
        return g.S.op(eng, lambda e: e.tensor_scalar(out=out, in0=in0, scalar1=s1, scalar2=None, op0=op0, **kw), reads, writes)
    return g.S.op(eng, lambda e: e.tensor_scalar(out=out, in0=in0, scalar1=s1, scalar2=s2, op0=op0, op1=op1, **kw), reads, writes)


def STT(g, out, in0, scalar, in1, op0, op1, reads, writes):
    return g.S.op("dve", lambda e: e.scalar_tensor_tensor(out=out, in0=in0, scalar=scalar, in1=in1, op0=op0, op1=op1), reads, writes)


def CP(g, eng, out, in_, reads, writes):
    if eng == "act":
        return g.S.op("act", lambda e: e.activation(out=out, in_=in_, func=AF.Copy), reads, writes)
    return g.S.op(eng, lambda e: e.tensor_copy(out=out, in_=in_), reads, writes)


def DMA(g, q, out, in_, semkey, reads, writes):
    return g.S.dma(q, lambda e: e.dma_start(out=out, in_=in_), semkey, reads, writes)


def x_src(g, si):
    grp, i, T = g.cfg.seqs[si]
    return (g.I["x_p"] if grp == "p" else g.I["x_s"])[i]


def y_dst(g, si):
    grp, i, T = g.cfg.seqs[si]
    return (g.O["y_p"] if grp == "p" else g.O["y_s"])[i]


def m_src(g, si):
    grp, i, T = g.cfg.seqs[si]
    return (g.I["m_p"] if grp == "p" else g.I["m_s"])[i]


def phase0(g):
    nc, S, cfg = g.nc, g.S, g.cfg
    S.barrier()
    with ExitStack() as es:
        NB = 2
        XL = [g.sbt(es, "p0_xl%d" % i, [128, D], F32) for i in range(NB)]
        HI = [g.sbt(es, "p0_hi%d" % i, [128, 3, D], BF16) for i in range(NB)]
        R1 = [g.sbt(es, "p0_r%d" % i, [128, D], F32) for i in range(NB)]
        ST = [g.sbt(es, "p0_st%d" % i, [128, KC, 512], F32) for i in range(2)]
        SS = g.sbt(es, "p0_ss", [128, 4], F32)
        PS = [g.pst(es, "p0_ps%d" % i, [128, 512], F32) for i in range(4)]
        rXL = [S.res() for _ in range(NB)]
        rHI = [S.res() for _ in range(NB)]
        rR1 = [S.res() for _ in range(NB)]
        rST = [S.res() for _ in range(2)]
        rSS = S.res()
        rPS = [S.pres() for _ in range(4)]
        rdram = S.res()
        cnt = [0]

        def do_tile(src_rows, dst, dst_col0, slot_in_group, stg, is_mem):
            b = cnt[0] % NB
            cnt[0] += 1
            DMA(g, "sp", XL[b][:], src_rows, "p0l%d" % b, [], [rXL[b]])
            xin = XL[b]
            if is_mem:
                ACT(g, R1[b][:], XL[b][:], AF.Square, [rXL[b]], [rR1[b], rSS], accum_out=SS[:, 0:1])
                ACT(g, SS[:, 1:2], SS[:, 0:1], AF.Sqrt, [rSS], [rSS], scale=1.0 / D, bias=EPS)
                S.op("dve", lambda e: e.reciprocal(out=SS[:, 2:3], in_=SS[:, 1:2]), [rSS], [rSS])
                TS_(g, "dve", XL[b][:], XL[b][:], SS[:, 2:3], None, ALU.mult, None, [rXL[b], rSS], [rXL[b]])
            CP(g, "act", HI[b][:, 0, :], xin[:], [rXL[b]], [rHI[b]])
            TT_(g, "dve", R1[b][:], xin[:], HI[b][:, 0, :], ALU.subtract, [rXL[b], rHI[b]], [rR1[b]])
            CP(g, "act", HI[b][:, 1, :], R1[b][:], [rR1[b]], [rHI[b]])
            TT_(g, "dve", R1[b][:], R1[b][:], HI[b][:, 1, :], ALU.subtract, [rR1[b], rHI[b]], [rR1[b]])
            CP(g, "act", HI[b][:, 2, :], R1[b][:], [rR1[b]], [rHI[b]])
            for half in range(2):
                pi = (2 * cnt[0] + half) % 4
                for q in range(4):
                    fc = half * 4 + q
                    for part in range(3):
                        MM(g, PS[pi][:, q * 128:(q + 1) * 128], HI[b][:, part, fc * 128:(fc + 1) * 128], g.IDB[:],
                           part == 0, part == 2, [rHI[b], g.rconst], [rPS[pi]])
                outv = ST[stg][:, half * 4:(half + 1) * 4, slot_in_group * 128:(slot_in_group + 1) * 128]
                inv = PS[pi][:].rearrange("p (q t) -> p q t", q=4)
                if half == 0:
                    CP(g, "act", outv, inv, [rPS[pi]], [rST[stg]])
                else:
                    CP(g, "dve", outv, inv, [rPS[pi]], [rST[stg]])

        grp_i = 0
        for si, (grp, i, T) in enumerate(cfg.seqs):
            xs = x_src(g, si)
            for t4 in range(T // 512):
                stg = grp_i % 2
                grp_i += 1
                for s in range(4):
                    r0 = t4 * 512 + s * 128
                    do_tile(xs[r0:r0 + 128, :], None, None, s, stg, False)
                col0 = cfg.offs[si] + t4 * 512
                dst = g.W["XT"].rearrange("(kc p) t -> p kc t", p=128)[:, :, col0:col0 + 512]
                DMA(g, "act", dst, ST[stg][:], "p0s%d" % stg, [rST[stg]], [rdram])
        for si in range(cfg.NS):
            ms = m_src(g, si)
            stg = grp_i % 2
            grp_i += 1
            for s in range(2):
                do_tile(ms[s * 128:(s + 1) * 128, :], None, None, s, stg, True)
            col0 = si * NMEM
            dst = g.W["MH"].rearrange("(kc p) t -> p kc t", p=128)[:, :, col0:col0 + 256]
            DMA(g, "act", dst, ST[stg][:, :, 0:256], "p0s%d" % stg, [rST[stg]], [rdram])
        S.barrier()


def rmsnorm_fm(g, XF, rXF, XN, rXN, SQ, rSQ, RSTD, rRSTD, PSs, rPSs, gain_idx, ncols, col0=0, engs=("dve", "pool")):
    ACT(g, SQ[:, :, 0:ncols], XF[:, :, 0:ncols], AF.Square, [rXF], [rSQ])
    for kc in range(KC):
        MM(g, PSs[:, 0:ncols], g.ONESB[:], SQ[:, kc, 0:ncols], kc == 0, kc == KC - 1, [rSQ, g.rconst], [rPSs])
    ACT(g, RSTD[:, 0:ncols], PSs[:, 0:ncols], AF.Ln, [rPSs], [rRSTD], scale=1.0 / D, bias=EPS)
    ACT(g, RSTD[:, 0:ncols], RSTD[:, 0:ncols], AF.Exp, [rRSTD], [rRSTD], scale=-0.5)
    for kc in range(KC):
        STT(g, XN[:, kc, col0:col0 + ncols], XF[:, kc, 0:ncols], g.NORMS[:, gain_idx, kc:kc + 1], RSTD[:, 0:ncols],
            ALU.mult, ALU.mult, [rXF, rRSTD, g.rconst], [rXN])


def phaseA(g, l):
    nc, S, cfg = g.nc, g.S, g.cfg
    S.barrier()
    with ExitStack() as es:
        WIN = g.sbt(es, "A_win", [128, KC, DIN], BF16)
        rW = S.res()
        wsrc = g.I["w_in"][l].rearrange("(kc p) n -> p kc n", p=128)
        for kc in range(KC):
            DMA(g, "pool", WIN[:, kc, :], wsrc[:, kc, :], "A_w%d" % (kc % 4), [], [rW])
        XF = [g.sbt(es, "A_xf%d" % i, [128, KC, TT], F32) for i in range(2)]
        rXF = [S.res() for _ in range(2)]
        SQ = g.sbt(es, "A_sq", [128, KC, TT], BF16)
        rSQ = S.res()
        RSTD = g.sbt(es, "A_rstd", [128, TT], F32)
        rRSTD = S.res()
        XN = [g.sbt(es, "A_xn%d" % i, [128, KC, TT], BF16) for i in range(2)]
        rXN = [S.res() for _ in range(2)]
        STG = [g.sbt(es, "A_stg%d" % i, [128, 4, TT], BF16) for i in range(3)]
        rSTG = [S.res() for _ in range(3)]
        GST = [g.sbt(es, "A_gst%d" % i, [32, TT], F32) for i in range(2)]
        rGST = [S.res() for _ in range(2)]
        PSs = g.pst(es, "A_pss", [128, TT], F32)
        rPSs = S.pres()
        PS = [g.pst(es, "A_ps%d" % i, [128, TT], F32) for i in range(6)]
        rPS = [S.pres() for _ in range(6)]
        rdram = S.res()
        ntile = cfg.NT // TT
        psi = [0]
        stgi = [0]
        XTv = g.W["XT"].rearrange("(kc p) t -> p kc t", p=128)
        def prepart(ti):
            c0 = ti * TT
            b = ti % 2
            DMA(g, "sp", XF[b][:], XTv[:, :, c0:c0 + TT], "A_x%d" % b, [], [rXF[b]])
            rmsnorm_fm(g, XF[b], rXF[b], XN[b], rXN[b], SQ, rSQ, RSTD, rRSTD, PSs, rPSs, 3 * l + 0, TT)

        def mainpart(ti):
            c0 = ti * TT
            b = ti % 2
            groups = [(g.W["QKN"], 0, 0, 4, 0.125), (g.W["QKN"], 512, 512, 4, None),
                      (g.W["QKVD"], 0, 1536, 4, None), (g.W["QKVD"], 512, 2048, 4, None), (g.W["QKVD"], 1024, 2560, 4, None),
                      (g.W["ZT"], 0, 3072, 4, None)]
            for (dst, r0, wc0, nm, scale) in groups:
                sg = stgi[0] % 3
                stgi[0] += 1
                for m in range(nm):
                    p = psi[0] % 6
                    psi[0] += 1
                    for kc in range(KC):
                        MM(g, PS[p][:], WIN[:, kc, wc0 + m * 128: wc0 + (m + 1) * 128], XN[b][:, kc, :], kc == 0, kc == KC - 1,
                           [rW, rXN[b]], [rPS[p]])
                    if scale is not None:
                        ACT(g, STG[sg][:, m, :], PS[p][:], AF.Copy, [rPS[p]], [rSTG[sg]], scale=scale)
                    elif m % 2 == 0:
                        CP(g, "act", STG[sg][:, m, :], PS[p][:], [rPS[p]], [rSTG[sg]])
                    else:
                        CP(g, "dve", STG[sg][:, m, :], PS[p][:], [rPS[p]], [rSTG[sg]])
                d = dst[r0:r0 + 512, c0:c0 + TT].rearrange("(m p) t -> p m t", p=128)
                DMA(g, "sp", d, STG[sg][:], "A_st%d" % sg, [rSTG[sg]], [rdram])
            p = psi[0] % 6
            psi[0] += 1
            for kc in range(KC):
                MM(g, PS[p][0:32, :], WIN[:, kc, 3584:3616], XN[b][:, kc, :], kc == 0, kc == KC - 1, [rW, rXN[b]], [rPS[p]])
            gb = ti % 2
            CP(g, "dve", GST[gb][:], PS[p][0:32, :], [rPS[p]], [rGST[gb]])
            DMA(g, "sp", g.W["GR"][:, c0:c0 + TT], GST[gb][:], "A_g%d" % gb, [rGST[gb]], [rdram])
            sg = stgi[0] % 3
            stgi[0] += 1
            for ts in range(4):
                p = psi[0] % 6
                psi[0] += 1
                for kc in range(KC):
                    MM(g, PS[p][:], XN[b][:, kc, ts * 128:(ts + 1) * 128], WIN[:, kc, 1024:1536], kc == 0, kc == KC - 1,
                       [rW, rXN[b]], [rPS[p]])
                CP(g, "act" if ts % 2 == 0 else "dve", STG[sg][:, ts, :], PS[p][:], [rPS[p]], [rSTG[sg]])
            d = g.W["VN"][c0:c0 + TT, :].rearrange("(s p) f -> p s f", p=128)
            DMA(g, "sp", d, STG[sg][:], "A_st%d" % sg, [rSTG[sg]], [rdram])

        def cap(fn, *a):
            S.capture()
            fn(*a)
            return S.end_capture()

        S.extend(cap(prepart, 0))
        for ti in range(ntile):
            lists = [cap(mainpart, ti)]
            if ti + 1 < ntile:
                lists.append(cap(prepart, ti + 1))
            S.extend(interleave(*lists))
        S.barrier()


def phaseB(g, l):
    nc, S, cfg = g.nc, g.S, g.cfg
    S.barrier()
    Tmax = max(T for (_, _, T) in cfg.seqs)
    with ExitStack() as es:
        BIAS = g.sbt(es, "B_bias", [128, 8, 9, 128], BF16)
        rBIAS = S.res()
        bsrc = g.I["na_bias"][l].rearrange("p (h c q) -> p h c q", h=8, c=9)
        for h in range(8):
            DMA(g, "pool", BIAS[:, h], bsrc[:, h], "B_b%d" % (h % 2), [], [rBIAS])
        NB = 2
        QT = [g.sbt(es, "B_qt%d" % i, [128, Tmax], BF16) for i in range(NB)]
        KT = [g.sbt(es, "B_kt%d" % i, [128, Tmax], BF16) for i in range(NB)]
        V = [g.sbt(es, "B_v%d" % i, [128, Tmax // 128, 2, 65], BF16) for i in range(NB)]
        YT = [g.sbt(es, "B_yt%d" % i, [128, Tmax], BF16) for i in range(NB)]
        rQT = [S.res() for _ in range(NB)]
        rKT = [S.res() for _ in range(NB)]
        rV = [S.res() for _ in range(NB)]
        rYT = [S.res() for _ in range(NB)]
        for i in range(NB):
            S.op("pool", lambda e, i=i: e.memset(V[i][:], 1.0), [], [rV[i]])
        NP_ = 2
        PT = [g.sbt(es, "B_pt%d" % i, [128, 5, 2, 128], BF16) for i in range(NP_)]
        rPT = [S.res() for _ in range(NP_)]
        QBD = [g.sbt(es, "B_qbd%d" % i, [128, 2, 128], BF16) for i in range(2)]
        rQBD = [S.res() for _ in range(2)]
        for i in range(2):
            S.op("pool", lambda e, i=i: e.memset(QBD[i][:], 0.0), [], [rQBD[i]])
        YTM = [g.sbt(es, "B_ytm%d" % i, [128, 128], BF16) for i in range(2)]
        rYTM = [S.res() for _ in range(2)]
        RC = [g.sbt(es, "B_rc%d" % i, [128, 2], F32) for i in range(2)]
        rRC = [S.res() for _ in range(2)]
        PSS = [g.pst(es, "B_pss%d" % i, [128, 1536], F32) for i in range(2)]
        rPSS = [S.pres() for _ in range(2)]
        PSO = g.pst(es, "B_pso", [128, 512], F32)
        rPSO = S.pres()
        PSTr = g.pst(es, "B_pstr", [128, 512], F32)
        rPSTr = S.pres()
        rdram = S.res()
        it = 0
        cnt = 0
        units = [(si_, hp_) for si_ in range(len(cfg.seqs)) for hp_ in range(4)]

        def load_unit(u):
            si_, hp_ = units[u]
            T_ = cfg.seqs[si_][2]
            c0_ = cfg.offs[si_]
            b_ = u % NB
            DMA(g, "sp", QT[b_][:, 0:T_], g.W["QKN"][hp_ * 128:(hp_ + 1) * 128, c0_:c0_ + T_], "B_q%d" % b_, [], [rQT[b_]])
            DMA(g, "sp", KT[b_][:, 0:T_], g.W["QKN"][512 + hp_ * 128:512 + (hp_ + 1) * 128, c0_:c0_ + T_], "B_k%d" % b_, [], [rKT[b_]])
            vsrc = g.W["VN"][c0_:c0_ + T_, hp_ * 128:(hp_ + 1) * 128].rearrange("(t p) (h e) -> p t h e", p=128, h=2)
            for hh_ in range(2):
                DMA(g, "pool", V[b_][:, 0:T_ // 128, hh_, 0:64], vsrc[:, :, hh_, :], "B_v%d%d" % (b_, hh_), [], [rV[b_]])

        load_unit(0)
        for si, (grp, sidx, T) in enumerate(cfg.seqs):
            pairs = _na_pairs(T)
            c0 = cfg.offs[si]
            ntl = T // 128
            for hp in range(4):
                b = it % NB
                it += 1
                if it < len(units):
                    load_unit(it)
                for i in range(ntl):
                    yb = i % 2
                    lst = pairs[i]
                    nt = len(lst)
                    sp_ = cnt % 2
                    pp = cnt % NP_
                    cnt += 1
                    qs = slice(i * 128, (i + 1) * 128)
                    for hh in range(2):
                        b_ = slice(64 * hh, 64 * hh + 64)
                        CP(g, "pool", QBD[sp_][b_, hh, :], QT[b][b_, qs], [rQT[b]], [rQBD[sp_]])
                    PSv = PSS[sp_][:].rearrange("p (t h q) -> p t h q", h=2, q=128)
                    for jt, (j, cls) in enumerate(lst):
                        MM(g, PSS[sp_][:, jt * 256:(jt + 1) * 256], KT[b][:, j * 128:(j + 1) * 128],
                           QBD[sp_][:].rearrange("p h q -> p (h q)"), True, False, [rKT[b], rQBD[sp_]], [rPSS[sp_]])
                        MM(g, PSv[:, jt], g.IDB[:], BIAS[:, 2 * hp:2 * hp + 2, cls, :], False, True,
                           [g.rconst, rBIAS], [rPSS[sp_]])
                    n1 = min(nt, 4)
                    ACT(g, PT[pp][:, 0:n1].rearrange("p t h q -> p (t h q)"), PSS[sp_][:, 0:n1 * 256], AF.Exp, [rPSS[sp_]], [rPT[pp]])
                    if nt > 4:
                        ACT(g, PT[pp][:, 4].rearrange("p h q -> p (h q)"), PSS[sp_][:, 1024:1280], AF.Exp, [rPSS[sp_]], [rPT[pp]])
                    for hh in range(2):
                        for jt, (j, cls) in enumerate(lst):
                            MM(g, PSO[:, hh * 128:hh * 128 + 65], PT[pp][:, jt, hh, :], V[b][:, j, hh, :], jt == 0, jt == nt - 1,
                               [rPT[pp], rV[b]], [rPSO])
                    PSOv = PSO[:, 0:256].rearrange("p (h e) -> p h e", h=2)
                    rc = sp_
                    S.op("dve", lambda e, rc=rc, PSOv=PSOv: e.reciprocal(out=RC[rc][:], in_=PSOv[:, :, 64]), [rPSO], [rRC[rc]])
                    TT_(g, "dve", YTM[yb][:].rearrange("p (h e) -> p h e", h=2), PSOv[:, :, 0:64],
                        RC[rc][:, :, None].to_broadcast([128, 2, 64]), ALU.mult, [rPSO, rRC[rc]], [rYTM[yb]])
                    MM(g, PSTr[:, 0:128], YTM[yb][:], g.IDB[:], True, True, [rYTM[yb], g.rconst], [rPSTr])
                    CP(g, "act", YT[b][:, i * 128:(i + 1) * 128], PSTr[:, 0:128], [rPSTr], [rYT[b]])
                DMA(g, "sp", g.W["YMIX"][hp * 128:(hp + 1) * 128, c0:c0 + T], YT[b][:, 0:T], "B_y%d" % b, [rYT[b]], [rdram])
        S.barrier()


def phaseC(g, l):
    if g.cfg.skip_gdn:
        nc, S, cfg = g.nc, g.S, g.cfg
        S.barrier()
        with ExitStack() as es:
            Z = g.sbt(es, "C_z", [128, 4, 2048], BF16)
            rZ, rd = S.res(), S.res()
            S.op("dve", lambda e: e.memset(Z[:], 0.0), [], [rZ])
            for c0 in range(0, cfg.NT, 2048):
                w = min(2048, cfg.NT - c0)
                DMA(g, "sp", g.W["YMIX"][512:1024, c0:c0 + w].rearrange("(m p) t -> p m t", p=128), Z[:, :, 0:w], "C_z", [rZ], [rd])
            S.barrier()
        return
    phaseC_real(g, l)


def phaseC_real(g, l):
    nc, S, cfg = g.nc, g.S, g.cfg
    S.barrier()
    Tmax = max(T for (_, _, T) in cfg.seqs)
    NTLmax = Tmax // 128
    NCHmax = Tmax // 64
    GB = 256
    CB = 512
    I, W = g.I, g.W
    with ExitStack() as es:
        sb = lambda n, shp, dt=F32: g.sbt(es, "C_" + n, shp, dt)
        MASK = sb("mask", [128, 2, 3, 128], BF16)
        SELR = sb("selr", [128, 16, 128], BF16)
        SELP = sb("selp", [128, 8, 128], BF16)
        IDF = sb("idf", [128, 128], F32)
        RESET = sb("reset", [128, GB], F32)
        CWQ = sb("cwq", [128, 12, 3], F32)
        NORMO = sb("normo", [128, 1], F32)
        DTB = sb("dtb", [40, 1], F32)
        NEGA = sb("nega", [40, 1], F32)
        rC = S.res()
        DMA(g, "pool", MASK[:].rearrange("p a b c -> p (a b c)"), I["c_gmask"], "C_c0", [], [rC])
        DMA(g, "pool", SELR[:].rearrange("p a b -> p (a b)"), I["c_selr"], "C_c1", [], [rC])
        DMA(g, "pool", SELP[:].rearrange("p a b -> p (a b)"), I["c_selp"], "C_c0", [], [rC])
        DMA(g, "sp", IDF[:], I["c_ident"], "C_c2", [], [rC])
        DMA(g, "sp", RESET[:], I["c_reset"][:, 0:GB], "C_c2", [], [rC])
        for k in range(3):
            small_vec_load(g, "sp", CWQ[:, :, k], I["conv_qkv"][l, k].rearrange("(f p) -> p f", p=128), "C_c2", [rC])
        for hh in range(2):
            small_vec_load(g, "sp", NORMO[64 * hh:64 * hh + 64, :], I["norm_o"][l].rearrange("(p o) -> p o", o=1), "C_c2", [rC])
        S.op("dve", lambda e: e.memset(DTB[:], 0.0), [], [rC])
        S.op("dve", lambda e: e.memset(NEGA[:], 0.0), [], [rC])
        for d_ in range(2):
            small_vec_load(g, "sp", DTB[32 * d_:32 * d_ + 8, :], I["dt_bias"][l, 8 * d_:8 * d_ + 8].rearrange("(p o) -> p o", o=1), "C_c2", [rC])
            small_vec_load(g, "sp", NEGA[32 * d_:32 * d_ + 8, :], I["a_log"][l, 8 * d_:8 * d_ + 8].rearrange("(p o) -> p o", o=1), "C_c2", [rC])
        ACT(g, NEGA[:], NEGA[:], AF.Exp, [rC], [rC])
        TS_(g, "dve", NEGA[:], NEGA[:], -1.0, None, ALU.mult, None, [rC], [rC])

        GCS = sb("gcs", [128, Tmax], BF16)
        HS = sb("hs", [128, Tmax], BF16)
        GLS = sb("gls", [128, NCHmax], BF16)
        SCT = sb("sct", [128, NTLmax, 5, 16], F32)
        rGCS, rHS, rGLS, rSCT = S.res(), S.res(), S.res(), S.res()
        for t_ in (GCS, HS, GLS):
            S.op("pool", lambda e, t_=t_: e.memset(t_[:], 0.0), [], [rGCS, rHS, rGLS])
        AR = sb("ar", [40, GB]); BR = sb("br", [40, GB])
        rAR, rBR = S.res(), S.res()
        S.op("pool", lambda e: e.memset(AR[:], 0.0), [], [rAR])
        S.op("pool", lambda e: e.memset(BR[:], 0.0), [], [rBR])
        G_ = sb("g", [40, GB]); PRE = sb("pre", [40, GB]); GC = sb("gc", [40, GB]); LNB = sb("lnb", [40, GB])
        TMPG = sb("tmpg", [40, GB])
        RR = sb("rr", [40, 5, GB])
        RH = sb("rh", [40, 5, GB], BF16); RL = sb("rl", [40, 5, GB], BF16)
        GLF = sb("glf", [40, GB // 64])
        rG = S.res()
        S.op("dve", lambda e: e.memset(GC[:], 0.0), [], [rG])

        XR_A = [sb("xr%d" % i, [128, 3, CB + 2], BF16) for i in range(2)]
        ACC_A = [[sb("acc%d_%d" % (p_, i), [128, CB]) for i in range(3)] for p_ in range(2)]
        SQB_A = [[sb("sqb%d_%d" % (p_, i), [128, CB], BF16) for i in range(2)] for p_ in range(2)]
        RSB_A = [[sb("rsb%d_%d" % (p_, i), [128, CB]) for i in range(2)] for p_ in range(2)]
        QT = sb("qt", [128, Tmax], BF16); KT = sb("kt", [128, Tmax], BF16); VT = sb("vt", [128, Tmax], BF16)
        KTZ = sb("ktz", [128, 2, Tmax], BF16)
        rKTZ = S.res()
        S.op("pool", lambda e: e.memset(KTZ[:], 0.0), [], [rKTZ])
        SZ = sb("sz", [128, Tmax], BF16)
        YT = sb("yt", [128, Tmax], BF16)
        OF = sb("of", [128, NTLmax, 128])
        GLB = sb("glb", [128, 2, NCHmax])
        rQT, rKT, rVT, rSZ, rYT, rOF, rGLB = [S.res() for _ in range(7)]
        rXR_A = [S.res() for _ in range(2)]
        rACC_A = [[S.res() for _ in range(3)] for _ in range(2)]
        rSQB_A = [[S.res() for _ in range(2)] for _ in range(2)]
        rRSB_A = [[S.res() for _ in range(2)] for _ in range(2)]
        qcnt = [0]
        KVTM = [sb("kvtm%d" % i, [128, 2, 128], BF16) for i in range(4)]
        E01 = [sb("e01%d" % i, [128, 2, 256]) for i in range(4)]
        E2 = [sb("e2%d" % i, [128, 2, 128]) for i in range(4)]
        EGC = [sb("egc%d" % i, [128, 128]) for i in range(4)]
        PB_A = [[sb("pb%d_%d" % (d_, i), [128, 2, 128], BF16) for i in range(2)] for d_ in range(2)]
        PTY_A = [[sb("pty%d_%d" % (d_, i), [128, 2, 2, 128], BF16) for i in range(2)] for d_ in range(2)]
        YF_A = [sb("yf%d" % d_, [128, 2, 128]) for d_ in range(2)]
        TB = [sb("tb%d" % i, [128, 2, 128], BF16) for i in range(4)]
        TBG = [sb("tbg%d" % i, [128, 2, 128], BF16) for i in range(4)]
        U = [sb("u%d" % i, [128, 128]) for i in range(4)]
        WT = [sb("wt%d" % i, [128, 2, 128], BF16) for i in range(4)]
        QGT = [sb("qgt%d" % i, [128, 2, 128], BF16) for i in range(4)]
        QKT = [sb("qkt%d" % i, [128, 2, 128], BF16) for i in range(4)]
        KGZ = [sb("kgz%d" % i, [128, 2, 128], BF16) for i in range(4)]
        rKVTM, rE01, rE2, rEGC, rTB, rTBG, rU, rWT, rQGT, rQKT, rKGZ = [[S.res() for _ in range(4)] for _ in range(11)]
        rPB_A = [[S.res() for _ in range(2)] for _ in range(2)]
        rPTY_A = [[S.res() for _ in range(2)] for _ in range(2)]
        rYF_A = [S.res() for _ in range(2)]
        for i in range(4):
            S.op("pool", lambda e, i=i: e.memset(KGZ[i][:], 0.0), [], [rKGZ[i]])
            S.op("pool", lambda e, i=i: e.memset(WT[i][:], 0.0), [], [rWT[i]])
            S.op("pool", lambda e, i=i: e.memset(QGT[i][:], 0.0), [], [rQGT[i]])
        SF_A = [sb("sf%d" % d_, [128, 64]) for d_ in range(2)]
        SBs_A = [sb("sbs%d" % d_, [128, 64], BF16) for d_ in range(2)]
        rSF_A = [S.res() for _ in range(2)]
        rSBs_A = [S.res() for _ in range(2)]
        VNEW_A = [[sb("vnew%d_%d" % (d_, i), [128, 128], BF16) for i in range(2)] for d_ in range(2)]
        rVNEW_A = [[S.res() for _ in range(2)] for _ in range(2)]
        OT = [sb("ot%d" % i, [128, 128]) for i in range(2)]
        ON = [sb("on%d" % i, [128, 128], BF16) for i in range(2)]
        SSQ = [sb("ssq%d" % i, [128, 4]) for i in range(2)]
        JUNK = sb("junk", [128, 64])
        rOT = [S.res() for _ in range(2)]
        rON = [S.res() for _ in range(2)]
        rSSQ = [S.res() for _ in range(2)]
        rJ = S.res()
        PD = g.pst(es, "C_pd", [128, 512]); rPD = S.pres()
        PG = g.pst(es, "C_pg", [128, 512]); rPG = S.pres()
        PX, rPX = PD, rPD
        PA_A = [g.pst(es, "C_pa%d" % d_, [128, 512]) for d_ in range(2)]
        rPA_A = [S.pres() for _ in range(2)]
        PBk_A = [g.pst(es, "C_pbk%d" % d_, [128, 512]) for d_ in range(2)]
        rPBk_A = [S.pres() for _ in range(2)]
        CH_A = [g.pst(es, "C_ch%d" % d_, [128, 512]) for d_ in range(2)]
        rCH_A = [S.pres() for _ in range(2)]
        rdram = S.res()
        IDB = g.IDB
        rI = g.rconst

        def row(d_, h):
            return 32 * d_ + h

        def gate_prep(si, T, c0):
            ntl = T // 128
            for gb in range(T // GB):
                cg = c0 + gb * GB
                for d_ in range(2):
                    DMA(g, "sp", AR[32 * d_:32 * d_ + 8, :], W["GR"][16 + 8 * d_:24 + 8 * d_, cg:cg + GB], "C_ga", [], [rAR])
                    DMA(g, "sp", BR[32 * d_:32 * d_ + 8, :], W["GR"][8 * d_:8 + 8 * d_, cg:cg + GB], "C_gb", [], [rBR])
                R = [rG, rC]
                TS_(g, "dve", G_[:], AR[:], DTB[:, 0:1], None, ALU.add, None, [rAR] + R, [rG])
                ACT(g, G_[:], G_[:], AF.Exp, R, [rG])
                ACT(g, G_[:], G_[:], AF.Ln, R, [rG], bias=1.0)
                TS_(g, "dve", G_[:], G_[:], NEGA[:, 0:1], None, ALU.mult, None, R, [rG])
                S.op("dve", lambda e: e.tensor_tensor_scan(out=PRE[:], data0=RESET[0:40, :], data1=G_[:], initial=0.0,
                                                           op0=ALU.mult, op1=ALU.add), R, [rG])
                totb = PRE[:].rearrange("p (n c) -> p n c", c=64)[:, :, 63:64].to_broadcast([40, GB // 64, 64])
                v3 = lambda t_: t_[:].rearrange("p (n c) -> p n c", c=64)
                CP(g, "dve", GC[0:8, :], PRE[0:8, :], R, [rG])
                TT_(g, "dve", v3(TMPG)[32:40], totb[32:40], v3(PRE)[32:40], ALU.subtract, R, [rG])
                TT_(g, "dve", GC[32:40, :], TMPG[32:40, :], G_[32:40, :], ALU.add, R, [rG])
                ACT(g, RR[:, 2, :], BR[:], AF.Sigmoid, [rBR] + R, [rG])
                ACT(g, LNB[:], RR[:, 2, :], AF.Ln, R, [rG])
                TS_(g, "dve", RR[:, 0, :], GC[:], -1.0, None, ALU.mult, None, R, [rG])
                TT_(g, "dve", RR[:, 1, :], GC[:], LNB[:], ALU.add, R, [rG])
                ACT(g, TMPG[:], GC[:], AF.Exp, R, [rG])
                TT_(g, "dve", RR[:, 3, :], RR[:, 2, :], TMPG[:], ALU.mult, R, [rG])
                TT_(g, "dve", v3(TMPG), totb, v3(GC), ALU.subtract, R, [rG])
                ACT(g, RR[:, 4, :], TMPG[:], AF.Exp, R, [rG])
                ACT(g, GLF[:], PRE[:].rearrange("p (n c) -> p n c", c=64)[:, :, 63], AF.Exp, R, [rG])
                CP(g, "act", RH[:], RR[:], R, [rG])
                TT_(g, "dve", RL[:], RR[:], RH[:], ALU.subtract, R, [rG])
                cs = slice(gb * GB, (gb + 1) * GB)
                CP(g, "act", GCS[0:40, cs], GC[:], R, [rGCS])
                TT_(g, "dve", GCS[64:104, cs], GC[:], GCS[0:40, cs], ALU.subtract, R + [rGCS], [rGCS])
                CP(g, "act", HS[0:40, cs], RR[:, 1, :], R, [rHS])
                TT_(g, "dve", HS[64:104, cs], RR[:, 1, :], HS[0:40, cs], ALU.subtract, R + [rHS], [rHS])
                ns = slice(gb * (GB // 64), (gb + 1) * (GB // 64))
                CP(g, "act", GLS[0:40, ns], GLF[:], R, [rGLS])
                TT_(g, "dve", GLS[64:104, ns], GLF[:], GLS[0:40, ns], ALU.subtract, R + [rGLS], [rGLS])
                for t4 in range(GB // 128):
                    m = gb * (GB // 128) + t4
                    for k in range(5):
                        MM(g, PD[:, k * 40:(k + 1) * 40], RH[:, k, t4 * 128:(t4 + 1) * 128], IDB[0:40, 0:40], True, False, [rG, rI], [rPD])
                        MM(g, PD[:, k * 40:(k + 1) * 40], RL[:, k, t4 * 128:(t4 + 1) * 128], IDB[0:40, 0:40], False, True, [rG, rI], [rPD])
                    PDv = PD[:, 0:200].rearrange("p (k r) -> p k r", r=40)
                    CP(g, "dve", SCT[:, m, :, 0:8], PDv[:, :, 0:8], [rPD], [rSCT])
                    CP(g, "act", SCT[:, m, :, 8:16], PDv[:, :, 32:40], [rPD], [rSCT])

        def sc(m, k, d_, h):
            c = 8 * d_ + h
            return SCT[:, m, k, c:c + 1]

        def qkv_prep(si, T, c0, hp):
            nb = T // CB
            pars = []
            for cb in range(nb):
                pars.append(qcnt[0] % 2)
                qcnt[0] += 1

            def partA(cb):
                par = pars[cb]
                XR, rXR = XR_A[par], rXR_A[par]
                ACC, rACC = ACC_A[par], rACC_A[par]
                cc = c0 + cb * CB
                lo = 0 if cb > 0 else 1
                hi = CB + 2 if cb < nb - 1 else CB + 1
                if lo == 1:
                    S.op("pool", lambda e, XR=XR: e.memset(XR[:, :, 0:1], 0.0), [], [rXR])
                if hi == CB + 1:
                    S.op("pool", lambda e, XR=XR: e.memset(XR[:, :, CB + 1:CB + 2], 0.0), [], [rXR])
                for j in range(3):
                    r0 = j * 512 + hp * 128
                    DMA(g, "sp" if j != 1 else "act", XR[:, j, lo:hi], W["QKVD"][r0:r0 + 128, cc - 1 + lo:cc - 1 + hi], "C_xr%d_%d" % (j, par), [], [rXR])
                for j in range(3):
                    f = j * 4 + hp
                    ACT(g, ACC[j][:], XR[:, j, 0:CB], AF.Copy, [rXR, rC], [rACC[j]], scale=CWQ[:, f, 0:1])
                for tap in (1, 2):
                    for j in range(3):
                        f = j * 4 + hp
                        STT(g, ACC[j][:], XR[:, j, tap:CB + tap], CWQ[:, f, tap:tap + 1], ACC[j][:], ALU.mult, ALU.add,
                            [rXR, rC, rACC[j]], [rACC[j]])

            def partB(cb):
                par = pars[cb]
                ACC, rACC, SQB, rSQB, RSB, rRSB = ACC_A[par], rACC_A[par], SQB_A[par], rSQB_A[par], RSB_A[par], rRSB_A[par]
                cs = slice(cb * CB, (cb + 1) * CB)
                ACT(g, VT[:, cs], ACC[2][:], AF.Silu, [rACC[2]], [rVT])
                for j in range(2):
                    ACT(g, ACC[j][:], ACC[j][:], AF.Silu, [rACC[j]], [rACC[j]])
                for j in range(2):
                    ACT(g, SQB[j][:], ACC[j][:], AF.Square, [rACC[j]], [rSQB[j]])
                PSj = [(PG, rPG), (PD, rPD)]
                for j in range(2):
                    S.begin_atomic()
                    MM(g, PSj[j][0][:, 0:CB], g.ONESBD[:], SQB[j][:], True, True, [rSQB[j], rI], [PSj[j][1]])
                    ACT(g, RSB[j][:], PSj[j][0][:, 0:CB], AF.Ln, [PSj[j][1]], [rRSB[j]], bias=1e-6, scale=1.0)
                    S.end_atomic()
                for j in range(2):
                    ACT(g, RSB[j][:], RSB[j][:], AF.Exp, [rRSB[j]], [rRSB[j]], scale=-0.5)
                STT(g, QT[:, cs], ACC[0][:], 0.125, RSB[0][:], ALU.mult, ALU.mult, [rACC[0], rRSB[0]], [rQT])
                TT_(g, "dve", KT[:, cs], ACC[1][:], RSB[1][:], ALU.mult, [rACC[1], rRSB[1]], [rKT])
                for hh in range(2):
                    b_ = slice(64 * hh, 64 * hh + 64)
                    TT_(g, "pool", KTZ[b_, hh, cs], ACC[1][b_, :], RSB[1][b_, :], ALU.mult, [rACC[1], rRSB[1]], [rKTZ])

            def capq(fn, *a):
                S.capture()
                fn(*a)
                return S.end_capture()

            S.extend(capq(partA, 0))
            for cb in range(nb):
                lists = [capq(partB, cb)]
                if cb + 1 < nb:
                    lists.append(capq(partA, cb + 1))
                S.extend(interleave(*lists))
            DMA(g, "act", SZ[:, 0:T], W["ZT"][hp * 128:(hp + 1) * 128, c0:c0 + T], "C_z", [], [rSZ])
            ACT(g, SZ[:, 0:T], SZ[:, 0:T], AF.Silu, [rSZ], [rSZ])
            nch = T // 64
            for d_ in range(2):
                MM(g, PX[:, 0:nch], SELP[:, d_ * 4 + hp, :], GLS[:, 0:nch], True, True, [rC, rGLS], [rPX])
                CP(g, "dve", GLB[:, d_, 0:nch], PX[:, 0:nch], [rPX], [rGLB])

        def pre(m, d_, hp, pb):
            ts_ = slice(m * 128, (m + 1) * 128)
            PB_, PTY, YF, rPB_, rPTY, rYF = PB_A[d_], PTY_A[d_], YF_A[d_], rPB_A[d_], rPTY_A[d_], rYF_A[d_]
            PA, PBk, rPA, rPBk = PA_A[d_], PBk_A[d_], rPA_A[d_], rPBk_A[d_]
            S.begin_atomic()
            MM(g, PX[:, 0:128], KT[:, ts_], IDB[:], True, True, [rKT, rI], [rPX])
            MM(g, PX[:, 128:256], VT[:, ts_], IDB[:], True, True, [rVT, rI], [rPX])
            CP(g, "dve", KVTM[pb][:].rearrange("p a b -> p (a b)"), PX[:, 0:256], [rPX], [rKVTM[pb]])
            if cfg.gdn_stage < 3.07:
                S.end_atomic()
                return
            for hh in range(2):
                b_ = slice(64 * hh, 64 * hh + 64)
                MM(g, PG[:, hh * 256:hh * 256 + 128], KTZ[:, hh, ts_], KT[:, ts_], True, True, [rKTZ, rKT], [rPG])
                MM(g, PG[:, hh * 256 + 128:hh * 256 + 256], KTZ[:, hh, ts_], QT[:, ts_], True, True, [rKTZ, rQT], [rPG])
            if cfg.gdn_stage < 3.15:
                S.end_atomic()
                return
            for hh in range(2):
                h = 2 * hp + hh
                sel = SELR[:, d_ * 8 + h, :]
                MM(g, PD[:, 0:128], sel, HS[:, ts_], True, False, [rC, rHS], [rPD])
                MM(g, PD[:, 128:512].rearrange("p (a b) -> p a b", a=3), sel, GCS[:, None, ts_].to_broadcast([128, 3, 128]),
                   False, False, [rC, rGCS], [rPD])
                MM(g, PD[:, 0:384], IDB[:], MASK[:, d_, :, :].rearrange("p a b -> p (a b)"), False, True, [rI, rC], [rPD])
                ACT(g, E01[pb][:, hh, :], PD[:, 0:256], AF.Exp, [rPD, rSCT], [rE01[pb]], bias=sc(m, 0, d_, h), scale=1.0)
                ACT(g, E2[pb][:, hh, :], PD[:, 256:384], AF.Exp, [rPD, rSCT], [rE2[pb]], bias=sc(m, 1, d_, h), scale=-1.0)
                b_ = slice(64 * hh, 64 * hh + 64)
                ACT(g, EGC[pb][b_, :], PD[b_, 384:512], AF.Exp, [rPD], [rEGC[pb]])
            if cfg.gdn_stage < 3.25:
                S.end_atomic()
                return
            PGv = PG[:].rearrange("p (a b) -> p a b", a=2)
            STT(g, PTY[0][:, :, 0, :], PGv[:, :, 0:128], -1.0, E01[pb][:, :, 0:128], ALU.mult, ALU.mult, [rPG, rE01[pb]], [rPTY[0]])
            TT_(g, "dve", PTY[1][:, :, 1, :], PTY[0][:, :, 0, :], IDF[:, None, :].to_broadcast([128, 2, 128]), ALU.add,
                [rPTY[0], rC], [rPTY[1]])
            STT(g, PB_[0][:], PGv[:, :, 0:128], -1.0, E2[pb][:], ALU.mult, ALU.mult, [rPG, rE2[pb]], [rPB_[0]])
            TT_(g, "dve", QKT[pb][:], PGv[:, :, 128:256], E01[pb][:, :, 128:256], ALU.mult, [rPG, rE01[pb]], [rQKT[pb]])
            S.end_atomic()
            for hh in range(2):
                b_ = slice(64 * hh, 64 * hh + 64)
                TT_(g, "pool", QGT[pb][b_, hh, :], QT[b_, ts_], EGC[pb][b_, :], ALU.mult, [rQT, rEGC[pb]], [rQGT[pb]])
            for hh in range(2):
                h = 2 * hp + hh
                b_ = slice(64 * hh, 64 * hh + 64)
                ACT(g, KGZ[pb][:, hh, b_], KVTM[pb][:, 0, b_], AF.Copy, [rKVTM[pb], rSCT], [rKGZ[pb]], scale=sc(m, 4, d_, h))
            if cfg.gdn_stage < 3.35:
                return
            PAv = PA[:].rearrange("p (a b) -> p a b", a=2)
            PBv = PBk[:, 0:256].rearrange("p (a b) -> p a b", a=2)
            for k in range(6):
                cur, nxt = k % 2, (k + 1) % 2
                for hh in range(2):
                    if k == 0:
                        MM(g, PAv[:, hh, 0:128], PB_[cur][:, hh, :], PTY[cur][:, hh, 0, :], True, True, [rPB_[cur], rPTY[cur]], [rPA])
                    elif k < 4:
                        MM(g, PAv[:, hh, :], PB_[cur][:, hh, :], PTY[cur][:, hh].rearrange("p a b -> p (a b)"), True, True,
                           [rPB_[cur], rPTY[cur]], [rPA])
                    elif k == 4:
                        MM(g, PAv[:, hh, 128:256], PB_[cur][:, hh, :], PTY[cur][:, hh, 1, :], True, True, [rPB_[cur], rPTY[cur]], [rPA])
                    else:
                        MM(g, PAv[:, hh, 128:256], PB_[cur][:, hh, :], PTY[cur][:, hh, 1, :], True, True, [rPB_[cur], rPTY[cur]], [rPA])
                    if k < 5:
                        MM(g, PBv[:, hh, :], PTY[cur][:, hh, 0, :], PB_[cur][:, hh, :], True, True, [rPB_[cur], rPTY[cur]], [rPBk])
                if k < 5:
                    if k < 4:
                        CP(g, "act", PTY[nxt][:, :, 0, :], PAv[:, :, 0:128], [rPA], [rPTY[nxt]])
                    if k == 0:
                        pass
                    else:
                        TT_(g, "dve", PTY[nxt][:, :, 1, :], PAv[:, :, 128:256], PTY[cur][:, :, 1, :], ALU.add, [rPA, rPTY[cur]], [rPTY[nxt]])
                    CP(g, "act" if k % 2 else "dve", PB_[nxt][:], PBv, [rPBk], [rPB_[nxt]])
                else:
                    TT_(g, "dve", YF[:], PAv[:, :, 128:256], PTY[cur][:, :, 1, :], ALU.add, [rPA, rPTY[cur]], [rYF])
            if cfg.gdn_stage < 3.45:
                return
            for hh in range(2):
                h = 2 * hp + hh
                TS_(g, "dve", TB[pb][:, hh, :], YF[:, hh, :], sc(m, 2, d_, h), None, ALU.mult, None, [rYF, rSCT], [rTB[pb]])
                ACT(g, TBG[pb][:, hh, :], YF[:, hh, :], AF.Copy, [rYF, rSCT], [rTBG[pb]], scale=sc(m, 3, d_, h))
            S.begin_atomic()
            for hh in range(2):
                MM(g, PX[:, hh * 64:(hh + 1) * 64], TB[pb][:, hh, :], KVTM[pb][:, 1, hh * 64:(hh + 1) * 64], True, True,
                   [rTB[pb], rKVTM[pb]], [rPX])
                MM(g, PX[:, 128 + hh * 128:256 + hh * 128], KVTM[pb][:, 0, :], TBG[pb][:, hh, :], True, True,
                   [rTBG[pb], rKVTM[pb]], [rPX])
            CP(g, "act", U[pb][:], PX[:, 0:128], [rPX], [rU[pb]])
            for hh in range(2):
                b_ = slice(64 * hh, 64 * hh + 64)
                CP(g, "dve", WT[pb][b_, hh, :], PX[b_, 128 + hh * 128:256 + hh * 128], [rPX], [rWT[pb]])
            S.end_atomic()

        vc = [0]
        oc = [0]

        def chain(m, d_, hp, pb, T, second):
            SF, SBs, rSF, rSBs = SF_A[d_], SBs_A[d_], rSF_A[d_], rSBs_A[d_]
            VNEW, rVNEW = VNEW_A[d_], rVNEW_A[d_]
            CH, rCH = CH_A[d_], rCH_A[d_]
            PW, PO, PSt = CH[:, 0:128], CH[:, 128:256], CH[:, 256:320]
            rPW = rPO = rPSt = rCH
            order = (0, 1) if d_ == 0 else (1, 0)
            for j in order:
                n = 2 * m + j
                tj = slice(64 * j, 64 * j + 64)
                v = vc[0] % 2
                vc[0] += 1
                for hh in range(2):
                    b_ = slice(64 * hh, 64 * hh + 64)
                    MM(g, PW[:, hh * 64:(hh + 1) * 64], WT[pb][:, hh, :], SBs[:, :], True, True, [rWT[pb], rSBs], [rPW])
                TT_(g, "dve", VNEW[v][tj, :], U[pb][tj, :], PW[tj, :], ALU.subtract, [rU[pb], rPW], [rVNEW[v]])
                for hh in range(2):
                    b_ = slice(64 * hh, 64 * hh + 64)
                    MM(g, PO[:, hh * 64:(hh + 1) * 64], QGT[pb][:, hh, :], SBs[:, :], True, False, [rQGT[pb], rSBs], [rPO])
                    MM(g, PO[:, hh * 64:(hh + 1) * 64], QKT[pb][tj, hh, :], VNEW[v][tj, hh * 64:(hh + 1) * 64], False, True,
                       [rQKT[pb], rVNEW[v]], [rPO])
                if not second:
                    CP(g, "act", OF[tj, m, :], PO[tj, :], [rPO], [rOF])
                else:
                    ob = oc[0] % 2
                    TT_(g, "dve", OT[ob][tj, :], PO[tj, :], OF[tj, m, :], ALU.add, [rPO, rOF], [rOT[ob]])
                for hh in range(2):
                    MM(g, PSt[:, :], KGZ[pb][tj, hh, :], VNEW[v][tj, hh * 64:(hh + 1) * 64], hh == 0, hh == 1,
                       [rKGZ[pb], rVNEW[v]], [rPSt])
                STT(g, SF[:], SF[:], GLB[:, d_, n:n + 1], PSt[:, :], ALU.mult, ALU.add, [rSF, rGLB, rPSt], [rSF])
                CP(g, "act", SBs[:], SF[:], [rSF], [rSBs])
            if second:
                ob = oc[0] % 2
                oc[0] += 1
                for hh in range(2):
                    ACT(g, JUNK[:], OT[ob][:, hh * 64:(hh + 1) * 64], AF.Square, [rOT[ob]], [rJ, rSSQ[ob]], accum_out=SSQ[ob][:, hh:hh + 1])
                ACT(g, SSQ[ob][:, 2:4], SSQ[ob][:, 0:2], AF.Sqrt, [rSSQ[ob]], [rSSQ[ob]], scale=1.0 / 64, bias=EPS)
                S.op("dve", lambda e, ob=ob: e.reciprocal(out=SSQ[ob][:, 2:4], in_=SSQ[ob][:, 2:4]), [rSSQ[ob]], [rSSQ[ob]])
                for hh in range(2):
                    TS_(g, "dve", ON[ob][:, hh * 64:(hh + 1) * 64], OT[ob][:, hh * 64:(hh + 1) * 64], SSQ[ob][:, 2 + hh:3 + hh], None,
                        ALU.mult, None, [rOT[ob], rSSQ[ob]], [rON[ob]])
                S.begin_atomic()
                MM(g, PX[:, 384:512], ON[ob][:], IDB[:], True, True, [rON[ob], rI], [rPX])
                ts_ = slice(m * 128, (m + 1) * 128)
                STT(g, YT[:, ts_], PX[:, 384:512], NORMO[:, 0:1], SZ[:, ts_], ALU.mult, ALU.mult, [rPX, rC, rSZ], [rYT])
                S.end_atomic()

        for si, (grp, sidx, T) in enumerate(cfg.seqs):
            c0 = cfg.offs[si]
            ntl = T // 128
            gate_prep(si, T, c0)
            if cfg.gdn_stage <= 1:
                continue
            for hp in range(4):
                qkv_prep(si, T, c0, hp)
                if cfg.gdn_stage <= 2:
                    continue
                for d_ in range(2):
                    S.op("dve", lambda e, d_=d_: e.memset(SF_A[d_][:], 0.0), [], [rSF_A[d_]])
                    S.op("pool", lambda e, d_=d_: e.memset(SBs_A[d_][:], 0.0), [], [rSBs_A[d_]])
                tl = [list(range(ntl)), list(range(ntl - 1, -1, -1))]
                seen = set()

                def cap(fn, *a):
                    S.capture()
                    fn(*a)
                    return S.end_capture()

                S.extend(interleave(cap(pre, tl[0][0], 0, hp, 0), cap(pre, tl[1][0], 1, hp, 2)))
                for ix in range(ntl):
                    lists = []
                    for d_ in range(2):
                        m = tl[d_][ix]
                        if cfg.gdn_stage >= 4:
                            lists.append(cap(chain, m, d_, hp, 2 * d_ + ix % 2, T, m in seen))
                        seen.add(m)
                    if ix + 1 < ntl:
                        for d_ in range(2):
                            lists.append(cap(pre, tl[d_][ix + 1], d_, hp, 2 * d_ + (ix + 1) % 2))
                    S.extend(interleave(*lists))
                DMA(g, "sp", W["YMIX"][512 + hp * 128:512 + (hp + 1) * 128, c0:c0 + T], YT[:, 0:T], "C_y", [rYT], [rdram])
        S.barrier()


def phaseM(g, l):
    pass


def small_vec_load(g, q, dst, src, semkey, writes):
    return g.S.dma(q, lambda e: e.dma_start(out=dst, in_=src, allow_slow_non_contiguous=True), semkey, [], writes)


def phaseD(g, l):
    nc, S, cfg = g.nc, g.S, g.cfg
    NS = cfg.NS
    S.barrier()
    with ExitStack() as es:
        KMT = g.sbt(es, "D_kmt", [128, 8, NS, NMEM], BF16)
        VM = g.sbt(es, "D_vm", [128, NS, 2, D], BF16)
        rKMT, rVM = S.res(), S.res()
        WOUT = g.sbt(es, "D_wout", [128, KC, D], BF16)
        WXQ = g.sbt(es, "D_wxq", [128, KC, D], BF16)
        WXO = g.sbt(es, "D_wxo", [128, KC, D], BF16)
        rWD = S.res()
        for nm, Wt in (("w_out", WOUT), ("w_xq", WXQ), ("w_xo", WXO)):
            wsrc = g.I[nm][l].rearrange("(kc p) n -> p kc n", p=128)
            for kc in range(KC):
                DMA(g, "pool", Wt[:, kc, :], wsrc[:, kc, :], "D_w%d" % (kc % 4), [], [rWD])
        with ExitStack() as esm:
            WKV = g.sbt(esm, "M_wkv", [128, KC, 2 * D], BF16)
            rWKV = S.res()
            wsrc = g.I["w_xkv"][l].rearrange("(kc p) n -> p kc n", p=128)
            for kc in range(KC):
                DMA(g, "pool", WKV[:, kc, :], wsrc[:, kc, :], "M_w%d" % (kc % 4), [], [rWKV])
            MHF = [g.sbt(esm, "M_mhf%d" % i, [128, KC, NMEM], F32) for i in range(2)]
            MN = [g.sbt(esm, "M_mn%d" % i, [128, KC, NMEM], BF16) for i in range(2)]
            rMHF = [S.res() for _ in range(2)]
            rMN = [S.res() for _ in range(2)]
            PSm = [g.pst(esm, "M_ps%d" % i, [128, 512], F32) for i in range(4)]
            rPSm = [S.pres() for _ in range(4)]
            MHv = g.W["MH"].rearrange("(kc p) t -> p kc t", p=128)
            pc = 0
            for si in range(NS):
                b = si % 2
                DMA(g, "sp", MHF[b][:], MHv[:, :, si * NMEM:(si + 1) * NMEM], "M_m%d" % b, [], [rMHF[b]])
                for kc in range(KC):
                    TS_(g, "dve" if kc % 2 == 0 else "pool", MN[b][:, kc, :], MHF[b][:, kc, :], g.NMEMG[:, l, kc:kc + 1], None,
                        ALU.mult, None, [rMHF[b], g.rconst], [rMN[b]])
                for m in range(8):
                    p = pc % 4
                    pc += 1
                    for kc in range(KC):
                        MM(g, PSm[p][:, 0:NMEM], WKV[:, kc, m * 128:(m + 1) * 128], MN[b][:, kc, :], kc == 0, kc == KC - 1,
                           [rWKV, rMN[b]], [rPSm[p]])
                    CP(g, "act" if m % 2 == 0 else "dve", KMT[:, m, si, :], PSm[p][:, 0:NMEM], [rPSm[p]], [rKMT])
                for mt in range(2):
                    for nh in range(2):
                        p = pc % 4
                        pc += 1
                        for kc in range(KC):
                            MM(g, PSm[p][:], MN[b][:, kc, mt * 128:(mt + 1) * 128], WKV[:, kc, D + nh * 512:D + (nh + 1) * 512],
                               kc == 0, kc == KC - 1, [rWKV, rMN[b]], [rPSm[p]])
                        CP(g, "act" if nh == 0 else "dve", VM[:, si, mt, nh * 512:(nh + 1) * 512], PSm[p][:], [rPSm[p]], [rVM])
            S.barrier()
        YM = g.sbt(es, "D_ym", [128, KC, TT], BF16)
        XF_A = [g.sbt(es, "D_xf%d" % i, [128, KC, TT], F32) for i in range(2)]
        SQ = g.sbt(es, "D_sq", [128, KC, TT], BF16)
        RSTD = g.sbt(es, "D_rstd", [128, TT], F32)
        XN = g.sbt(es, "D_xn", [128, KC, TT], BF16)
        QX_A = [g.sbt(es, "D_qx%d" % i, [128, KC, TT], BF16) for i in range(2)]
        PT = [g.sbt(es, "D_pt%d" % i, [128, 2, TT], BF16) for i in range(2)]
        RCP = [g.sbt(es, "D_rcp%d" % i, [128, TT], F32) for i in range(2)]
        ATT = g.sbt(es, "D_att", [128, KC, TT], BF16)
        XN3 = g.sbt(es, "D_xn3", [128, KC, TT], BF16)
        rYM, rSQ, rRSTD, rXN, rATT, rXN3 = [S.res() for _ in range(6)]
        rXF_A = [S.res() for _ in range(2)]
        rQX_A = [S.res() for _ in range(2)]
        rPT = [S.res() for _ in range(2)]
        rRCP = [S.res() for _ in range(2)]
        PSs = g.pst(es, "D_pss", [128, TT], F32)
        rPSs = S.pres()
        PS2 = g.pst(es, "D_pssc", [128, 2 * TT], F32)
        rPS2 = S.pres()
        PSd = g.pst(es, "D_psd", [128, TT], F32)
        rPSd = S.pres()
        PS = [g.pst(es, "D_ps%d" % i, [128, TT], F32) for i in range(4)]
        rPS = [S.pres() for _ in range(4)]
        rdram = S.res()
        XTv = g.W["XT"].rearrange("(kc p) t -> p kc t", p=128)
        YMv = g.W["YMIX"].rearrange("(kc p) t -> p kc t", p=128)
        X3v = g.W["XN3"].rearrange("(kc p) t -> p kc t", p=128)
        pcs = [0]
        hcs = [0, 0]
        tiles = []
        for si, (grp, sidx, T) in enumerate(cfg.seqs):
            for t4 in range(T // TT):
                tiles.append((si, cfg.offs[si] + t4 * TT))

        def part1(ix):
            si, c0 = tiles[ix]
            XF, rXF, QX, rQX = XF_A[ix % 2], rXF_A[ix % 2], QX_A[ix % 2], rQX_A[ix % 2]
            DMA(g, "sp", YM[:], YMv[:, :, c0:c0 + TT], "D_ym", [], [rYM])
            DMA(g, "act", XF[:], XTv[:, :, c0:c0 + TT], "D_xf%d" % (ix % 2), [], [rXF])
            for m in range(KC):
                p = pcs[0] % 2
                pcs[0] += 1
                for kc in range(KC):
                    MM(g, PS[p][:], WOUT[:, kc, m * 128:(m + 1) * 128], YM[:, kc, :], kc == 0, kc == KC - 1, [rWD, rYM], [rPS[p]])
                TT_(g, "dve", XF[:, m, :], PS[p][:], XF[:, m, :], ALU.add, [rPS[p], rXF], [rXF])
            S.begin_atomic()
            rmsnorm_fm(g, XF, rXF, XN, rXN, SQ, rSQ, RSTD, rRSTD, PSs, rPSs, 3 * l + 1, TT)
            S.end_atomic()
            for m in range(KC):
                p = pcs[0] % 2
                pcs[0] += 1
                for kc in range(KC):
                    MM(g, PS[p][:], WXQ[:, kc, m * 128:(m + 1) * 128], XN[:, kc, :], kc == 0, kc == KC - 1, [rWD, rXN], [rPS[p]])
                ACT(g, QX[:, m, :], PS[p][:], AF.Copy, [rPS[p]], [rQX], scale=1.0 / 16.0)

        def part2(ix):
            si, c0 = tiles[ix]
            XF, rXF, QX, rQX = XF_A[ix % 2], rXF_A[ix % 2], QX_A[ix % 2], rQX_A[ix % 2]
            for hx in range(4):
                hb = hcs[0] % 2
                hcs[0] += 1
                for mt in range(2):
                    for c in range(2):
                        MM(g, PS2[:, mt * TT:(mt + 1) * TT], KMT[:, hx * 2 + c, si, mt * 128:(mt + 1) * 128], QX[:, hx * 2 + c, :],
                           c == 0, c == 1, [rKMT, rQX], [rPS2])
                ACT(g, PT[hb][:], PS2[:].rearrange("p (a b) -> p a b", a=2), AF.Exp, [rPS2], [rPT[hb]])
                for mt in range(2):
                    MM(g, PSd[:], g.ONESB[:], PT[hb][:, mt, :], mt == 0, mt == 1, [g.rconst, rPT[hb]], [rPSd])
                ACT(g, RCP[hb][:], PSd[:], AF.Ln, [rPSd], [rRCP[hb]])
                ACT(g, RCP[hb][:], RCP[hb][:], AF.Exp, [rRCP[hb]], [rRCP[hb]], scale=-1.0)
                for c in range(2):
                    p = 2 + hcs[1] % 2
                    hcs[1] += 1
                    for mt in range(2):
                        MM(g, PS[p][:], VM[:, si, mt, hx * 256 + c * 128:hx * 256 + (c + 1) * 128], PT[hb][:, mt, :],
                           mt == 0, mt == 1, [rVM, rPT[hb]], [rPS[p]])
                    TT_(g, "dve", ATT[:, hx * 2 + c, :], PS[p][:], RCP[hb][:], ALU.mult, [rPS[p], rRCP[hb]], [rATT])
            for m in range(KC):
                p = 2 + hcs[1] % 2
                hcs[1] += 1
                for kc in range(KC):
                    MM(g, PS[p][:], WXO[:, kc, m * 128:(m + 1) * 128], ATT[:, kc, :], kc == 0, kc == KC - 1, [rWD, rATT], [rPS[p]])
                TT_(g, "dve", XF[:, m, :], PS[p][:], XF[:, m, :], ALU.add, [rPS[p], rXF], [rXF])
            DMA(g, "sp", XTv[:, :, c0:c0 + TT], XF[:], "D_xs%d" % (ix % 2), [rXF], [rdram])
            S.begin_atomic()
            rmsnorm_fm(g, XF, rXF, XN3, rXN3, SQ, rSQ, RSTD, rRSTD, PSs, rPSs, 3 * l + 2, TT)
            S.end_atomic()
            DMA(g, "act", X3v[:, :, c0:c0 + TT], XN3[:], "D_x3", [rXN3], [rdram])

        def cap(fn, *a):
            S.capture()
            fn(*a)
            return S.end_capture()

        S.extend(cap(part1, 0))
        for ix in range(len(tiles)):
            lists = [cap(part2, ix)]
            if ix + 1 < len(tiles):
                lists.append(cap(part1, ix + 1))
            S.extend(interleave(*lists))
        S.barrier()


def phaseE(g, l):
    nc, S, cfg = g.nc, g.S, g.cfg
    S.barrier()
    NF = 2 * DFF // 128
    NP2 = DFF // 128
    with ExitStack() as es:
        WUP = g.sbt(es, "E_wup", [128, KC, 2 * DFF], BF16)
        WDN = g.sbt(es, "E_wdn", [128, NP2, D], BF16)
        rWU, rWDn = S.res(), S.res()
        wsrc = g.I["w_up"][l].rearrange("(kc p) n -> p kc n", p=128)
        for kc in range(KC):
            DMA(g, "pool", WUP[:, kc, :], wsrc[:, kc, :], "E_w%d" % (kc % 4), [], [rWU])
        wsrc = g.I["w_down"][l].rearrange("(k p) n -> p k n", p=128)
        for k in range(NP2):
            DMA(g, "pool", WDN[:, k, :], wsrc[:, k, :], "E_w%d" % (k % 4), [], [rWDn])
        CW = g.sbt(es, "E_cw", [128, NF, 4], F32)
        rCW = S.res()
        for k in range(3):
            small_vec_load(g, "sp", CW[:, :, k], g.I["conv_ffn"][l, k].rearrange("(f p) -> p f", p=128), "E_c", [rCW])
        small_vec_load(g, "sp", CW[:, :, 3], g.I["conv_ffn_b"][l].rearrange("(f p) -> p f", p=128), "E_c", [rCW])
        XH = [g.sbt(es, "E_xh%d" % i, [128, KC, TT + 2], BF16) for i in range(2)]
        rXH = [S.res() for _ in range(2)]
        ACTT = g.sbt(es, "E_act", [128, NP2, TT], BF16)
        rACTT = S.res()
        XF = g.sbt(es, "E_xf", [128, KC, TT], F32)
        rXF = S.res()
        T1 = [g.sbt(es, "E_t1%d" % i, [128, TT], F32) for i in range(2)]
        T2 = [g.sbt(es, "E_t2%d" % i, [128, TT], F32) for i in range(2)]
        T3 = [g.sbt(es, "E_t3%d" % i, [128, TT], F32) for i in range(2)]
        rT1 = [S.res() for _ in range(2)]
        rT2 = [S.res() for _ in range(2)]
        rT3 = [S.res() for _ in range(2)]
        PSH = g.pst(es, "E_psh", [128, 6 * 512], F32)
        rSLOT = [S.pres() for _ in range(3)]
        PSO = [g.pst(es, "E_pso%d" % i, [128, TT], F32) for i in range(2)]
        rPSO = [S.pres() for _ in range(2)]
        rdram = S.res()
        XTv = g.W["XT"].rearrange("(kc p) t -> p kc t", p=128)
        X3v = g.W["XN3"].rearrange("(kc p) t -> p kc t", p=128)
        W_ = TT + 2

        def slot_pieces(k):
            a = 1024 * k
            return [(a, a + 512, 0, 512), (a + 512, a + W_, 512, W_)]

        ti = 0
        pcount = 0
        oc = 0
        slotc = [0]
        etiles = []
        for si, (grp, sidx, T) in enumerate(cfg.seqs):
            for t4 in range(T // TT):
                etiles.append((si, t4, T // TT))

        def load_xh(ix):
            si, t4, nt4 = etiles[ix]
            c0 = cfg.offs[si] + t4 * TT
            b = ix % 2
            lo = 0 if t4 > 0 else 1
            hi = W_ if t4 < nt4 - 1 else W_ - 1
            if lo == 1:
                S.op("pool", lambda e, b=b: e.memset(XH[b][:, :, 0:1], 0.0), [], [rXH[b]])
            if hi == W_ - 1:
                S.op("pool", lambda e, b=b: e.memset(XH[b][:, :, W_ - 1:W_], 0.0), [], [rXH[b]])
            DMA(g, "sp", XH[b][:, :, lo:hi], X3v[:, :, c0 - 1 + lo:c0 - 1 + hi], "E_xh%d" % b, [], [rXH[b]])

        load_xh(0)
        for ix_, (si, t4, nt4) in enumerate(etiles):
            if True:
                c0 = cfg.offs[si] + t4 * TT
                b = ti % 2
                ti += 1
                if ix_ + 1 < len(etiles):
                    load_xh(ix_ + 1)
                DMA(g, "act", XF[:], XTv[:, :, c0:c0 + TT], "E_xf", [], [rXF])
                for i in range(NP2):
                    pr = pcount % 2
                    pcount += 1
                    for which, f in ((0, i), (1, NP2 + i)):
                        k = slotc[0] % 3
                        slotc[0] += 1
                        for kc in range(KC):
                            for (pa, pb, ra, rb) in slot_pieces(k):
                                MM(g, PSH[:, pa:pb], WUP[:, kc, f * 128:(f + 1) * 128], XH[b][:, kc, ra:rb], kc == 0, kc == KC - 1,
                                   [rWU, rXH[b]], [rSLOT[k]])
                        a0 = 1024 * k
                        Tt, rTt = (T1[pr], rT1[pr]) if which == 0 else (T2[pr], rT2[pr])
                        ACT(g, Tt[:], PSH[:, a0 + 1:a0 + 1 + TT], AF.Identity, [rSLOT[k], rCW], [rTt],
                            scale=CW[:, f, 1:2], bias=CW[:, f, 3:4])
                        STT(g, Tt[:], PSH[:, a0:a0 + TT], CW[:, f, 0:1], Tt[:], ALU.mult, ALU.add, [rSLOT[k], rCW, rTt], [rTt])
                        STT(g, Tt[:], PSH[:, a0 + 2:a0 + 2 + TT], CW[:, f, 2:3], Tt[:], ALU.mult, ALU.add, [rSLOT[k], rCW, rTt], [rTt])
                    ACT(g, T3[pr][:], T2[pr][:], AF.Silu, [rT2[pr]], [rT3[pr]])
                    TT_(g, "pool", ACTT[:, i, :], T3[pr][:], T1[pr][:], ALU.mult, [rT3[pr], rT1[pr]], [rACTT])
                for m in range(KC):
                    p = oc % 2
                    oc += 1
                    for k in range(NP2):
                        MM(g, PSO[p][:], WDN[:, k, m * 128:(m + 1) * 128], ACTT[:, k, :], k == 0, k == NP2 - 1, [rWDn, rACTT], [rPSO[p]])
                    TT_(g, "dve", XF[:, m, :], PSO[p][:], XF[:, m, :], ALU.add, [rPSO[p], rXF], [rXF])
                DMA(g, "sp", XTv[:, :, c0:c0 + TT], XF[:], "E_xs", [rXF], [rdram])
        S.barrier()


def phaseF(g):
    nc, S, cfg = g.nc, g.S, g.cfg
    L = cfg.depth
    S.barrier()
    with ExitStack() as es:
        XF = [g.sbt(es, "F_xf%d" % i, [128, KC, TT], F32) for i in range(2)]
        rXF = [S.res() for _ in range(2)]
        SQ = g.sbt(es, "F_sq", [128, KC, TT], BF16)
        RSTD = g.sbt(es, "F_rstd", [128, TT], F32)
        R1 = g.sbt(es, "F_r1", [128, KC, TT], F32)
        HI = g.sbt(es, "F_hi", [128, 3, KC, TT], BF16)
        OUT = [g.sbt(es, "F_out%d" % i, [128, 4, D], F32) for i in range(2)]
        rSQ, rRSTD, rR1, rHI = [S.res() for _ in range(4)]
        rOUT = [S.res() for _ in range(2)]
        PSs = g.pst(es, "F_pss", [128, TT], F32)
        rPSs = S.pres()
        PS = [g.pst(es, "F_ps%d" % i, [128, TT], F32) for i in range(4)]
        rPS = [S.pres() for _ in range(4)]
        rdram = S.res()
        XTv = g.W["XT"].rearrange("(kc p) t -> p kc t", p=128)
        ti = 0
        pc = 0
        for si, (grp, sidx, T) in enumerate(cfg.seqs):
            ydst = y_dst(g, si)
            for t4 in range(T // TT):
                c0 = cfg.offs[si] + t4 * TT
                b = ti % 2
                ti += 1
                DMA(g, "sp", XF[b][:], XTv[:, :, c0:c0 + TT], "F_x%d" % b, [], [rXF[b]])
                ACT(g, SQ[:], XF[b][:], AF.Square, [rXF[b]], [rSQ])
                for kc in range(KC):
                    MM(g, PSs[:], g.ONESB[:], SQ[:, kc, :], kc == 0, kc == KC - 1, [rSQ, g.rconst], [rPSs])
                ACT(g, RSTD[:], PSs[:], AF.Ln, [rPSs], [rRSTD], scale=1.0 / D, bias=EPS)
                ACT(g, RSTD[:], RSTD[:], AF.Exp, [rRSTD], [rRSTD], scale=-0.5)
                for kc in range(KC):
                    STT(g, XF[b][:, kc, :], XF[b][:, kc, :], g.NORMS[:, 3 * L, kc:kc + 1], RSTD[:], ALU.mult, ALU.mult,
                        [rXF[b], rRSTD, g.rconst], [rXF[b]])
                CP(g, "act", HI[:, 0], XF[b][:], [rXF[b]], [rHI])
                TT_(g, "dve", R1[:], XF[b][:], HI[:, 0], ALU.subtract, [rXF[b], rHI], [rR1])
                CP(g, "act", HI[:, 1], R1[:], [rR1], [rHI])
                TT_(g, "pool", R1[:], R1[:], HI[:, 1], ALU.subtract, [rR1, rHI], [rR1])
                CP(g, "act", HI[:, 2], R1[:], [rR1], [rHI])
                for ts in range(4):
                    for half in range(2):
                        p = pc % 4
                        pc += 1
                        for q in range(4):
                            kc = half * 4 + q
                            for part in range(3):
                                MM(g, PS[p][:, q * 128:(q + 1) * 128], HI[:, part, kc, ts * 128:(ts + 1) * 128], g.IDB[:],
                                   part == 0, part == 2, [rHI, g.rconst], [rPS[p]])
                        CP(g, "act" if half == 0 else "dve", OUT[b][:, ts, half * 512:(half + 1) * 512], PS[p][:], [rPS[p]], [rOUT[b]])
                t0 = t4 * TT
                DMA(g, "act", ydst[t0:t0 + TT, :].rearrange("(s p) f -> p s f", p=128), OUT[b][:], "F_o%d" % b, [rOUT[b]], [rdram])
        S.barrier()


N_CORES = 8
_FULL = dict(n_p=4, T_p=2048, n_s=2, T_s=4096, depth=4)
SKIP_GDN = False


def _device_inputs(inputs, L):
    m = {}
    f = lambda a: np.ascontiguousarray(np.asarray(a, dtype=np.float32))
    for k in ("norm_mix", "w_in", "conv_qkv", "norm_o", "w_out", "norm_x", "norm_mem", "w_xq", "w_xkv", "w_xo",
              "norm_ffn", "w_up", "conv_ffn", "conv_ffn_b", "w_down", "norm_final"):
        m[k] = f(inputs[k])
    m["a_log"] = f(inputs["a_log"]).reshape(L, 16)
    m["dt_bias"] = f(inputs["dt_bias"]).reshape(L, 16)
    m["na_bias"] = _na_bias_table(f(inputs["rpb"])).reshape(L, 128, 8 * 9 * 128)
    m.update(_consts())
    return m


def kernel(**inputs):
    cfg = Cfg(_FULL["n_p"], _FULL["T_p"], _FULL["n_s"], _FULL["T_s"], _FULL["depth"], skip_gdn=SKIP_GDN)
    nc, g = build_program(cfg)
    shared = _device_inputs(inputs, cfg.depth)
    xp = np.asarray(inputs["x_prompt"], dtype=np.float32)
    xs = np.asarray(inputs["x_sample"], dtype=np.float32)
    mp = np.asarray(inputs["mem_prompt"], dtype=np.float32)
    ms = np.asarray(inputs["mem_sample"], dtype=np.float32)
    in_maps = []
    for c in range(N_CORES):
        m = dict(shared)
        m["x_p"] = np.ascontiguousarray(xp[cfg.n_p * c:cfg.n_p * (c + 1)])
        m["x_s"] = np.ascontiguousarray(xs[cfg.n_s * c:cfg.n_s * (c + 1)])
        m["m_p"] = np.ascontiguousarray(mp[cfg.n_p * c:cfg.n_p * (c + 1)])
        m["m_s"] = np.ascontiguousarray(ms[cfg.n_s * c:cfg.n_s * (c + 1)])
        in_maps.append(m)
    res = run_bass_kernel_spmd(nc, in_maps, core_ids=list(range(N_CORES)))
    y_p = np.concatenate([np.asarray(r["y_p"], dtype=np.float32) for r in res.results], axis=0)
    y_s = np.concatenate([np.asarray(r["y_s"], dtype=np.float32) for r in res.results], axis=0)
    return (y_p, y_s)
```
